# Optimizing a Trainium2 kernel written in Bass

```python
import math
import jax, jax.numpy as jnp
from jax import lax
import numpy as np


D_MODEL = 1024
BATCH = 8
SEQ = 4096
DEPTH = 4

CHUNK = 64
Q_BLOCK = 128
MEM_LEN = 256

MLA_HEADS = 8
MLA_NOPE = 64
MLA_ROPE = 32
MLA_V = 64
MLA_QK = MLA_NOPE + MLA_ROPE
Q_LORA = 384
KV_LORA = 256
ROPE_BASE = 10000.0

DIFF_HEADS = 4
DIFF_QK = 64
DIFF_V = 2 * DIFF_QK

MEM_HEADS = 4
MEM_HEAD_DIM = 128

BRANCH_W = 512
N_BRANCH = 3
D_FF = 4 * D_MODEL

T5_BUCKETS = 32
T5_MAX_DIST = 128

EPS = 1e-6
NEG = -1e30

SPLITS = [Q_LORA, KV_LORA, MLA_ROPE,
          DIFF_HEADS * 2 * DIFF_QK, DIFF_HEADS * 2 * DIFF_QK, DIFF_HEADS * DIFF_V,
          MEM_HEADS * MEM_HEAD_DIM, N_BRANCH * D_MODEL]
D_IN = sum(SPLITS)

kernel_name = 'hybrid_mla_diffattn_memxattn_gated_block'


def rmsnorm(x, g):
    x32 = x.astype(jnp.float32)
    y = x32 * lax.rsqrt(jnp.mean(x32 * x32, axis=-1, keepdims=True) + EPS)
    return (y * g.astype(jnp.float32)).astype(x.dtype)


def rope(x, positions):
    half = MLA_ROPE // 2
    inv = jnp.power(jnp.float32(ROPE_BASE), -jnp.arange(half, dtype=jnp.float32) / half)
    ang = positions.astype(jnp.float32)[..., None] * inv
    ang = ang.reshape(ang.shape[:2] + (1,) * (x.ndim - 3) + (half,))
    cos, sin = jnp.cos(ang), jnp.sin(ang)
    x32 = x.astype(jnp.float32)
    x1, x2 = x32[..., :half], x32[..., half:]
    out = jnp.concatenate([x1 * cos - x2 * sin, x2 * cos + x1 * sin], axis=-1)
    return out.astype(x.dtype)


def t5_bucket(rel):
    n = T5_BUCKETS // 2
    ret = jnp.where(rel > 0, n, 0)
    a = jnp.abs(rel)
    max_exact = n // 2
    af = jnp.maximum(a, 1).astype(jnp.float32)
    large = max_exact + (jnp.log(af / max_exact) / math.log(T5_MAX_DIST / max_exact)
                         * (n - max_exact)).astype(jnp.int32)
    large = jnp.minimum(large, n - 1)
    return ret + jnp.where(a < max_exact, a, large)


def to_blocks(t):
    b, s = t.shape[0], t.shape[1]
    t = t.reshape((b, s // Q_BLOCK, Q_BLOCK) + t.shape[2:])
    return jnp.moveaxis(t, 1, 0)


def from_blocks(t):
    t = jnp.moveaxis(t, 0, 1)
    return t.reshape((t.shape[0], t.shape[1] * t.shape[2]) + t.shape[3:])


def chunk_mask(blk, seq):
    q_chunk = (blk * Q_BLOCK + jnp.arange(Q_BLOCK)) // CHUNK
    k_chunk = jnp.arange(seq) // CHUNK
    return k_chunk[None, :] <= q_chunk[:, None]


def mla_attention(q, k, v):
    seq = q.shape[1]
    scale = MLA_QK ** -0.5

    def one(args):
        qb, blk = args
        s = jnp.einsum('bqhd,bkhd->bhqk', qb, k, preferred_element_type=jnp.float32) * scale
        s = jnp.where(chunk_mask(blk, seq)[None, None], s, NEG)
        p = jax.nn.softmax(s, axis=-1)
        return jnp.einsum('bhqk,bkhd->bqhd', p.astype(v.dtype), v)

    o = lax.map(one, (to_blocks(q), jnp.arange(seq // Q_BLOCK)))
    return from_blocks(o)


def diff_attention(q, k, v, positions, t5_table, lam):
    seq = q.shape[1]
    scale = DIFF_QK ** -0.5
    table = t5_table.astype(jnp.float32)

    def one(args):
        qb, pq, blk = args
        s = jnp.einsum('bqhcd,bkhcd->bhcqk', qb, k, preferred_element_type=jnp.float32) * scale
        rel = positions[:, None, :] - pq[:, :, None]
        bias = jnp.moveaxis(table[t5_bucket(rel)], -1, 1)
        s = s + bias[:, :, None]
        s = jnp.where(chunk_mask(blk, seq)[None, None, None], s, NEG)
        p = jax.nn.softmax(s, axis=-1)
        pd = p[:, :, 0] - lam * p[:, :, 1]
        return jnp.einsum('bhqk,bkhd->bqhd', pd.astype(v.dtype), v)

    o = lax.map(one, (to_blocks(q), to_blocks(positions), jnp.arange(seq // Q_BLOCK)))
    return from_blocks(o)


def cross_attention(q, km, vm):
    scale = MEM_HEAD_DIM ** -0.5
    s = jnp.einsum('bqhd,bmhd->bhqm', q, km, preferred_element_type=jnp.float32) * scale
    p = jax.nn.softmax(s, axis=-1)
    return jnp.einsum('bhqm,bmhd->bqhd', p.astype(vm.dtype), vm)


def setup_inputs(seed: int = 0) -> dict:
    key = jax.random.key(seed)
    ks = jax.random.split(key, 32)
    f32 = jnp.float32

    def dense(k, shape, fan_in, scale=1.0):
        return jax.random.normal(k, shape, f32) * (scale * fan_in ** -0.5)

    def gain(k, shape):
        return 1.0 + 0.02 * jax.random.normal(k, shape, f32)

    start = jax.random.randint(ks[2], (BATCH, 1), 0, 4096, dtype=jnp.int32)
    positions = start + jnp.arange(SEQ, dtype=jnp.int32)[None, :]
    return {
        'x': jax.random.normal(ks[0], (BATCH, SEQ, D_MODEL), f32),
        'mem': jax.random.normal(ks[1], (BATCH, MEM_LEN, D_MODEL), f32),
        'positions': positions,
        't5_table': 0.5 * jax.random.normal(ks[3], (T5_BUCKETS, DIFF_HEADS), f32),
        'g_mix': gain(ks[4], (DEPTH, D_MODEL)),
        'g_mem': gain(ks[5], (DEPTH, D_MODEL)),
        'w_in': dense(ks[6], (DEPTH, D_MODEL, D_IN), D_MODEL),
        'g_cq': gain(ks[7], (DEPTH, Q_LORA)),
        'w_uq': dense(ks[8], (DEPTH, Q_LORA, MLA_HEADS * MLA_QK), Q_LORA),
        'g_ckv': gain(ks[9], (DEPTH, KV_LORA)),
        'w_ukv': dense(ks[10], (DEPTH, KV_LORA, MLA_HEADS * (MLA_NOPE + MLA_V)), KV_LORA),
        'g_mla_q': gain(ks[11], (DEPTH, MLA_QK)),
        'g_mla_k': gain(ks[12], (DEPTH, MLA_QK)),
        'g_diff_q': gain(ks[13], (DEPTH, DIFF_QK)),
        'g_diff_k': gain(ks[14], (DEPTH, DIFF_QK)),
        'lam_q1': 0.1 * jax.random.normal(ks[15], (DEPTH, DIFF_QK), f32),
        'lam_k1': 0.1 * jax.random.normal(ks[16], (DEPTH, DIFF_QK), f32),
        'lam_q2': 0.1 * jax.random.normal(ks[17], (DEPTH, DIFF_QK), f32),
        'lam_k2': 0.1 * jax.random.normal(ks[18], (DEPTH, DIFF_QK), f32),
        'g_diff_out': gain(ks[19], (DEPTH, DIFF_V)),
        'w_mem_kv': dense(ks[20], (DEPTH, D_MODEL, 2 * MEM_HEADS * MEM_HEAD_DIM), D_MODEL),
        'g_mem_q': gain(ks[21], (DEPTH, MEM_HEAD_DIM)),
        'g_mem_k': gain(ks[22], (DEPTH, MEM_HEAD_DIM)),
        'w_branch': dense(ks[23], (DEPTH, N_BRANCH, BRANCH_W, D_MODEL), BRANCH_W),
        'w_out': dense(ks[24], (DEPTH, D_MODEL, D_MODEL), D_MODEL, 0.5),
        'g_mlp': gain(ks[25], (DEPTH, D_MODEL)),
        'w_ff1': dense(ks[26], (DEPTH, D_MODEL, D_FF), D_MODEL),
        'w_ff2': dense(ks[27], (DEPTH, D_FF, D_MODEL), D_FF, 0.5),
    }


def reference(x, mem, positions, t5_table, g_mix, g_mem, w_in, g_cq, w_uq, g_ckv, w_ukv,
              g_mla_q, g_mla_k, g_diff_q, g_diff_k, lam_q1, lam_k1, lam_q2, lam_k2,
              g_diff_out, w_mem_kv, g_mem_q, g_mem_k, w_branch, w_out, g_mlp, w_ff1, w_ff2):
    b, s, _ = x.shape
    m = mem.shape[1]
    split_points = np.cumsum(SPLITS)[:-1].tolist()
    for l in range(DEPTH):
        h = rmsnorm(x, g_mix[l])
        z = h @ w_in[l]
        c_q, c_kv, k_r, dq, dk, dv, mq, gl = jnp.split(z, split_points, axis=-1)

        q = (rmsnorm(c_q, g_cq[l]) @ w_uq[l]).reshape(b, s, MLA_HEADS, MLA_QK)
        kv = (rmsnorm(c_kv, g_ckv[l]) @ w_ukv[l]).reshape(b, s, MLA_HEADS, MLA_NOPE + MLA_V)
        k_nope, v_a = kv[..., :MLA_NOPE], kv[..., MLA_NOPE:]
        k = jnp.concatenate(
            [k_nope, jnp.broadcast_to(k_r[:, :, None, :], (b, s, MLA_HEADS, MLA_ROPE))], axis=-1)
        q = rmsnorm(q, g_mla_q[l])
        k = rmsnorm(k, g_mla_k[l])
        q = jnp.concatenate([q[..., :MLA_NOPE], rope(q[..., MLA_NOPE:], positions)], axis=-1)
        k = jnp.concatenate([k[..., :MLA_NOPE], rope(k[..., MLA_NOPE:], positions)], axis=-1)
        o_a = mla_attention(q, k, v_a).reshape(b, s, BRANCH_W)

        lam_init = 0.8 - 0.6 * math.exp(-0.3 * l)
        lam = (jnp.exp(jnp.sum(lam_q1[l].astype(jnp.float32) * lam_k1[l].astype(jnp.float32)))
               - jnp.exp(jnp.sum(lam_q2[l].astype(jnp.float32) * lam_k2[l].astype(jnp.float32)))
               + lam_init)
        dq = rmsnorm(dq.reshape(b, s, DIFF_HEADS, 2, DIFF_QK), g_diff_q[l])
        dk = rmsnorm(dk.reshape(b, s, DIFF_HEADS, 2, DIFF_QK), g_diff_k[l])
        dv = dv.reshape(b, s, DIFF_HEADS, DIFF_V)
        o_b = diff_attention(dq, dk, dv, positions, t5_table, lam)
        o_b = (rmsnorm(o_b, g_diff_out[l]) * (1.0 - lam_init)).reshape(b, s, BRANCH_W)

        hm = rmsnorm(mem, g_mem[l])
        mkv = (hm @ w_mem_kv[l]).reshape(b, m, 2, MEM_HEADS, MEM_HEAD_DIM)
        km = rmsnorm(mkv[:, :, 0], g_mem_k[l])
        vm = mkv[:, :, 1]
        mq = rmsnorm(mq.reshape(b, s, MEM_HEADS, MEM_HEAD_DIM), g_mem_q[l])
        o_c = cross_attention(mq, km, vm).reshape(b, s, BRANCH_W)

        gates = jax.nn.sigmoid(gl.reshape(b, s, N_BRANCH, D_MODEL))
        y = (gates[:, :, 0] * (o_a @ w_branch[l, 0])
             + gates[:, :, 1] * (o_b @ w_branch[l, 1])
             + gates[:, :, 2] * (o_c @ w_branch[l, 2]))
        x = x + y @ w_out[l]

        h2 = rmsnorm(x, g_mlp[l])
        x = x + jnp.square(jax.nn.relu(h2 @ w_ff1[l])) @ w_ff2[l]
    return x
```

```python
import math
import numpy as np
import concourse.bass as bass
import concourse.mybir as mybir
from concourse.bass_utils import run_bass_kernel_spmd

F32 = mybir.dt.float32
BF16 = mybir.dt.bfloat16
I32 = mybir.dt.int32
ALU = mybir.AluOpType
AF = mybir.ActivationFunctionType
AX = mybir.AxisListType

D = 1024
SEQ = 4096
DEPTH = 4
NT = SEQ // 128
NG = SEQ // 512
MEM = 256
D_IN = 5792
C1 = 2720
EPS = 1e-6
ENGS = ("pe", "act", "dve", "pool", "sp")
PI_LO = 3.1415925


class Obj:
    __slots__ = ("name", "w", "r", "dkey", "persist")

    def __init__(self, name="", persist=False):
        self.name = name
        self.w = {}
        self.r = {}
        self.dkey = None
        self.persist = persist


class _Rec:
    def __init__(self):
        self.call = None

    def __getattr__(self, name):
        def f(*a, **k):
            self.call = (name, a, k)
            return self
        return f


class Sched:
    def __init__(self, nc):
        self.nc = nc
        self.prog = {e: [] for e in ENGS}
        self.sems = {}
        self.cur = {}
        self.waited = {e: {} for e in ENGS}
        self.nsem = 0
        self.epoch = -1
        self.live = {}
        self.persist_keys = set()
        self.free_dma = {True: [], False: []}
        self.used_dma = {True: [], False: []}
        self.new_epoch()

    def _alloc(self, key, is_dma):
        h = self.nc.alloc_semaphore(f"s{self.nsem}_{key}")
        self.nsem += 1
        self.sems[key] = [h, 0, is_dma]

    def new_epoch(self):
        self.epoch += 1
        for e in ENGS:
            if e == "sp":
                continue
            key = f"{e}{self.epoch}"
            self._alloc(key, False)
            self.cur[e] = key

    def _waits(self, eng, reads, writes):
        need = {}
        for o in reads:
            for k, v in o.w.items():
                if need.get(k, 0) < v:
                    need[k] = v
        for o in writes:
            for d in (o.w, o.r):
                for k, v in d.items():
                    if need.get(k, 0) < v:
                        need[k] = v
        self._emit_waits(eng, need)

    def _emit_waits(self, eng, need):
        wd = self.waited[eng]
        for k, v in need.items():
            h, total, is_dma = self.sems[k]
            if is_dma:
                v = total
            elif eng == "pe" and k.startswith("pe"):
                continue
            if wd.get(k, 0) >= v:
                continue
            wd[k] = v
            self.prog[eng].append(("wait", h, v))

    def op(self, eng, fn, reads=(), writes=(), inc=True):
        rec = _Rec()
        fn(rec)
        fn = rec.call
        self._waits(eng, reads, writes)
        for o in reads:
            self.live[id(o)] = o
        for o in writes:
            self.live[id(o)] = o
        key = self.cur[eng]
        s = self.sems[key]
        if inc:
            s[1] += 1
            val = s[1]
            self.prog[eng].append(("op", fn, s[0]))
        else:
            val = s[1] + 1
            self.prog[eng].append(("op", fn, None))
        for o in reads:
            o.r[key] = val
        for o in writes:
            o.w = {key: val}
            o.r = {}

    def dma(self, eng, out, in_, reads, writes, slot):
        self._waits(eng, reads, writes)
        for o in list(reads) + list(writes) + [slot]:
            self.live[id(o)] = o
        if slot.dkey is None and slot.persist:
            slot.dkey = f"d{self.nsem}"
            self._alloc(slot.dkey, True)
            self.persist_keys.add(slot.dkey)
        if slot.dkey is None:
            sw = (eng == "pool")
            if self.free_dma[sw]:
                slot.dkey = self.free_dma[sw].pop()
            else:
                slot.dkey = f"d{self.nsem}"
                self._alloc(slot.dkey, True)
            self.used_dma[sw].append(slot.dkey)
        s = self.sems[slot.dkey]
        s[1] += 16
        self.prog[eng].append(("dma", out, in_, s[0]))
        for o in reads:
            o.r[slot.dkey] = s[1]
        for o in writes:
            o.w = {slot.dkey: s[1]}
            o.r = {}

    def barrier(self, final=False):
        need = {}
        for k, (h, total, is_dma) in self.sems.items():
            if k in self.persist_keys and not final:
                continue
            if total > 0 and (is_dma or k in self.cur.values()):
                need[k] = total
        for e in ENGS:
            self._emit_waits(e, dict(need))
        for o in self.live.values():
            if o.persist:
                continue
            o.w = {}
            o.r = {}
            o.dkey = None
        self.live = {}
        for sw in (True, False):
            self.free_dma[sw].extend(self.used_dma[sw])
            self.used_dma[sw] = []

    def replay(self):
        nc = self.nc
        prog = self.prog

        def run(items, e):
            for it in items:
                if it[0] == "wait":
                    e.wait_ge(it[1], it[2])
                elif it[0] == "op":
                    name, a, k = it[1]
                    ins = getattr(e, name)(*a, **k)
                    if it[2] is not None:
                        ins.then_inc(it[2], 1)
                else:
                    e.dma_start(out=it[1], in_=it[2]).then_inc(it[3], 16)

        with nc.Block() as block:
            @block.tensor
            def _(e):
                run(prog["pe"], e)

            @block.scalar
            def _(e):
                run(prog["act"], e)

            @block.vector
            def _(e):
                run(prog["dve"], e)

            @block.gpsimd
            def _(e):
                run(prog["pool"], e)

            @block.sync
            def _(e):
                run(prog["sp"], e)


class Tl:
    __slots__ = ("ap", "o")

    def __init__(self, ap, name=""):
        self.ap = ap
        self.o = Obj(name)


def _dsz(dt):
    return 4 if dt in (F32, I32) else 2


class Arena:
    def __init__(self, nc, name, nbytes):
        self.t = nc.alloc_sbuf_tensor(name, [128, nbytes // 4], F32)
        self.size = nbytes
        self.off = 0
        self.top = nbytes

    def reset(self, keep_top=False):
        self.off = 0
        if not keep_top:
            self.top = self.size

    def tile(self, shape, dt, name="", top=False):
        n = 1
        for s in shape[1:]:
            n *= s
        nb = (n * _dsz(dt) + 31) // 32 * 32
        assert self.off + nb <= self.top, (name, self.off, nb, self.top)
        if top:
            self.top -= nb
            w0 = self.top // 4
            self.off -= nb
        else:
            w0 = self.off // 4
        ap = self.t[0:shape[0], w0:w0 + nb // 4]
        if dt != F32:
            ap = ap.bitcast(dt)
        ap = ap[:, 0:n]
        if len(shape) == 3:
            ap = ap.rearrange("p (a b) -> p a b", a=shape[1], b=shape[2])
        elif len(shape) == 4:
            ap = ap.rearrange("p (a b c) -> p a b c", a=shape[1], b=shape[2], c=shape[3])
        self.off += nb
        return Tl(ap, name)


def _t5_bucket_np(rel):
    n = 16
    ret = np.where(rel > 0, n, 0)
    a = np.abs(rel)
    max_exact = 8
    af = np.maximum(a, 1).astype(np.float32)
    large = max_exact + (np.log(af / np.float32(max_exact)) / np.float32(math.log(128 / max_exact))
                         * np.float32(n - max_exact)).astype(np.int32)
    large = np.minimum(large, n - 1)
    return ret + np.where(a < max_exact, a, large)


GD0 = 511
GLEN = 1280
_rel = GD0 - np.arange(GLEN)
_bk = _t5_bucket_np(_rel)
FAR_BUCKET = int(_t5_bucket_np(np.array([-4095]))[0])
_dfar = -4095
for _d in range(-4095, 64):
    if int(_t5_bucket_np(np.array([_d]))[0]) != FAR_BUCKET:
        break
    _dfar = _d
NEAR = [dl for dl in range(-128 * 8, 0, 128) if dl + 127 > _dfar] + [0, 128, 256, 384]
NEAR_IDX = {dl: i for i, dl in enumerate(NEAR)}
NNEAR = len(NEAR)


def _consts():
    ident = np.eye(128, dtype=np.float32)
    jx = np.ascontiguousarray(ident[::-1])
    oh = np.zeros((32, GLEN), np.float32)
    oh[_bk, np.arange(GLEN)] = 1.0
    k = np.arange(128)[:, None]
    q = np.arange(512)[None, :]
    cm = np.ones((NNEAR, 128, 512), np.float32)
    for dl, i in NEAR_IDX.items():
        if dl >= 0:
            cm[i] = ((dl + k) // 64 <= q // 64).astype(np.float32)
    return ident, jx, oh, cm


class Builder:
    def __init__(self, n_layers=DEPTH, dbg=(), stop_after=None):
        self.n_layers = n_layers
        self.stop_after = stop_after
        self.dbg = set(dbg)
        nc = bass.Bass("TRN2", target_bir_lowering=False)
        self.nc = nc
        self.S = Sched(nc)
        self.inp = {}
        self.per = Arena(nc, "per", 7 * 1024)
        self.ar = Arena(nc, "arena", 198 * 1024)
        self.ps = [Tl(nc.alloc_psum_tensor(f"ps{i}", [128, 512], F32)[:], f"ps{i}") for i in range(8)]

    def din(self, name, shape, dt=F32):
        t = self.nc.dram_tensor(name, list(shape), dt, kind="ExternalInput")
        self.inp[name] = t
        return t

    def dscr(self, name, shape, dt):
        kind = "ExternalOutput" if name in self.dbg else "Internal"
        return self.nc.dram_tensor(name, list(shape), dt, kind=kind)

    def op(self, eng, fn, reads=(), writes=(), inc=True):
        self.S.op(eng, fn, [t.o for t in reads], [t.o for t in writes], inc)

    def load(self, dst, src_ap, eng="sp", dst_ap=None):
        self.S.dma(eng, dst.ap if dst_ap is None else dst_ap, src_ap, [], [dst.o], dst.o)

    def store(self, dst_ap, src, src_ap=None):
        self.S.dma("sp", dst_ap, src.ap if src_ap is None else src_ap, [src.o], [], src.o)

    def psb(self, i):
        return self.ps[i].ap.bitcast(BF16)

    def rstd_from_ssq(self, ssq, dim):
        eps = self.eps_t
        self.op("act", lambda e: e.activation(out=ssq.ap, in_=ssq.ap, func=AF.Sqrt, bias=eps.ap, scale=1.0 / dim),
                [ssq, eps], [ssq])
        self.op("dve", lambda e: e.reciprocal(out=ssq.ap, in_=ssq.ap), [ssq], [ssq])

    def transposes(self, src, src_aps, bank, rows, cols_each):
        pb = self.psb(bank)
        n = len(src_aps)
        for i, a in enumerate(src_aps):
            self.op("pe", (lambda a, i: lambda e: e.transpose(out=pb[0:rows, i * 128:(i + 1) * 128], in_=a,
                                                               identity=self.ident.ap))(a, i),
                    [src, self.ident], [self.ps[bank]], inc=(i == n - 1))
        return pb

    def build(self):
        nc, S = self.nc, self.S
        L = self.n_layers
        x_d = self.din("x", [SEQ, D])
        mem_d = self.din("mem", [MEM, D])
        pos_d = self.din("pos", [128, NT], I32)
        t5_d = self.din("t5_table", [32, 4])
        cid_d = self.din("c_ident", [128, 128])
        cjx_d = self.din("c_jx", [128, 128])
        coh_d = self.din("c_oh", [32, GLEN])
        ccm_d = self.din("c_cm", [NNEAR, 128, 512])
        W = {}
        for nm, shp in (("g_mix", [DEPTH, D]), ("g_mem", [DEPTH, D]), ("w_in", [DEPTH, D, D_IN]),
                        ("g_cq", [DEPTH, 384]), ("w_uq", [DEPTH, 384, 768]), ("g_ckv", [DEPTH, 256]),
                        ("w_ukv", [DEPTH, 256, 1024]), ("g_mla_q", [DEPTH, 96]), ("g_mla_k", [DEPTH, 96]),
                        ("g_diff_q", [DEPTH, 64]), ("g_diff_k", [DEPTH, 64]), ("lam_q1", [DEPTH, 64]),
                        ("lam_k1", [DEPTH, 64]), ("lam_q2", [DEPTH, 64]), ("lam_k2", [DEPTH, 64]),
                        ("g_diff_out", [DEPTH, 128]), ("w_mem_kv", [DEPTH, D, D]), ("g_mem_q", [DEPTH, 128]),
                        ("g_mem_k", [DEPTH, 128]), ("w_branch", [DEPTH, 3, 512, D]), ("w_out", [DEPTH, D, D]),
                        ("g_mlp", [DEPTH, D]), ("w_ff1", [DEPTH, D, 4 * D]), ("w_ff2", [DEPTH, 4 * D, D])):
            W[nm] = self.din(nm, shp)
        self.W = W
        out_d = nc.dram_tensor("out", [SEQ, D], F32, kind="ExternalOutput")
        sc = {}
        sc["hT"] = self.dscr("hTd", [NT, 128, 8, 128], BF16)
        sc["qT"] = self.dscr("qTd", [8, 96, SEQ], BF16)
        sc["kT"] = self.dscr("kTd", [8, 96, SEQ], BF16)
        sc["v"] = self.dscr("vd", [SEQ, 512], BF16)
        sc["dqT"] = self.dscr("dqTd", [512, SEQ], BF16)
        sc["dkT"] = self.dscr("dkTd", [512, SEQ], BF16)
        sc["dv"] = self.dscr("dvd", [SEQ, 512], BF16)
        sc["mqT"] = self.dscr("mqTd", [512, SEQ], BF16)
        sc["oT"] = self.dscr("oTd", [3, 512, SEQ], BF16)
        sc["kmT"] = self.dscr("kmTd", [128, 4, MEM], BF16)
        sc["vm"] = self.dscr("vmd", [MEM, 512], BF16)
        sc["G"] = self.dscr("Gd", [4, GLEN], F32)
        sc["EB"] = self.dscr("EBd", [4, NNEAR, 128, 512], F32)
        self.sc = sc
        WSH = {"w_in": [D, D_IN], "w_uq": [384, 768], "w_ukv": [256, 1024], "w_mem_kv": [D, D], "w_branch": [1536, D],
               "w_out": [D, D], "w_ff1": [D, 4 * D], "w_ff2": [4 * D, D]}
        self.wbf = [{nm: self.nc.dram_tensor(f"bf_{nm}_{l}", shp, BF16, kind="Internal") for nm, shp in WSH.items()} for l in range(L)]
        self.bgq = []
        self.bgobj = {}
        self.bgleft = {}

        def add_conv(grp, nm, l, r0, r1, c0, c1):
            src = (W[nm][l].rearrange("n r c -> (n r) c") if nm == "w_branch" else W[nm][l])[r0:r1, c0:c1]
            dst = self.wbf[l][nm][r0:r1, c0:c1]
            g = self.bgobj.setdefault(grp, Obj(str(grp), persist=True))
            self.bgleft[grp] = self.bgleft.get(grp, 0) + 1

            def emit():
                self.S.dma("pool", dst, src, [], [g], g)
                self.bgleft[grp] -= 1
            self.bgq.append((grp, emit))

        for l in range(L):
            for c in range(8):
                add_conv((l, "A"), "w_mem_kv", l, c * 128, (c + 1) * 128, 0, D)
            for c in range(8):
                add_conv((l, "A"), "w_in", l, c * 128, (c + 1) * 128, 0, C1)
            for c in range(3):
                add_conv((l, "A"), "w_uq", l, c * 128, (c + 1) * 128, 0, 768)
            for c in range(2):
                add_conv((l, "A"), "w_ukv", l, c * 128, (c + 1) * 128, 0, 1024)
            for c in range(8):
                add_conv((l, "B"), "w_in", l, c * 128, (c + 1) * 128, C1, D_IN)
            for c in range(12):
                add_conv((l, "B"), "w_branch", l, c * 128, (c + 1) * 128, 0, D)
            for c in range(8):
                add_conv((l, "B"), "w_out", l, c * 128, (c + 1) * 128, 0, D)
            for c in range(8):
                add_conv((l, "C"), "w_ff1", l, c * 128, (c + 1) * 128, 0, 4 * D)
            for c in range(32):
                add_conv((l, "D"), "w_ff2", l, c * 128, (c + 1) * 128, 0, D)

        per = self.per
        self.ident = per.tile([128, 128], BF16, "ident")
        self.identf = per.tile([128, 128], F32, "identf")
        self.jx = per.tile([128, 128], F32, "jx")
        self.ones_bf = per.tile([128, 128], BF16, "ones_bf")
        self.ones_f = per.tile([128, 128], F32, "ones_f")
        self.eps_t = per.tile([128, 1], F32, "eps")
        self.cos_t = per.tile([128, NT, 16], F32, "cos")
        self.sin_t = per.tile([128, NT, 16], F32, "sin")
        self.b15 = per.tile([128, 4], F32, "b15")
        self.nlam = per.tile([128, 1], F32, "nlam")
        self.gdo = per.tile([128, 1], F32, "gdo")

        self.setup(pos_d, t5_d, cid_d, cjx_d, coh_d, ccm_d)
        stop = self.stop_after
        for l in range(L):
            xin = x_d if l == 0 else out_d
            S.barrier()
            if l > 0:
                S.new_epoch()
            self.phase_mem(l, mem_d)
            S.barrier()
            self.phase_p1(l, xin)
            S.barrier()
            if stop == "p1":
                break
            self.phase_attn(l)
            S.barrier()
            if stop == "attn":
                break
            self.phase_p3(l, xin, out_d)
            S.barrier()
            if stop == "p3":
                break
            self.phase_p4(l, out_d)
        self.bg_pump(len(self.bgq))
        S.barrier(final=True)
        S.replay()
        return nc

    def setup(self, pos_d, t5_d, cid_d, cjx_d, coh_d, ccm_d):
        ar = self.ar
        ar.reset()
        op = self.op
        self.load(self.identf, cid_d[:, :])
        self.load(self.jx, cjx_d[:, :])
        op("dve", lambda e: e.tensor_copy(out=self.ident.ap, in_=self.identf.ap), [self.identf], [self.ident])
        op("dve", lambda e: e.memset(self.ones_bf.ap, 1.0), [], [self.ones_bf])
        op("dve", lambda e: e.memset(self.ones_f.ap, 1.0), [], [self.ones_f])
        op("dve", lambda e: e.memset(self.eps_t.ap, EPS), [], [self.eps_t])
        self.load(self.b15, t5_d[FAR_BUCKET, :].partition_broadcast(128))
        posi = ar.tile([128, NT], I32, "posi")
        posf = ar.tile([128, NT], F32, "posf")
        ang = ar.tile([128, NT, 16], F32, "ang")
        a2 = ar.tile([128, NT * 16], F32, "a2")
        kf = ar.tile([128, NT * 16], F32, "kf")
        ki = ar.tile([128, NT * 16], I32, "ki")
        self.load(posi, pos_d[:, :])
        op("dve", lambda e: e.tensor_copy(out=posf.ap, in_=posi.ap), [posi], [posf])
        inv = np.power(np.float32(10000.0), -np.arange(16, dtype=np.float32) / np.float32(16)).astype(np.float32)
        for j in range(16):
            op("dve", (lambda j: lambda e: e.tensor_scalar(out=ang.ap[:, :, j], in0=posf.ap, scalar1=float(inv[j]),
                                                           scalar2=None, op0=ALU.mult))(j), [posf], [ang])
        angf = ang.ap.rearrange("p t j -> p (t j)")
        for tab, shift in ((self.sin_t, 0.0), (self.cos_t, math.pi / 2)):
            tabf = tab.ap.rearrange("p t j -> p (t j)")
            op("dve", lambda e: e.tensor_scalar(out=a2.ap, in0=angf, scalar1=float(shift), scalar2=None, op0=ALU.add),
               [ang], [a2])
            op("dve", lambda e: e.tensor_scalar(out=ki.ap, in0=a2.ap, scalar1=float(1 / (2 * math.pi)), scalar2=None,
                                                op0=ALU.mult), [a2], [ki])
            op("dve", lambda e: e.tensor_copy(out=kf.ap, in_=ki.ap), [ki], [kf])
            op("dve", lambda e: e.scalar_tensor_tensor(out=a2.ap, in0=kf.ap, scalar=float(-2 * math.pi), in1=a2.ap,
                                                       op0=ALU.mult, op1=ALU.add), [kf, a2], [a2])
            op("dve", lambda e: e.tensor_scalar(out=kf.ap, in0=a2.ap, scalar1=float(math.pi), scalar2=float(2 * math.pi),
                                                op0=ALU.is_gt, op1=ALU.mult), [a2], [kf])
            op("dve", lambda e: e.tensor_tensor(out=a2.ap, in0=a2.ap, in1=kf.ap, op=ALU.subtract), [a2, kf], [a2])
            op("dve", lambda e: e.tensor_scalar(out=kf.ap, in0=a2.ap, scalar1=float(-math.pi), scalar2=float(-2 * math.pi),
                                                op0=ALU.is_lt, op1=ALU.mult), [a2], [kf])
            op("dve", lambda e: e.tensor_tensor(out=a2.ap, in0=a2.ap, in1=kf.ap, op=ALU.subtract), [a2, kf], [a2])
            op("dve", lambda e: e.tensor_scalar(out=a2.ap, in0=a2.ap, scalar1=float(-PI_LO), scalar2=float(PI_LO),
                                                op0=ALU.max, op1=ALU.min), [a2], [a2])
            op("act", (lambda tabf: lambda e: e.activation(out=tabf, in_=a2.ap, func=AF.Sin))(tabf), [a2], [tab])
        tab32 = ar.tile([32, 4], F32, "tab32")
        oh = ar.tile([32, GLEN], F32, "oh")
        gsb = ar.tile([4, GLEN], F32, "gsb")
        self.load(tab32, t5_d[:, :])
        self.load(oh, coh_d[:, :])
        for c0 in range(0, GLEN, 512):
            c1 = min(c0 + 512, GLEN)
            op("pe", (lambda c0, c1: lambda e: e.matmul(self.ps[0].ap[0:4, 0:c1 - c0], lhsT=tab32.ap, rhs=oh.ap[:, c0:c1],
                                                        start=True, stop=True))(c0, c1), [tab32, oh], [self.ps[0]])
            op("dve", (lambda c0, c1: lambda e: e.tensor_copy(out=gsb.ap[:, c0:c1], in_=self.ps[0].ap[0:4, 0:c1 - c0]))(c0, c1),
               [self.ps[0]], [gsb])
        self.store(self.sc["G"][:, :], gsb)
        S = self.S
        S.barrier()
        hk = [ar.tile([128, 512], F32, f"hk{i}") for i in range(2)]
        eb = [ar.tile([128, 512], F32, f"eb{i}") for i in range(2)]
        cm = [ar.tile([128, 512], F32, f"cm{i}") for i in range(2)]
        n = 0
        for h in range(4):
            for dl, i in NEAR_IDX.items():
                s = n % 2
                n += 1
                off = GD0 - dl - 127
                assert 0 <= off and off + 127 + 511 < GLEN
                self.load(hk[s], bass.AP(self.sc["G"], h * GLEN + off, [[1, 128], [1, 512]]))
                self.load(cm[s], ccm_d[i, :, :])
                bk = 1 + s
                op("pe", (lambda s, bk: lambda e: e.matmul(self.ps[bk].ap, lhsT=self.jx.ap, rhs=hk[s].ap, start=True, stop=True))(s, bk),
                   [self.jx, hk[s]], [self.ps[bk]])
                op("act", (lambda s, bk: lambda e: e.activation(out=eb[s].ap, in_=self.ps[bk].ap, func=AF.Exp))(s, bk),
                   [self.ps[bk]], [eb[s]])
                op("dve", (lambda s: lambda e: e.tensor_tensor(out=eb[s].ap, in0=eb[s].ap, in1=cm[s].ap, op=ALU.mult))(s),
                   [eb[s], cm[s]], [eb[s]])
                self.store(self.sc["EB"][h, i, :, :], eb[s])

    def norm_tile(self, xt, gbc, hout, junk, ssq):
        self.op("act", lambda e: e.activation(out=junk.ap, in_=xt.ap, func=AF.Square, accum_out=ssq.ap), [xt], [junk, ssq])
        self.rstd_from_ssq(ssq, D)
        self.op("dve", lambda e: e.scalar_tensor_tensor(out=hout.ap, in0=xt.ap, scalar=ssq.ap, in1=gbc.ap,
                                                        op0=ALU.mult, op1=ALU.mult), [xt, ssq, gbc], [hout])

    def bg_pump(self, n):
        for _ in range(min(n, len(self.bgq))):
            self.bgq.pop(0)[1]()

    def bg_require(self, grp):
        while self.bgleft.get(grp, 0) > 0:
            self.bg_pump(1)
        return self.bgobj[grp]

    def wload(self, dst, src3, nchunk, grp):
        g = self.bg_require(grp)
        for c in range(nchunk):
            self.S.dma("sp", dst.ap[:, c, :], src3[c * 128:(c + 1) * 128, :], [g], [dst.o], dst.o)

    def bcast_load(self, dst, vec_ap):
        self.load(dst, vec_ap.partition_broadcast(128))

    def phase_mem(self, l, mem_d):
        ar = self.ar
        ar.reset()
        op, W = self.op, self.W
        wkv = ar.tile([128, 8, D], BF16, "wkv")
        self.wload(wkv, self.wbf[l]["w_mem_kv"], 8, (l, "A"))
        gmem = ar.tile([128, D], F32, "gmem")
        gk = ar.tile([128, 128], F32, "gmemk")
        self.bcast_load(gmem, W["g_mem"][l])
        self.bcast_load(gk, W["g_mem_k"][l])
        junk = ar.tile([128, D], BF16, "junk")
        kmT = ar.tile([128, 4, MEM], BF16, "kmT")
        for t in range(2):
            xt = ar.tile([128, D], F32, f"mx{t}")
            hm = ar.tile([128, D], BF16, f"hm{t}")
            hmT = ar.tile([128, 8, 128], BF16, f"hmT{t}")
            ssq = ar.tile([128, 1], F32, f"mssq{t}")
            self.load(xt, mem_d[t * 128:(t + 1) * 128, :])
            self.norm_tile(xt, gmem, hm, junk, ssq)
            pb = self.transposes(hm, [hm.ap[:, c * 128:(c + 1) * 128] for c in range(8)], 0, 128, 128)
            op("act", lambda e: e.copy(out=hmT.ap.rearrange("p c t -> p (c t)"), in_=pb), [self.ps[0]], [hmT])
            for cb in range(2):
                bk = 1 + cb
                for c in range(8):
                    op("pe", (lambda c, cb, bk: lambda e: e.matmul(self.ps[bk].ap, lhsT=hmT.ap[:, c, :],
                                                                    rhs=wkv.ap[:, c, cb * 512:(cb + 1) * 512],
                                                                    start=(c == 0), stop=(c == 7)))(c, cb, bk),
                       [hmT, wkv], [self.ps[bk]], inc=(c == 7))
            sq = ar.tile([128, 512], F32, f"msq{t}")
            s4 = ar.tile([128, 4], F32, f"ms4{t}")
            kn = ar.tile([128, 4, 128], F32, f"mkn{t}")
            kb_ = ar.tile([128, 4, 128], BF16, f"mkb{t}")
            vb = ar.tile([128, 512], BF16, f"mvb{t}")
            op("act", lambda e: e.activation(out=sq.ap, in_=self.ps[1].ap, func=AF.Square), [self.ps[1]], [sq])
            op("dve", lambda e: e.tensor_reduce(out=s4.ap, in_=sq.ap.rearrange("p (h d) -> p h d", h=4), axis=AX.X, op=ALU.add),
               [sq], [s4])
            self.rstd_from_ssq(s4, 128)
            op("dve", lambda e: e.tensor_tensor(out=kn.ap, in0=self.ps[1].ap.rearrange("p (h d) -> p h d", h=4),
                                                in1=s4.ap.unsqueeze(2).to_broadcast([128, 4, 128]), op=ALU.mult),
               [self.ps[1], s4], [kn])
            op("dve", lambda e: e.tensor_tensor(out=kb_.ap, in0=kn.ap, in1=gk.ap.unsqueeze(1).to_broadcast([128, 4, 128]),
                                                op=ALU.mult), [kn, gk], [kb_])
            op("act", lambda e: e.copy(out=vb.ap, in_=self.ps[2].ap), [self.ps[2]], [vb])
            self.store(self.sc["vm"][t * 128:(t + 1) * 128, :], vb)
            pb = self.transposes(kb_, [kb_.ap[:, h, :] for h in range(4)], 3, 128, 128)
            op("dve", (lambda t, pb: lambda e: e.tensor_copy(out=kmT.ap[:, :, t * 128:(t + 1) * 128],
                                                             in_=pb[:, 0:512].rearrange("p (h m) -> p h m", h=4)))(t, pb),
               [self.ps[3]], [kmT])
        self.store(self.sc["kmT"][:, :, :], kmT)

    def phase_p1(self, l, xin):
        ar = self.ar
        ar.reset()
        op, W, sc = self.op, self.W, self.sc
        ps = self.ps
        w1 = ar.tile([128, 8, C1], BF16, "w1")
        wuq = ar.tile([128, 3, 768], BF16, "wuq")
        wukv = ar.tile([128, 2, 1024], BF16, "wukv")
        self.wload(w1, self.wbf[l]["w_in"][:, 0:C1], 8, (l, "A"))
        self.wload(wuq, self.wbf[l]["w_uq"], 3, (l, "A"))
        self.wload(wukv, self.wbf[l]["w_ukv"], 2, (l, "A"))
        gmix = ar.tile([128, D], F32, "gmix")
        gcq = ar.tile([128, 384], F32, "gcq")
        gckv = ar.tile([128, 256], F32, "gckv")
        gq = ar.tile([128, 96], F32, "gq")
        gk = ar.tile([128, 96], F32, "gk")
        gdq = ar.tile([128, 64], F32, "gdq")
        gdk = ar.tile([128, 64], F32, "gdk")
        gmq = ar.tile([128, 128], F32, "gmq")
        for t_, nm in ((gmix, "g_mix"), (gcq, "g_cq"), (gckv, "g_ckv"), (gq, "g_mla_q"), (gk, "g_mla_k"),
                       (gdq, "g_diff_q"), (gdk, "g_diff_k"), (gmq, "g_mem_q")):
            self.bcast_load(t_, W[nm][l])
        NB = 2
        junk = [ar.tile([128, D], BF16, f"junk{i}") for i in range(NB)]
        xt = [[ar.tile([128, D], F32, f"xt{i}{j}") for j in range(2)] for i in range(NB)]
        ssq = [ar.tile([128, 1], F32, f"ssq{i}") for i in range(NB)]
        hb = [ar.tile([128, D], BF16, f"hb{i}") for i in range(NB)]
        hT = [ar.tile([128, 8, 128], BF16, f"hT{i}") for i in range(NB)]
        sq_q = [ar.tile([128, 1], F32, f"sqq{i}") for i in range(NB)]
        cqn = [ar.tile([128, 384], BF16, f"cqn{i}") for i in range(NB)]
        cqT = [ar.tile([128, 3, 128], BF16, f"cqT{i}") for i in range(NB)]
        sqt = [ar.tile([128, 1024], F32, f"sqt{i}") for i in range(NB)]
        s8q = [ar.tile([128, 8], F32, f"s8q{i}") for i in range(NB)]
        qn = [ar.tile([128, 8, 96], F32, f"qn{i}") for i in range(NB)]
        qb = [ar.tile([128, 8, 96], BF16, f"qb{i}") for i in range(NB)]
        rt = [ar.tile([128, 4, 8, 16], F32, f"rt{i}") for i in range(NB)]
        qTs = [ar.tile([96, 8, 128], BF16, f"qTs{i}") for i in range(NB)]
        ckr = [ar.tile([128, 32], F32, f"ckr{i}") for i in range(NB)]
        sq_kv = [ar.tile([128, 1], F32, f"sqkv{i}") for i in range(NB)]
        ckvn = [ar.tile([128, 256], BF16, f"ckvn{i}") for i in range(NB)]
        ckvT = [ar.tile([128, 2, 128], BF16, f"ckvT{i}") for i in range(NB)]
        s8k = [ar.tile([128, 8], F32, f"s8k{i}") for i in range(NB)]
        s1k = [ar.tile([128, 1], F32, f"s1k{i}") for i in range(NB)]
        kn = [ar.tile([128, 8, 96], F32, f"kn{i}") for i in range(NB)]
        kb_ = [ar.tile([128, 8, 96], BF16, f"kb{i}") for i in range(NB)]
        kTs = [ar.tile([96, 8, 128], BF16, f"kTs{i}") for i in range(NB)]
        vb = [ar.tile([128, 8, 64], BF16, f"vb{i}") for i in range(NB)]
        s8d = [ar.tile([128, 8], F32, f"s8d{i}") for i in range(NB)]
        dn = [ar.tile([128, 8, 64], F32, f"dn{i}") for i in range(NB)]
        db = [[ar.tile([128, 8, 64], BF16, f"db{j}{i}") for i in range(NB)] for j in range(2)]
        dTs = [[ar.tile([128, 4, 128], BF16, f"dTs{j}{i}") for i in range(NB)] for j in range(2)]
        dvb = [ar.tile([128, 512], BF16, f"dvb{i}") for i in range(NB)]
        s4m = [ar.tile([128, 4], F32, f"s4m{i}") for i in range(NB)]
        mn = [ar.tile([128, 4, 128], F32, f"mn{i}") for i in range(NB)]
        mb = [ar.tile([128, 4, 128], BF16, f"mb{i}") for i in range(NB)]
        mTs = [ar.tile([128, 4, 128], BF16, f"mTs{i}") for i in range(NB)]

        def bc(ap2, shape, axis):
            return ap2.unsqueeze(axis).to_broadcast(shape)

        rtab = {}
        for nm_, g_ in (("q", gq), ("k", gk)):
            tb = ar.tile([128, 4, NT, 16], F32, f"rtab{nm_}")
            for i_, (src_, lo) in enumerate(((self.cos_t, 64), (self.sin_t, 80), (self.cos_t, 80), (self.sin_t, 64))):
                op("pool", (lambda tb, i_, src_, lo, g_: lambda e: e.tensor_tensor(
                    out=tb.ap[:, i_], in0=src_.ap, in1=g_.ap[:, lo:lo + 16].unsqueeze(1).to_broadcast([128, NT, 16]),
                    op=ALU.mult))(tb, i_, src_, lo, g_), [src_, g_], [tb])
            rtab[nm_] = tb

        def rope(src, dst, s, t, which, op=op):
            tb = rtab[which]
            c1, s2, c2, s1 = (bc(tb.ap[:, i_, t, :], [128, 8, 16], 1) for i_ in range(4))
            x1 = src.ap[:, :, 64:80]
            x2 = src.ap[:, :, 80:96]
            r = rt[s]
            op("dve", lambda e: e.tensor_tensor(out=r.ap[:, 0], in0=x1, in1=c1, op=ALU.mult), [src, tb], [r])
            op("dve", lambda e: e.tensor_tensor(out=r.ap[:, 1], in0=x2, in1=s2, op=ALU.mult), [src, tb], [r])
            op("dve", lambda e: e.tensor_tensor(out=r.ap[:, 2], in0=x2, in1=c2, op=ALU.mult), [src, tb], [r])
            op("dve", lambda e: e.tensor_tensor(out=r.ap[:, 3], in0=x1, in1=s1, op=ALU.mult), [src, tb], [r])
            op("dve", lambda e: e.tensor_tensor(out=dst.ap[:, :, 64:80], in0=r.ap[:, 0], in1=r.ap[:, 1], op=ALU.subtract), [r], [dst])
            op("dve", lambda e: e.tensor_tensor(out=dst.ap[:, :, 80:96], in0=r.ap[:, 2], in1=r.ap[:, 3], op=ALU.add), [r], [dst])

        def load_x(t, s):
            self.load(xt[s][(t // NB) % 2], xin[t * 128:(t + 1) * 128, :])

        def rstd_ap(tl, ap, dim):
            op("act", lambda e: e.activation(out=ap, in_=ap, func=AF.Sqrt, bias=self.eps_t.ap, scale=1.0 / dim), [tl, self.eps_t], [tl])
            op("dve", lambda e: e.reciprocal(out=ap, in_=ap), [tl], [tl])

        def rstd_q(qq, tl, ap, dim):
            qq("act", lambda e: e.activation(out=ap, in_=ap, func=AF.Sqrt, bias=self.eps_t.ap, scale=1.0 / dim), [tl, self.eps_t], [tl])
            qq("dve", lambda e: e.reciprocal(out=ap, in_=ap), [tl], [tl])

        def merge(*gs):
            gs = list(gs)
            while gs:
                for g_ in list(gs):
                    try:
                        next(g_)
                        yield
                    except StopIteration:
                        gs.remove(g_)

        class Q:
            def __init__(q):
                q.items = []

            def __call__(q, eng, fn, reads=(), writes=(), inc=True):
                q.items.append((eng, fn, reads, writes, inc))

            def call(q, fn):
                q.items.append((None, fn, None, None, None))

            def flush(q):
                prev = None
                items, q.items = q.items, []
                for eng, fn, r, w, inc in items:
                    if eng is None:
                        fn()
                        continue
                    if prev is not None and eng != prev:
                        yield
                    op(eng, fn, r, w, inc)
                    prev = eng
                yield

        def tile_gen(t, s):
            BT, Z0, Z1, Z2 = 4 * s, 4 * s + 1, 4 * s + 2, 4 * s + 3
            tok = slice(t * 128, (t + 1) * 128)
            x_ = xt[s][(t // NB) % 2]
            if t + NB < NT:
                load_x(t + NB, s)
            self.bg_pump(1)
            q0 = Q()
            q0("act", lambda e: e.activation(out=junk[s].ap, in_=x_.ap, func=AF.Square, accum_out=ssq[s].ap), [x_], [junk[s], ssq[s]])
            q0("act", lambda e: e.activation(out=ssq[s].ap, in_=ssq[s].ap, func=AF.Sqrt, bias=self.eps_t.ap, scale=1.0 / D),
               [ssq[s], self.eps_t], [ssq[s]])
            q0("dve", lambda e: e.reciprocal(out=ssq[s].ap, in_=ssq[s].ap), [ssq[s]], [ssq[s]])
            q0("dve", lambda e: e.scalar_tensor_tensor(out=hb[s].ap, in0=x_.ap, scalar=ssq[s].ap, in1=gmix.ap,
                                                       op0=ALU.mult, op1=ALU.mult), [x_, ssq[s], gmix], [hb[s]])
            yield from q0.flush()
            pb = self.transposes(hb[s], [hb[s].ap[:, c * 128:(c + 1) * 128] for c in range(8)], BT, 128, 128)
            op("act", lambda e: e.copy(out=hT[s].ap.rearrange("p c t -> p (c t)"), in_=pb), [ps[BT]], [hT[s]])
            self.store(sc["hT"][t], hT[s])
            yield

            def zmm_now(bank, c0, c1):
                for c in range(8):
                    op("pe", lambda e: e.matmul(ps[bank].ap[:, 0:c1 - c0], lhsT=hT[s].ap[:, c, :], rhs=w1.ap[:, c, c0:c1],
                                                start=(c == 0), stop=(c == 7)), [hT[s], w1], [ps[bank]], inc=(c == 7))

            def chain_q():
                qq = Q()
                qq.call((lambda *a: (lambda: zmm_now(*a)))(Z0, 0, 384))
                yield from qq.flush()
                qq("act", lambda e: e.activation(out=junk[s].ap[:, 0:384], in_=ps[Z0].ap[:, 0:384], func=AF.Square,
                                                 accum_out=sq_q[s].ap), [ps[Z0]], [junk[s], sq_q[s]])
                rstd_q(qq, sq_q[s], sq_q[s].ap, 384)
                qq("dve", lambda e: e.scalar_tensor_tensor(out=cqn[s].ap, in0=ps[Z0].ap[:, 0:384], scalar=sq_q[s].ap,
                                                           in1=gcq.ap, op0=ALU.mult, op1=ALU.mult), [ps[Z0], sq_q[s], gcq], [cqn[s]])
                yield from qq.flush()
                pb = self.psb(BT)
                qq.call((lambda *a: (lambda: self.transposes(*a)))(cqn[s], [cqn[s].ap[:, c * 128:(c + 1) * 128] for c in range(3)], BT, 128, 128))
                qq("act", lambda e: e.copy(out=cqT[s].ap.rearrange("p c t -> p (c t)"), in_=pb[:, 0:384]), [ps[BT]], [cqT[s]])
                yield from qq.flush()
                for hb_ in range(2):
                    hs = slice(hb_ * 4, (hb_ + 1) * 4)
                    def qup_now(hb_):
                        for c in range(3):
                            op("pe", lambda e: e.matmul(ps[Z0].ap[:, 0:384], lhsT=cqT[s].ap[:, c, :], rhs=wuq.ap[:, c, hb_ * 384:(hb_ + 1) * 384],
                                                        start=(c == 0), stop=(c == 2)), [cqT[s], wuq], [ps[Z0]], inc=(c == 2))
                    qq.call((lambda a: (lambda: qup_now(a)))(hb_))
                    yield from qq.flush()
                    qv = ps[Z0].ap[:, 0:384].rearrange("p (h d) -> p h d", h=4)
                    qq("act", lambda e: e.activation(out=qn[s].ap[:, hs, :], in_=qv, func=AF.Square), [ps[Z0]], [qn[s]])
                    qq("dve", lambda e: e.tensor_reduce(out=s8q[s].ap[:, hs], in_=qn[s].ap[:, hs, :], axis=AX.X, op=ALU.add), [qn[s]], [s8q[s]])
                    rstd_q(qq, s8q[s], s8q[s].ap[:, hs], 96)
                    yield from qq.flush()
                    qq("dve", lambda e: e.tensor_tensor(out=qn[s].ap[:, hs, :], in0=qv, in1=bc(s8q[s].ap[:, hs], [128, 4, 96], 2), op=ALU.mult),
                       [ps[Z0], s8q[s]], [qn[s]])
                    yield from qq.flush()
                qq("dve", lambda e: e.tensor_tensor(out=qb[s].ap[:, :, 0:64], in0=qn[s].ap[:, :, 0:64],
                                                    in1=bc(gq.ap[:, 0:64], [128, 8, 64], 1), op=ALU.mult), [qn[s], gq], [qb[s]])
                rope(qn[s], qb[s], s, t, "q", qq)
                yield from qq.flush()
                pb = self.psb(BT)
                qq.call((lambda *a: (lambda: self.transposes(*a)))(qb[s], [qb[s].ap[:, h, :] for h in range(8)], BT, 96, 128))
                qq("act", lambda e: e.copy(out=qTs[s].ap.rearrange("p h t -> p (h t)"), in_=pb[0:96, :]), [ps[BT]], [qTs[s]])
                qq.call((lambda *a: (lambda: self.store(*a)))(sc["qT"][:, :, tok].rearrange("h d t -> d h t"), qTs[s]))
                yield from qq.flush()

            def chain_k():
                qq = Q()
                qq.call((lambda *a: (lambda: zmm_now(*a)))(Z1, 384, 672))
                yield from qq.flush()
                qq("act", lambda e: e.activation(out=junk[s].ap[:, 384:640], in_=ps[Z1].ap[:, 0:256], func=AF.Square,
                                                 accum_out=sq_kv[s].ap), [ps[Z1]], [junk[s], sq_kv[s]])
                rstd_q(qq, sq_kv[s], sq_kv[s].ap, 256)
                qq("dve", lambda e: e.scalar_tensor_tensor(out=ckvn[s].ap, in0=ps[Z1].ap[:, 0:256], scalar=sq_kv[s].ap,
                                                           in1=gckv.ap, op0=ALU.mult, op1=ALU.mult), [ps[Z1], sq_kv[s], gckv], [ckvn[s]])
                qq("dve", lambda e: e.tensor_copy(out=ckr[s].ap, in_=ps[Z1].ap[:, 256:288]), [ps[Z1]], [ckr[s]])
                yield from qq.flush()
                pb = self.psb(BT)
                qq.call((lambda *a: (lambda: self.transposes(*a)))(ckvn[s], [ckvn[s].ap[:, c * 128:(c + 1) * 128] for c in range(2)], BT, 128, 128))
                qq("act", lambda e: e.copy(out=ckvT[s].ap.rearrange("p c t -> p (c t)"), in_=pb[:, 0:256]), [ps[BT]], [ckvT[s]])
                qq("act", lambda e: e.activation(out=junk[s].ap[:, 640:672], in_=ckr[s].ap, func=AF.Square, accum_out=s1k[s].ap),
                   [ckr[s]], [junk[s], s1k[s]])
                yield from qq.flush()
                for hb_ in range(2):
                    hs = slice(hb_ * 4, (hb_ + 1) * 4)
                    def kvup_now(hb_):
                        for c in range(2):
                            op("pe", lambda e: e.matmul(ps[Z1].ap, lhsT=ckvT[s].ap[:, c, :], rhs=wukv.ap[:, c, hb_ * 512:(hb_ + 1) * 512],
                                                        start=(c == 0), stop=(c == 1)), [ckvT[s], wukv], [ps[Z1]], inc=(c == 1))
                    qq.call((lambda a: (lambda: kvup_now(a)))(hb_))
                    yield from qq.flush()
                    kv = ps[Z1].ap.rearrange("p (h d) -> p h d", h=4)
                    qq("act", lambda e: e.copy(out=vb[s].ap[:, hs, :], in_=kv[:, :, 64:128]), [ps[Z1]], [vb[s]])
                    qq("act", lambda e: e.activation(out=kn[s].ap[:, hs, 0:64], in_=kv[:, :, 0:64], func=AF.Square), [ps[Z1]], [kn[s]])
                    qq("dve", lambda e: e.tensor_reduce(out=s8k[s].ap[:, hs], in_=kn[s].ap[:, hs, 0:64], axis=AX.X, op=ALU.add), [kn[s]], [s8k[s]])
                    qq("dve", lambda e: e.tensor_scalar(out=s8k[s].ap[:, hs], in0=s8k[s].ap[:, hs], scalar1=s1k[s].ap, scalar2=None, op0=ALU.add),
                       [s8k[s], s1k[s]], [s8k[s]])
                    rstd_q(qq, s8k[s], s8k[s].ap[:, hs], 96)
                    yield from qq.flush()
                    qq("dve", lambda e: e.tensor_tensor(out=kn[s].ap[:, hs, 0:64], in0=kv[:, :, 0:64],
                                                        in1=bc(s8k[s].ap[:, hs], [128, 4, 64], 2), op=ALU.mult), [ps[Z1], s8k[s]], [kn[s]])
                    yield from qq.flush()
                qq.call((lambda *a: (lambda: self.store(*a)))(sc["v"][tok, :], vb[s], vb[s].ap.rearrange("p h d -> p (h d)")))
                qq("dve", lambda e: e.tensor_tensor(out=kn[s].ap[:, :, 64:96], in0=bc(ckr[s].ap, [128, 8, 32], 1),
                                                    in1=bc(s8k[s].ap, [128, 8, 32], 2), op=ALU.mult), [ckr[s], s8k[s]], [kn[s]])
                qq("dve", lambda e: e.tensor_tensor(out=kb_[s].ap[:, :, 0:64], in0=kn[s].ap[:, :, 0:64],
                                                    in1=bc(gk.ap[:, 0:64], [128, 8, 64], 1), op=ALU.mult), [kn[s], gk], [kb_[s]])
                yield from qq.flush()
                rope(kn[s], kb_[s], s, t, "k", qq)
                yield from qq.flush()
                pb = self.psb(BT)
                qq.call((lambda *a: (lambda: self.transposes(*a)))(kb_[s], [kb_[s].ap[:, h, :] for h in range(8)], BT, 96, 128))
                qq("act", lambda e: e.copy(out=kTs[s].ap.rearrange("p h t -> p (h t)"), in_=pb[0:96, :]), [ps[BT]], [kTs[s]])
                qq.call((lambda *a: (lambda: self.store(*a)))(sc["kT"][:, :, tok].rearrange("h d t -> d h t"), kTs[s]))
                yield from qq.flush()

            def chain_d():
                qq = Q()
                def diff_part(j, g_):
                    qq("act", lambda e: e.activation(out=sqt[s].ap[:, 0:512], in_=ps[Z2].ap, func=AF.Square), [ps[Z2]], [sqt[s]])
                    qq("dve", lambda e: e.tensor_reduce(out=s8d[s].ap, in_=sqt[s].ap[:, 0:512].rearrange("p (h d) -> p h d", h=8),
                                                        axis=AX.X, op=ALU.add), [sqt[s]], [s8d[s]])
                    rstd_q(qq, s8d[s], s8d[s].ap, 64)
                    qq("dve", lambda e: e.tensor_tensor(out=dn[s].ap, in0=ps[Z2].ap.rearrange("p (h d) -> p h d", h=8),
                                                        in1=bc(s8d[s].ap, [128, 8, 64], 2), op=ALU.mult), [ps[Z2], s8d[s]], [dn[s]])
                    qq("pool", lambda e: e.tensor_tensor(out=db[j][s].ap, in0=dn[s].ap, in1=bc(g_.ap, [128, 8, 64], 1), op=ALU.mult),
                       [dn[s], g_], [db[j][s]])

                def diff_tr(j, dst):
                    dflat = db[j][s].ap.rearrange("p h d -> p (h d)")
                    pb = self.psb(BT)
                    qq.call((lambda *a: (lambda: self.transposes(*a)))(db[j][s], [dflat[:, c * 128:(c + 1) * 128] for c in range(4)], BT, 128, 128))
                    qq("act", lambda e: e.copy(out=dTs[j][s].ap.rearrange("p c t -> p (c t)"), in_=pb[:, 0:512]), [ps[BT]], [dTs[j][s]])
                    qq.call((lambda *a: (lambda: self.store(*a)))(dst[:, tok].rearrange("(c p) t -> p c t", p=128), dTs[j][s]))

                qq.call((lambda *a: (lambda: zmm_now(*a)))(Z2, 672, 1184))
                yield from qq.flush()
                diff_part(0, gdq)
                yield from qq.flush()
                qq.call((lambda *a: (lambda: zmm_now(*a)))(Z2, 1184, 1696))
                yield from qq.flush()
                diff_tr(0, sc["dqT"])
                yield from qq.flush()
                diff_part(1, gdk)
                yield from qq.flush()
                qq.call((lambda *a: (lambda: zmm_now(*a)))(Z2, 1696, 2208))
                yield from qq.flush()
                diff_tr(1, sc["dkT"])
                qq("act", lambda e: e.copy(out=dvb[s].ap, in_=ps[Z2].ap), [ps[Z2]], [dvb[s]])
                qq.call((lambda *a: (lambda: self.store(*a)))(sc["dv"][tok, :], dvb[s]))
                yield from qq.flush()
                qq.call((lambda *a: (lambda: zmm_now(*a)))(Z2, 2208, 2720))
                yield from qq.flush()
                qq("act", lambda e: e.activation(out=sqt[s].ap[:, 512:1024], in_=ps[Z2].ap, func=AF.Square), [ps[Z2]], [sqt[s]])
                qq("dve", lambda e: e.tensor_reduce(out=s4m[s].ap, in_=sqt[s].ap[:, 512:1024].rearrange("p (h d) -> p h d", h=4),
                                                    axis=AX.X, op=ALU.add), [sqt[s]], [s4m[s]])
                rstd_q(qq, s4m[s], s4m[s].ap, 128)
                yield from qq.flush()
                qq("dve", lambda e: e.tensor_tensor(out=mn[s].ap, in0=ps[Z2].ap.rearrange("p (h d) -> p h d", h=4),
                                                    in1=bc(s4m[s].ap, [128, 4, 128], 2), op=ALU.mult), [ps[Z2], s4m[s]], [mn[s]])
                qq("pool", lambda e: e.tensor_tensor(out=mb[s].ap, in0=mn[s].ap, in1=bc(gmq.ap, [128, 4, 128], 1), op=ALU.mult),
                   [mn[s], gmq], [mb[s]])
                yield from qq.flush()
                pb = self.psb(BT)
                qq.call((lambda *a: (lambda: self.transposes(*a)))(mb[s], [mb[s].ap[:, h, :] for h in range(4)], BT, 128, 128))
                qq("act", lambda e: e.copy(out=mTs[s].ap.rearrange("p c t -> p (c t)"), in_=pb[:, 0:512]), [ps[BT]], [mTs[s]])
                qq.call((lambda *a: (lambda: self.store(*a)))(sc["mqT"][:, tok].rearrange("(c p) t -> p c t", p=128), mTs[s]))
                yield from qq.flush()

            for _ in merge(chain_q(), chain_k(), chain_d()):
                yield

        for s in range(NB):
            load_x(s, s)
        gens = [self._chain([(lambda t, s: (lambda: tile_gen(t, s)))(t, s) for t in range(s, NT, NB)]) for s in range(NB)]
        self._interleave(gens, lead=(18, 0))

    @staticmethod
    def _chain(makers):
        for mk in makers:
            for _ in mk():
                yield

    @staticmethod
    def _interleave(gens, lead=()):
        gens = list(gens)
        for g, n in zip(list(gens), lead):
            for _ in range(n):
                try:
                    next(g)
                except StopIteration:
                    gens.remove(g)
                    break
        while gens:
            for g in list(gens):
                try:
                    next(g)
                except StopIteration:
                    gens.remove(g)

    def phase_attn(self, l):
        ar = self.ar
        ar.reset()
        op, W, sc, ps = self.op, self.W, self.sc, self.ps
        wg = ar.tile([128, 8, 3072], BF16, "wg", top=True)
        wb = ar.tile([128, 12, D], BF16, "wb", top=True)
        wo = ar.tile([128, 8, D], BF16, "wo", top=True)
        self.p3w = (wg, wb, wo)
        bg = []
        gB = self.bg_require((l, "B"))
        for dst, src3, nch in ((wg, self.wbf[l]["w_in"][:, C1:D_IN], 8), (wb, self.wbf[l]["w_branch"], 12),
                               (wo, self.wbf[l]["w_out"], 8)):
            for c in range(nch):
                bg.append((lambda dst, src3, c: lambda: self.S.dma("sp", dst.ap[:, c, :], src3[c * 128:(c + 1) * 128, :],
                                                                   [gB], [dst.o], dst.o))(dst, src3, c))
        lam_init = 0.8 - 0.6 * math.exp(-0.3 * l)
        lv = [ar.tile([128, 64], F32, f"lv{i}") for i in range(4)]
        for t_, nm in zip(lv, ("lam_q1", "lam_k1", "lam_q2", "lam_k2")):
            self.bcast_load(t_, W[nm][l])
        lj = ar.tile([128, 64], F32, "lj")
        ls = [ar.tile([128, 1], F32, f"ls{i}") for i in range(2)]
        for i in range(2):
            op("dve", (lambda i: lambda e: e.tensor_tensor(out=lj.ap, in0=lv[2 * i].ap, in1=lv[2 * i + 1].ap, op=ALU.mult))(i),
               [lv[2 * i], lv[2 * i + 1]], [lj])
            op("dve", (lambda i: lambda e: e.tensor_reduce(out=ls[i].ap, in_=lj.ap, axis=AX.X, op=ALU.add))(i), [lj], [ls[i]])
            op("act", (lambda i: lambda e: e.activation(out=ls[i].ap, in_=ls[i].ap, func=AF.Exp))(i), [ls[i]], [ls[i]])
        op("dve", lambda e: e.tensor_tensor(out=self.nlam.ap, in0=ls[1].ap, in1=ls[0].ap, op=ALU.subtract), [ls[0], ls[1]], [self.nlam])
        op("dve", lambda e: e.tensor_scalar(out=self.nlam.ap, in0=self.nlam.ap, scalar1=float(-lam_init), scalar2=None, op0=ALU.add),
           [self.nlam], [self.nlam])
        self.load(self.gdo, W["g_diff_out"][l].rearrange("(p o) -> p o", o=1))
        op("dve", lambda e: e.tensor_scalar(out=self.gdo.ap, in0=self.gdo.ap, scalar1=float(1.0 - lam_init), scalar2=None, op0=ALU.mult),
           [self.gdo], [self.gdo])

        NP = 8
        pT = [ar.tile([128, 512], BF16, f"pT{i}") for i in range(NP)]
        pF = [ar.tile([128, 512], F32, f"pF{i}") for i in range(4)]
        st = {"p": 0, "f": 0, "s": 0}
        base_off = ar.off

        kT = [ar.tile([96, SEQ], BF16, f"kTa{i}") for i in range(3)]
        va = [ar.tile([128, NT, 65], BF16, f"va{i}") for i in range(3)]
        NQ = 6
        qT = [ar.tile([96, 512], BF16, f"qTa{i}") for i in range(NQ)]
        done_items = set()
        osb = [ar.tile([65, 512], F32, f"osb{i}") for i in range(2)]
        rl = [ar.tile([65, 512], F32, f"rl{i}") for i in range(2)]
        on = [[ar.tile([64, 512], BF16, f"on{i}{j}") for j in range(2)] for i in range(2)]
        for i in range(3):
            op("dve", (lambda i: lambda e: e.memset(va[i].ap[:, :, 64:65], 1.0))(i), [], [va[i]])
        scale_a = 96 ** -0.5
        gorder = [7, 0, 6, 1, 5, 2, 4, 3]
        items = [(h, g) for h in range(8) for g in gorder]
        loaded_heads = set()
        nxt = {"n": 0, 0: 0, 1: 0}

        def load_kv_a(h):
            if h in loaded_heads or h >= 8:
                return
            loaded_heads.add(h)
            s3 = h % 3
            self.load(kT[s3], sc["kT"][h, :, :])
            self.load(va[s3], sc["v"][:, h * 64:(h + 1) * 64].rearrange("(b p) d -> p b d", p=128), dst_ap=va[s3].ap[:, :, 0:64])

        def load_q_a(n):
            if n < len(items):
                h, g = items[n]
                assert n < NQ or (n - NQ) in done_items
                self.load(qT[n % NQ], sc["qT"][h, :, g * 512:(g + 1) * 512])

        def mla_item(lane, n):
            h, g = items[n]
            s3 = h % 3
            if bg:
                bg.pop(0)()
            self.bg_pump(1)
            load_kv_a(h)
            load_kv_a(h + 1)
            load_q_a(n + 2)
            q = qT[n % NQ]
            assert all(items[m][0] >= h - 1 for m in range(n) if m not in done_items)
            SB = (4 * lane, 4 * lane + 1)
            ob = 4 * lane + 2
            bcb = 4 * lane + 3
            nkb = 4 * g + 4
            cnt = {"s": 0, "p": 0}

            def qk(kb):
                c0 = max(0, kb - 4 * g) * 128
                bank = SB[cnt["s"] % 2]
                cnt["s"] += 1
                op("pe", lambda e: e.matmul(ps[bank].ap[:, c0:512], lhsT=kT[s3].ap[:, kb * 128:(kb + 1) * 128], rhs=q.ap[:, c0:512],
                                            start=True, stop=True), [kT[s3], q], [ps[bank]])
                p = pT[lane * 4 + cnt["p"] % 4]
                cnt["p"] += 1
                op("act", lambda e: e.activation(out=p.ap[:, c0:512], in_=ps[bank].ap[:, c0:512], func=AF.Exp, scale=float(scale_a)),
                   [ps[bank]], [p])
                if kb >= 4 * g:
                    op("pool", lambda e: e.memset(p.ap[64:128, c0:c0 + 64], 0.0), [], [p])
                return (kb, c0, p)

            def pv(item):
                kb, c0, p = item
                op("pe", lambda e: e.matmul(ps[ob].ap[0:65, c0:512], lhsT=va[s3].ap[:, kb, :], rhs=p.ap[:, c0:512],
                                            start=(kb == 0), stop=(kb == nkb - 1)), [va[s3], p], [ps[ob]])

            pend = []
            for kb in range(nkb):
                pend.append(qk(kb))
                if len(pend) > 1:
                    pv(pend.pop(0))
                if kb == min(nkb - 1, 6) and fin[lane] is not None:
                    fin[lane]()
                    fin[lane] = None
                yield
            while pend:
                pv(pend.pop(0))
            yield
            o_, r_, n_ = osb[lane], rl[lane], on[lane][nxt[lane] % 2]
            nxt[lane] += 1
            op("dve", lambda e: e.tensor_copy(out=o_.ap, in_=ps[ob].ap[0:65, :]), [ps[ob]], [o_])
            op("dve", lambda e: e.reciprocal(out=r_.ap[64:65, :], in_=o_.ap[64:65, :]), [o_], [r_])

            def stage2():
                op("pe", lambda e: e.matmul(ps[bcb].ap[0:64, :], lhsT=self.ones_f.ap[64:65, 0:64], rhs=r_.ap[64:65, :],
                                            start=True, stop=True), [self.ones_f, r_], [ps[bcb]])
                op("dve", lambda e: e.tensor_tensor(out=n_.ap, in0=o_.ap[0:64, :], in1=ps[bcb].ap[0:64, :], op=ALU.mult), [o_, ps[bcb]], [n_])
                self.store(sc["oT"][0, h * 64:(h + 1) * 64, g * 512:(g + 1) * 512], n_)

            fin[lane] = stage2
            done_items.add(n)
            yield

        load_kv_a(0)
        load_q_a(0)
        load_q_a(1)

        fin = [None, None]

        def lane_gen(lane):
            while nxt["n"] < len(items):
                n = nxt["n"]
                nxt["n"] += 1
                for _ in mla_item(lane, n):
                    yield
            if fin[lane] is not None:
                fin[lane]()
                fin[lane] = None

        self._interleave([lane_gen(0), lane_gen(1)])
        while bg:
            bg.pop(0)()
        self.S.barrier()

        ar.off = base_off
        kTd_ = [ar.tile([64, SEQ], BF16, f"kTd{i}") for i in range(4)]
        vd_ = [ar.tile([128, NT, 128], BF16, f"vdd{i}") for i in range(2)]
        qTd_ = [[ar.tile([64, 512], BF16, f"qTd{c}{i}") for i in range(2)] for c in range(2)]
        ebt = [ar.tile([128, NNEAR, 512], F32, f"ebt{i}") for i in range(2)]
        evs = [[ar.tile([128, 512], F32, f"ev{i}{j}") for j in range(4)] for i in range(2)]
        finb = [None]
        obf = [ar.tile([128, 512], BF16, f"obf{i}") for i in range(2)]
        scale_b = 64 ** -0.5
        SBd = ((0, 1), (2, 3))
        OBd = ((4, 5), (6, 7))
        MSB = 0

        def load_h_b(h):
            if h >= 4:
                return
            for c in range(2):
                u = 2 * h + c
                self.load(kTd_[(h % 2) * 2 + c], sc["dkT"][u * 64:(u + 1) * 64, :])
            self.load(vd_[h % 2], sc["dv"][:, h * 128:(h + 1) * 128].rearrange("(b p) d -> p b d", p=128))
            self.load(ebt[h % 2], sc["EB"][h].rearrange("n p q -> p n q"))

        def load_q_b(i):
            if i >= 4 * NG:
                return
            h, g = divmod(i, NG)
            for c in range(2):
                u = 2 * h + c
                self.load(qTd_[c][i % 2], sc["dqT"][u * 64:(u + 1) * 64, g * 512:(g + 1) * 512])

        load_h_b(0)
        load_q_b(0)
        for h in range(4):
            load_h_b(h + 1)
            vt = vd_[h % 2]
            et = ebt[h % 2]
            for g in range(NG):
                i = h * NG + g
                load_q_b(i + 1)
                self.bg_pump(1)
                nkb = 4 * g + 4
                cnt = {"s": 0}

                def qk(kb, c):
                    q = qTd_[c][i % 2]
                    kt = kTd_[(h % 2) * 2 + c]
                    dl = (kb - 4 * g) * 128
                    c0 = max(0, dl)
                    bank = SBd[c][kb % 2]
                    op("pe", lambda e: e.matmul(ps[bank].ap[:, c0:512], lhsT=kt.ap[:, kb * 128:(kb + 1) * 128], rhs=q.ap[:, c0:512],
                                                start=True, stop=True), [kt, q], [ps[bank]])
                    p = pT[c * 4 + kb % 4]
                    if dl in NEAR_IDX:
                        ni = NEAR_IDX[dl]
                        f = pF[c * 2 + kb % 2]
                        op("act", lambda e: e.activation(out=f.ap[:, c0:512], in_=ps[bank].ap[:, c0:512], func=AF.Exp,
                                                         scale=float(scale_b)), [ps[bank]], [f])
                        op("dve", lambda e: e.tensor_tensor(out=p.ap[:, c0:512], in0=f.ap[:, c0:512], in1=et.ap[:, ni, c0:512],
                                                            op=ALU.mult), [f, et], [p])
                    else:
                        op("act", lambda e: e.activation(out=p.ap[:, c0:512], in_=ps[bank].ap[:, c0:512], func=AF.Exp,
                                                         bias=self.b15.ap[:, h:h + 1], scale=float(scale_b)), [ps[bank], self.b15], [p])
                    return (kb, c0, p)

                def pv(item, c):
                    kb, c0, p = item
                    obk, lbk = OBd[c]
                    op("pe", lambda e: e.matmul(ps[obk].ap[:, c0:512], lhsT=vt.ap[:, kb, :], rhs=p.ap[:, c0:512],
                                                start=(kb == 0), stop=(kb == nkb - 1)), [vt, p], [ps[obk]], inc=False)
                    op("pe", lambda e: e.matmul(ps[lbk].ap[:, c0:512], lhsT=self.ones_bf.ap, rhs=p.ap[:, c0:512],
                                                start=(kb == 0), stop=(kb == nkb - 1)), [self.ones_bf, p], [ps[lbk]])

                pend = []
                for kb in range(nkb):
                    cur = [qk(kb, 0), qk(kb, 1)]
                    if pend:
                        pv(pend[0], 0)
                        pv(pend[1], 1)
                    pend = cur
                    if kb == min(nkb - 1, 11) and finb[0] is not None:
                        finb[0]()
                        finb[0] = None
                pv(pend[0], 0)
                pv(pend[1], 1)
                ob_ = obf[i % 2]
                ev = evs[i % 2]
                op("dve", lambda e: e.tensor_copy(out=ev[0].ap, in_=ps[OBd[0][1]].ap), [ps[OBd[0][1]]], [ev[0]])
                op("dve", lambda e: e.tensor_copy(out=ev[1].ap, in_=ps[OBd[0][0]].ap), [ps[OBd[0][0]]], [ev[1]])
                op("dve", lambda e: e.tensor_copy(out=ev[2].ap, in_=ps[OBd[1][1]].ap), [ps[OBd[1][1]]], [ev[2]])
                op("dve", lambda e: e.tensor_copy(out=ev[3].ap, in_=ps[OBd[1][0]].ap), [ps[OBd[1][0]]], [ev[3]])

                def tail(ev=ev, ob_=ob_, h=h, g=g):
                    op("dve", lambda e: e.reciprocal(out=ev[0].ap, in_=ev[0].ap), [ev[0]], [ev[0]])
                    op("pool", lambda e: e.tensor_tensor(out=ev[1].ap, in0=ev[1].ap, in1=ev[0].ap, op=ALU.mult), [ev[0], ev[1]], [ev[1]])
                    op("dve", lambda e: e.reciprocal(out=ev[2].ap, in_=ev[2].ap), [ev[2]], [ev[2]])
                    op("pool", lambda e: e.tensor_tensor(out=ev[3].ap, in0=ev[3].ap, in1=ev[2].ap, op=ALU.mult), [ev[2], ev[3]], [ev[3]])
                    op("dve", lambda e: e.scalar_tensor_tensor(out=ev[1].ap, in0=ev[3].ap, scalar=self.nlam.ap, in1=ev[1].ap,
                                                               op0=ALU.mult, op1=ALU.add), [ev[3], self.nlam, ev[1]], [ev[1]])
                    op("pool", lambda e: e.tensor_tensor(out=ev[0].ap, in0=ev[1].ap, in1=ev[1].ap, op=ALU.mult), [ev[1]], [ev[0]])

                    def tail2():
                        op("pe", lambda e: e.matmul(ps[MSB].ap, lhsT=self.ones_f.ap, rhs=ev[0].ap, start=True, stop=True),
                           [self.ones_f, ev[0]], [ps[MSB]])
                        op("act", lambda e: e.activation(out=ev[2].ap, in_=ps[MSB].ap, func=AF.Ln, bias=self.eps_t.ap, scale=1.0 / 128),
                           [ps[MSB], self.eps_t], [ev[2]])
                        op("act", lambda e: e.activation(out=ev[2].ap, in_=ev[2].ap, func=AF.Exp, scale=-0.5), [ev[2]], [ev[2]])
                        op("dve", lambda e: e.scalar_tensor_tensor(out=ob_.ap, in0=ev[1].ap, scalar=self.gdo.ap, in1=ev[2].ap,
                                                                   op0=ALU.mult, op1=ALU.mult), [ev[1], self.gdo, ev[2]], [ob_])
                        self.store(sc["oT"][1, h * 128:(h + 1) * 128, g * 512:(g + 1) * 512], ob_)
                    return tail2

                finb[0] = tail()
        if finb[0] is not None:
            finb[0]()
            finb[0] = None
        self.S.barrier()

        ar.off = base_off
        r1 = ar.tile([128, 512], F32, "r1c")
        kmT = ar.tile([128, 4, MEM], BF16, "kmTc")
        vm = ar.tile([128, 2, 512], BF16, "vmc")
        qTc = [ar.tile([128, 512], BF16, f"qTc{i}") for i in range(3)]
        ocb = [ar.tile([128, 512], BF16, f"ocb{i}") for i in range(2)]
        self.load(kmT, sc["kmT"][:, :, :])
        self.load(vm, sc["vm"][:, :].rearrange("(b p) d -> p b d", p=128))
        scale_c = 128 ** -0.5
        SBc = (0, 1, 7)
        OBc = ((2, 3), (4, 5))

        def load_q_c(i):
            h, g = divmod(i, NG)
            self.load(qTc[i % 3], sc["mqT"][h * 128:(h + 1) * 128, g * 512:(g + 1) * 512])

        load_q_c(0)
        for h in range(4):
            for g in range(NG):
                i = h * NG + g
                if i + 1 < 4 * NG:
                    load_q_c(i + 1)
                q = qTc[i % 3]
                obk, lbk = OBc[i % 2]
                items = []
                for mbk in range(2):
                    bank = SBc[st["s"] % 3]
                    st["s"] += 1
                    op("pe", (lambda mbk, bank, q: lambda e: e.matmul(ps[bank].ap, lhsT=kmT.ap[:, h, mbk * 128:(mbk + 1) * 128], rhs=q.ap,
                                                                      start=True, stop=True))(mbk, bank, q), [kmT, q], [ps[bank]])
                    p = pT[st["p"] % NP]
                    st["p"] += 1
                    op("act", (lambda bank, p: lambda e: e.activation(out=p.ap, in_=ps[bank].ap, func=AF.Exp, scale=float(scale_c)))(bank, p),
                       [ps[bank]], [p])
                    items.append((mbk, p))
                for mbk, p in items:
                    op("pe", (lambda mbk, p: lambda e: e.matmul(ps[obk].ap, lhsT=vm.ap[:, mbk, h * 128:(h + 1) * 128], rhs=p.ap,
                                                                start=(mbk == 0), stop=(mbk == 1)))(mbk, p), [vm, p], [ps[obk]], inc=False)
                    op("pe", (lambda mbk, p: lambda e: e.matmul(ps[lbk].ap, lhsT=self.ones_bf.ap, rhs=p.ap,
                                                                start=(mbk == 0), stop=(mbk == 1)))(mbk, p), [self.ones_bf, p], [ps[lbk]])
                ob_ = ocb[i % 2]
                op("dve", (lambda lbk: lambda e: e.reciprocal(out=r1.ap, in_=ps[lbk].ap))(lbk), [ps[lbk]], [r1])
                op("dve", (lambda obk, ob_: lambda e: e.tensor_tensor(out=ob_.ap, in0=r1.ap, in1=ps[obk].ap, op=ALU.mult))(obk, ob_),
                   [r1, ps[obk]], [ob_])
                self.store(sc["oT"][2, h * 128:(h + 1) * 128, g * 512:(g + 1) * 512], ob_)

    def phase_p3(self, l, xin, out_d):
        ar = self.ar
        ar.reset(keep_top=True)
        op, W, sc, ps = self.op, self.W, self.sc, self.ps
        wg, wb, wo = self.p3w
        hT = [ar.tile([128, 8, 512], BF16, f"hT3{i}") for i in range(2)]
        oT = [ar.tile([128, 12, 512], BF16, f"oT3{i}") for i in range(2)]
        yT = ar.tile([128, 8, 512], BF16, "yT")
        gs = [ar.tile([128, 512], F32, f"gs{i}") for i in range(6)]
        tm = [ar.tile([128, 512], F32, f"tm{i}") for i in range(4)]
        xo = [ar.tile([128, D], F32, f"xo{i}") for i in range(2)]

        def loads(g):
            s = g % 2
            for tt in range(4):
                self.load(hT[s], sc["hT"][g * 4 + tt], dst_ap=hT[s].ap[:, :, tt * 128:(tt + 1) * 128])
            for n in range(3):
                self.load(oT[s], sc["oT"][n, :, g * 512:(g + 1) * 512].rearrange("(c p) t -> p c t", p=128),
                          dst_ap=oT[s].ap[:, n * 4:(n + 1) * 4, :])

        loads(0)
        k = 0
        for g in range(NG):
            s = g % 2
            self.bg_pump(2)
            if g + 1 < NG:
                loads(g + 1)
            for fc in range(8):
                gset = gs[(fc % 2) * 3:(fc % 2) * 3 + 3]
                for n in range(3):
                    for kc in range(8):
                        op("pe", (lambda n, kc: lambda e: e.matmul(ps[n].ap, lhsT=wg.ap[:, kc, n * D + fc * 128:n * D + (fc + 1) * 128],
                                                                   rhs=hT[s].ap[:, kc, :], start=(kc == 0), stop=(kc == 7)))(n, kc),
                           [wg, hT[s]], [ps[n]], inc=(kc == 7))
                for n in range(3):
                    for c in range(4):
                        op("pe", (lambda n, c: lambda e: e.matmul(ps[3 + n].ap, lhsT=wb.ap[:, n * 4 + c, fc * 128:(fc + 1) * 128],
                                                                  rhs=oT[s].ap[:, n * 4 + c, :], start=(c == 0), stop=(c == 3)))(n, c),
                           [wb, oT[s]], [ps[3 + n]], inc=(c == 3))
                for n in range(3):
                    op("act", (lambda n, gset: lambda e: e.activation(out=gset[n].ap, in_=ps[n].ap, func=AF.Sigmoid))(n, gset),
                       [ps[n]], [gset[n]])
                t0, t1 = tm[(k % 2) * 2], tm[(k % 2) * 2 + 1]
                k += 1
                op("dve", (lambda gset, t0: lambda e: e.tensor_tensor(out=t0.ap, in0=gset[0].ap, in1=ps[3].ap, op=ALU.mult))(gset, t0),
                   [gset[0], ps[3]], [t0])
                op("dve", (lambda gset, t1: lambda e: e.tensor_tensor(out=t1.ap, in0=gset[1].ap, in1=ps[4].ap, op=ALU.mult))(gset, t1),
                   [gset[1], ps[4]], [t1])
                op("dve", (lambda t0, t1: lambda e: e.tensor_tensor(out=t0.ap, in0=t0.ap, in1=t1.ap, op=ALU.add))(t0, t1), [t0, t1], [t0])
                op("dve", (lambda gset, t1: lambda e: e.tensor_tensor(out=t1.ap, in0=gset[2].ap, in1=ps[5].ap, op=ALU.mult))(gset, t1),
                   [gset[2], ps[5]], [t1])
                op("dve", (lambda t0, t1, fc: lambda e: e.tensor_tensor(out=yT.ap[:, fc, :], in0=t0.ap, in1=t1.ap, op=ALU.add))(t0, t1, fc),
                   [t0, t1], [yT])
            for tt in range(4):
                x_o = xo[tt % 2]
                r0 = g * 512 + tt * 128
                self.load(x_o, xin[r0:r0 + 128, :])
                for cb in range(2):
                    bank = 6 + cb
                    for fc in range(8):
                        op("pe", (lambda fc, cb, bank: lambda e: e.matmul(ps[bank].ap, lhsT=yT.ap[:, fc, tt * 128:(tt + 1) * 128],
                                                                          rhs=wo.ap[:, fc, cb * 512:(cb + 1) * 512],
                                                                          start=(fc == 0), stop=(fc == 7)))(fc, cb, bank),
                           [yT, wo], [ps[bank]], inc=(fc == 7))
                    op("dve", (lambda cb, bank, x_o: lambda e: e.tensor_tensor(out=x_o.ap[:, cb * 512:(cb + 1) * 512],
                                                                               in0=x_o.ap[:, cb * 512:(cb + 1) * 512],
                                                                               in1=ps[bank].ap, op=ALU.add))(cb, bank, x_o),
                       [x_o, ps[bank]], [x_o])
                self.store(out_d[r0:r0 + 128, :], x_o)

    def phase_p4(self, l, out_d):
        ar = self.ar
        ar.reset()
        op, W, ps = self.op, self.W, self.ps
        w1 = ar.tile([128, 8, 4 * D], BF16, "wf1")
        w2 = ar.tile([128, 32, D], BF16, "wf2")
        self.wload(w1, self.wbf[l]["w_ff1"], 8, (l, "C"))
        self.wload(w2, self.wbf[l]["w_ff2"], 32, (l, "D"))
        gm = ar.tile([128, D], F32, "gmlp")
        self.bcast_load(gm, W["g_mlp"][l])
        xt = [ar.tile([128, D], F32, f"x4{i}") for i in range(2)]
        hb = [ar.tile([128, D], BF16, f"hb4{i}") for i in range(2)]
        ssq = [ar.tile([128, 1], F32, f"ssq4{i}") for i in range(2)]
        hT = ar.tile([128, 8, 512], BF16, "hT4")
        uT = ar.tile([128, 32, 512], BF16, "uT")
        rr = [ar.tile([128, 512], F32, f"rr{i}") for i in range(2)]
        xo = [ar.tile([128, D], F32, f"xo4{i}") for i in range(2)]

        def loadx(i):
            self.load(xt[i % 2], out_d[i * 128:(i + 1) * 128, :])

        loadx(0)
        k = 0
        for g in range(NG):
            s = g % 2
            self.bg_pump(2)
            for tt in range(4):
                i = g * 4 + tt
                if i + 1 < NT:
                    loadx(i + 1)
                xv = xt[i % 2]
                h_ = hb[tt % 2]
                self.norm_tile(xv, gm, h_, h_, ssq[tt % 2])
                bank = tt % 2
                pb = self.transposes(h_, [h_.ap[:, c * 128:(c + 1) * 128] for c in range(8)], bank, 128, 128)
                op("act", (lambda tt, pb: lambda e: e.copy(out=hT.ap[:, :, tt * 128:(tt + 1) * 128],
                                                           in_=pb.rearrange("p (c t) -> p c t", c=8)))(tt, pb), [ps[bank]], [hT])
            for f in range(32):
                bank = 2 + f % 3
                for kc in range(8):
                    op("pe", (lambda f, kc, bank: lambda e: e.matmul(ps[bank].ap, lhsT=w1.ap[:, kc, f * 128:(f + 1) * 128], rhs=hT.ap[:, kc, :],
                                                                     start=(kc == 0), stop=(kc == 7)))(f, kc, bank),
                       [w1, hT], [ps[bank]], inc=(kc == 7))
                r = rr[k % 2]
                k += 1
                op("act", (lambda bank, r: lambda e: e.activation(out=r.ap, in_=ps[bank].ap, func=AF.Relu))(bank, r), [ps[bank]], [r])
                op("dve", (lambda f, r: lambda e: e.tensor_tensor(out=uT.ap[:, f, :], in0=r.ap, in1=r.ap, op=ALU.mult))(f, r), [r], [uT])
            for tt in range(4):
                x_o = xo[tt % 2]
                r0 = g * 512 + tt * 128
                self.load(x_o, out_d[r0:r0 + 128, :])
                for cb in range(2):
                    bank = 5 + (tt * 2 + cb) % 3
                    for f in range(32):
                        op("pe", (lambda f, cb, bank: lambda e: e.matmul(ps[bank].ap, lhsT=uT.ap[:, f, tt * 128:(tt + 1) * 128],
                                                                         rhs=w2.ap[:, f, cb * 512:(cb + 1) * 512],
                                                                         start=(f == 0), stop=(f == 31)))(f, cb, bank),
                           [uT, w2], [ps[bank]], inc=(f == 31))
                    op("dve", (lambda cb, bank, x_o: lambda e: e.tensor_tensor(out=x_o.ap[:, cb * 512:(cb + 1) * 512],
                                                                               in0=x_o.ap[:, cb * 512:(cb + 1) * 512],
                                                                               in1=ps[bank].ap, op=ALU.add))(cb, bank, x_o),
                       [x_o, ps[bank]], [x_o])
                self.store(out_d[r0:r0 + 128, :], x_o)


WNAMES = ("g_mix", "g_mem", "w_in", "g_cq", "w_uq", "g_ckv", "w_ukv", "g_mla_q", "g_mla_k", "g_diff_q", "g_diff_k",
          "lam_q1", "lam_k1", "lam_q2", "lam_k2", "g_diff_out", "w_mem_kv", "g_mem_q", "g_mem_k", "w_branch", "w_out",
          "g_mlp", "w_ff1", "w_ff2")


def make_in_maps(inputs, cores):
    ident, jx, oh, cm = _consts()
    shared = {k: np.ascontiguousarray(np.asarray(inputs[k], dtype=np.float32)) for k in WNAMES}
    shared["t5_table"] = np.ascontiguousarray(np.asarray(inputs["t5_table"], dtype=np.float32))
    shared.update({"c_ident": ident, "c_jx": jx, "c_oh": oh, "c_cm": cm})
    x = np.asarray(inputs["x"], dtype=np.float32)
    mem = np.asarray(inputs["mem"], dtype=np.float32)
    pos = np.asarray(inputs["positions"]).astype(np.int32)
    maps = []
    for b in cores:
        m = dict(shared)
        m["x"] = np.ascontiguousarray(x[b])
        m["mem"] = np.ascontiguousarray(mem[b])
        m["pos"] = np.ascontiguousarray(pos[b].reshape(NT, 128).T)
        maps.append(m)
    return maps


def kernel(**inputs):
    nc = Builder().build()
    maps = make_in_maps(inputs, range(8))
    res = run_bass_kernel_spmd(nc, maps, core_ids=list(range(8)))
    return np.stack([np.asarray(r["out"], dtype=np.float32) for r in res.results], axis=0)
```

```python
import math
import numpy as np
import concourse.bass as bass
import concourse.mybir as mybir
from concourse.bass_utils import run_bass_kernel_spmd

F32 = mybir.dt.float32
BF16 = mybir.dt.bfloat16
I32 = mybir.dt.int32
ALU = mybir.AluOpType
AF = mybir.ActivationFunctionType
AX = mybir.AxisListType

D = 1024
SEQ = 4096
DEPTH = 4
NT = SEQ // 128
NG = SEQ // 512
MEM = 256
D_IN = 5792
C1 = 2720
EPS = 1e-6
ENGS = ("pe", "act", "dve", "pool", "sp")
PI_LO = 3.1415925


class Obj:
    __slots__ = ("name", "w", "r", "dkey", "persist")

    def __init__(self, name="", persist=False):
        self.name = name
        self.w = {}
        self.r = {}
        self.dkey = None
        self.persist = persist


class _Rec:
    def __init__(self):
        self.call = None

    def __getattr__(self, name):
        def f(*a, **k):
            self.call = (name, a, k)
            return self
        return f


class Sched:
    def __init__(self, nc):
        self.nc = nc
        self.prog = {e: [] for e in ENGS}
        self.sems = {}
        self.cur = {}
        self.waited = {e: {} for e in ENGS}
        self.nsem = 0
        self.epoch = -1
        self.live = {}
        self.persist_keys = set()
        self.free_dma = {True: [], False: []}
        self.used_dma = {True: [], False: []}
        self.new_epoch()

    def _alloc(self, key, is_dma):
        h = self.nc.alloc_semaphore(f"s{self.nsem}_{key}")
        self.nsem += 1
        self.sems[key] = [h, 0, is_dma]

    def new_epoch(self):
        self.epoch += 1
        for e in ENGS:
            if e == "sp":
                continue
            key = f"{e}{self.epoch}"
            self._alloc(key, False)
            self.cur[e] = key

    def _waits(self, eng, reads, writes):
        need = {}
        for o in reads:
            for k, v in o.w.items():
                if need.get(k, 0) < v:
                    need[k] = v
        for o in writes:
            for d in (o.w, o.r):
                for k, v in d.items():
                    if need.get(k, 0) < v:
                        need[k] = v
        self._emit_waits(eng, need)

    def _emit_waits(self, eng, need):
        wd = self.waited[eng]
        for k, v in need.items():
            h, total, is_dma = self.sems[k]
            if is_dma:
                v = total
            elif eng == "pe" and k.startswith("pe"):
                continue
            if wd.get(k, 0) >= v:
                continue
            wd[k] = v
            self.prog[eng].append(("wait", h, v))

    def op(self, eng, fn, reads=(), writes=(), inc=True):
        rec = _Rec()
        fn(rec)
        fn = rec.call
        self._waits(eng, reads, writes)
        for o in reads:
            self.live[id(o)] = o
        for o in writes:
            self.live[id(o)] = o
        key = self.cur[eng]
        s = self.sems[key]
        if inc:
            s[1] += 1
            val = s[1]
            self.prog[eng].append(("op", fn, s[0]))
        else:
            val = s[1] + 1
            self.prog[eng].append(("op", fn, None))
        for o in reads:
            o.r[key] = val
        for o in writes:
            o.w = {key: val}
            o.r = {}

    def dma(self, eng, out, in_, reads, writes, slot):
        need = {}
        for o in reads:
            for k, v in o.w.items():
                if need.get(k, 0) < v:
                    need[k] = v
        for o in writes:
            for k, v in o.r.items():
                if need.get(k, 0) < v:
                    need[k] = v
            for k, v in o.w.items():
                if k != slot.dkey and need.get(k, 0) < v:
                    need[k] = v
        self._emit_waits(eng, need)
        for o in list(reads) + list(writes) + [slot]:
            self.live[id(o)] = o
        if slot.dkey is None and slot.persist:
            slot.dkey = f"d{self.nsem}"
            self._alloc(slot.dkey, True)
            self.persist_keys.add(slot.dkey)
        if slot.dkey is None:
            sw = (eng == "pool")
            if self.free_dma[sw]:
                slot.dkey = self.free_dma[sw].pop()
            else:
                slot.dkey = f"d{self.nsem}"
                self._alloc(slot.dkey, True)
            self.used_dma[sw].append(slot.dkey)
        s = self.sems[slot.dkey]
        s[1] += 16
        self.prog[eng].append(("dma", out, in_, s[0]))
        for o in reads:
            o.r[slot.dkey] = s[1]
        for o in writes:
            o.w = {slot.dkey: s[1]}
            o.r = {}

    def barrier(self, final=False):
        need = {}
        for k, (h, total, is_dma) in self.sems.items():
            if k in self.persist_keys and not final:
                continue
            if total > 0 and (is_dma or k in self.cur.values()):
                need[k] = total
        for e in ENGS:
            self._emit_waits(e, dict(need))
        for o in self.live.values():
            if o.persist:
                continue
            o.w = {}
            o.r = {}
            o.dkey = None
        self.live = {}
        for sw in (True, False):
            self.free_dma[sw].extend(self.used_dma[sw])
            self.used_dma[sw] = []

    def replay(self):
        nc = self.nc
        prog = self.prog

        def run(items, e):
            for it in items:
                if it[0] == "wait":
                    e.wait_ge(it[1], it[2])
                elif it[0] == "op":
                    name, a, k = it[1]
                    ins = getattr(e, name)(*a, **k)
                    if it[2] is not None:
                        ins.then_inc(it[2], 1)
                else:
                    e.dma_start(out=it[1], in_=it[2]).then_inc(it[3], 16)

        with nc.Block() as block:
            @block.tensor
            def _(e):
                run(prog["pe"], e)

            @block.scalar
            def _(e):
                run(prog["act"], e)

            @block.vector
            def _(e):
                run(prog["dve"], e)

            @block.gpsimd
            def _(e):
                run(prog["pool"], e)

            @block.sync
            def _(e):
                run(prog["sp"], e)


class Tl:
    __slots__ = ("ap", "o")

    def __init__(self, ap, name=""):
        self.ap = ap
        self.o = Obj(name)


def _dsz(dt):
    return 4 if dt in (F32, I32) else 2


class Arena:
    def __init__(self, nc, name, nbytes):
        self.t = nc.alloc_sbuf_tensor(name, [128, nbytes // 4], F32)
        self.size = nbytes
        self.off = 0
        self.top = nbytes

    def reset(self, keep_top=False):
        self.off = 0
        if not keep_top:
            self.top = self.size

    def tile(self, shape, dt, name="", top=False):
        n = 1
        for s in shape[1:]:
            n *= s
        nb = (n * _dsz(dt) + 31) // 32 * 32
        assert self.off + nb <= self.top, (name, self.off, nb, self.top)
        if top:
            self.top -= nb
            w0 = self.top // 4
            self.off -= nb
        else:
            w0 = self.off // 4
        ap = self.t[0:shape[0], w0:w0 + nb // 4]
        if dt != F32:
            ap = ap.bitcast(dt)
        ap = ap[:, 0:n]
        if len(shape) == 3:
            ap = ap.rearrange("p (a b) -> p a b", a=shape[1], b=shape[2])
        elif len(shape) == 4:
            ap = ap.rearrange("p (a b c) -> p a b c", a=shape[1], b=shape[2], c=shape[3])
        self.off += nb
        return Tl(ap, name)


def _t5_bucket_np(rel):
    n = 16
    ret = np.where(rel > 0, n, 0)
    a = np.abs(rel)
    max_exact = 8
    af = np.maximum(a, 1).astype(np.float32)
    large = max_exact + (np.log(af / np.float32(max_exact)) / np.float32(math.log(128 / max_exact))
                         * np.float32(n - max_exact)).astype(np.int32)
    large = np.minimum(large, n - 1)
    return ret + np.where(a < max_exact, a, large)


GD0 = 511
GLEN = 1280
_rel = GD0 - np.arange(GLEN)
_bk = _t5_bucket_np(_rel)
FAR_BUCKET = int(_t5_bucket_np(np.array([-4095]))[0])
_dfar = -4095
for _d in range(-4095, 64):
    if int(_t5_bucket_np(np.array([_d]))[0]) != FAR_BUCKET:
        break
    _dfar = _d
NEAR = [dl for dl in range(-128 * 8, 0, 128) if dl + 127 > _dfar] + [0, 128, 256, 384]
NEAR_IDX = {dl: i for i, dl in enumerate(NEAR)}
NNEAR = len(NEAR)


def _consts():
    ident = np.eye(128, dtype=np.float32)
    jx = np.ascontiguousarray(ident[::-1])
    oh = np.zeros((32, GLEN), np.float32)
    oh[_bk, np.arange(GLEN)] = 1.0
    k = np.arange(128)[:, None]
    q = np.arange(512)[None, :]
    cm = np.ones((NNEAR, 128, 512), np.float32)
    for dl, i in NEAR_IDX.items():
        if dl >= 0:
            cm[i] = ((dl + k) // 64 <= q // 64).astype(np.float32)
    return ident, jx, oh, cm


class Builder:
    def __init__(self, n_layers=DEPTH, dbg=(), stop_after=None):
        self.n_layers = n_layers
        self.stop_after = stop_after
        self.dbg = set(dbg)
        nc = bass.Bass("TRN2", target_bir_lowering=False)
        self.nc = nc
        self.S = Sched(nc)
        self.inp = {}
        self.per = Arena(nc, "per", 7 * 1024)
        self.ar = Arena(nc, "arena", 198 * 1024)
        self.ps = [Tl(nc.alloc_psum_tensor(f"ps{i}", [128, 512], F32)[:], f"ps{i}") for i in range(8)]

    def din(self, name, shape, dt=F32):
        t = self.nc.dram_tensor(name, list(shape), dt, kind="ExternalInput")
        self.inp[name] = t
        return t

    def dscr(self, name, shape, dt):
        kind = "ExternalOutput" if name in self.dbg else "Internal"
        return self.nc.dram_tensor(name, list(shape), dt, kind=kind)

    def op(self, eng, fn, reads=(), writes=(), inc=True):
        self.S.op(eng, fn, [t.o for t in reads], [t.o for t in writes], inc)

    def load(self, dst, src_ap, eng="sp", dst_ap=None):
        self.S.dma(eng, dst.ap if dst_ap is None else dst_ap, src_ap, [], [dst.o], dst.o)

    def store(self, dst_ap, src, src_ap=None):
        self.S.dma("sp", dst_ap, src.ap if src_ap is None else src_ap, [src.o], [], src.o)

    def psb(self, i):
        return self.ps[i].ap.bitcast(BF16)

    def rstd_from_ssq(self, ssq, dim):
        eps = self.eps_t
        self.op("act", lambda e: e.activation(out=ssq.ap, in_=ssq.ap, func=AF.Sqrt, bias=eps.ap, scale=1.0 / dim),
                [ssq, eps], [ssq])
        self.op("dve", lambda e: e.reciprocal(out=ssq.ap, in_=ssq.ap), [ssq], [ssq])

    def transposes(self, src, src_aps, bank, rows, cols_each):
        pb = self.psb(bank)
        n = len(src_aps)
        for i, a in enumerate(src_aps):
            self.op("pe", (lambda a, i: lambda e: e.transpose(out=pb[0:rows, i * 128:(i + 1) * 128], in_=a,
                                                               identity=self.ident.ap))(a, i),
                    [src, self.ident], [self.ps[bank]], inc=(i == n - 1))
        return pb

    def build(self):
        nc, S = self.nc, self.S
        L = self.n_layers
        x_d = self.din("x", [SEQ, D])
        mem_d = self.din("mem", [MEM, D])
        pos_d = self.din("pos", [128, NT], I32)
        t5_d = self.din("t5_table", [32, 4])
        cid_d = self.din("c_ident", [128, 128])
        cjx_d = self.din("c_jx", [128, 128])
        coh_d = self.din("c_oh", [32, GLEN])
        ccm_d = self.din("c_cm", [NNEAR, 128, 512])
        W = {}
        for nm, shp in (("g_mix", [DEPTH, D]), ("g_mem", [DEPTH, D]), ("w_in", [DEPTH, D, D_IN]),
                        ("g_cq", [DEPTH, 384]), ("w_uq", [DEPTH, 384, 768]), ("g_ckv", [DEPTH, 256]),
                        ("w_ukv", [DEPTH, 256, 1024]), ("g_mla_q", [DEPTH, 96]), ("g_mla_k", [DEPTH, 96]),
                        ("g_diff_q", [DEPTH, 64]), ("g_diff_k", [DEPTH, 64]), ("lam_q1", [DEPTH, 64]),
                        ("lam_k1", [DEPTH, 64]), ("lam_q2", [DEPTH, 64]), ("lam_k2", [DEPTH, 64]),
                        ("g_diff_out", [DEPTH, 128]), ("w_mem_kv", [DEPTH, D, D]), ("g_mem_q", [DEPTH, 128]),
                        ("g_mem_k", [DEPTH, 128]), ("w_branch", [DEPTH, 3, 512, D]), ("w_out", [DEPTH, D, D]),
                        ("g_mlp", [DEPTH, D]), ("w_ff1", [DEPTH, D, 4 * D]), ("w_ff2", [DEPTH, 4 * D, D])):
            W[nm] = self.din(nm, shp)
        self.W = W
        out_d = nc.dram_tensor("out", [SEQ, D], F32, kind="ExternalOutput")
        sc = {}
        sc["hT"] = self.dscr("hTd", [NT, 128, 8, 128], BF16)
        sc["qT"] = self.dscr("qTd", [8, 96, SEQ], BF16)
        sc["kT"] = self.dscr("kTd", [8, 96, SEQ], BF16)
        sc["v"] = self.dscr("vd", [SEQ, 512], BF16)
        sc["dqT"] = self.dscr("dqTd", [512, SEQ], BF16)
        sc["dkT"] = self.dscr("dkTd", [512, SEQ], BF16)
        sc["dv"] = self.dscr("dvd", [SEQ, 512], BF16)
        sc["mqT"] = self.dscr("mqTd", [512, SEQ], BF16)
        sc["oT"] = self.dscr("oTd", [3, 512, SEQ], BF16)
        sc["kmT"] = self.dscr("kmTd", [128, 4, MEM], BF16)
        sc["vm"] = self.dscr("vmd", [MEM, 512], BF16)
        sc["G"] = self.dscr("Gd", [4, GLEN], F32)
        sc["EB"] = self.dscr("EBd", [4, NNEAR, 128, 512], F32)
        self.sc = sc
        WSH = {"w_in": [D, D_IN], "w_uq": [384, 768], "w_ukv": [256, 1024], "w_mem_kv": [D, D], "w_branch": [1536, D],
               "w_out": [D, D], "w_ff1": [D, 4 * D], "w_ff2": [4 * D, D]}
        self.wbf = [{nm: self.nc.dram_tensor(f"bf_{nm}_{l}", shp, BF16, kind="Internal") for nm, shp in WSH.items()} for l in range(L)]
        self.bgq = []
        self.bgobj = {}
        self.bgleft = {}

        def add_conv(grp, nm, l, r0, r1, c0, c1):
            src = (W[nm][l].rearrange("n r c -> (n r) c") if nm == "w_branch" else W[nm][l])[r0:r1, c0:c1]
            dst = self.wbf[l][nm][r0:r1, c0:c1]
            g = self.bgobj.setdefault(grp, Obj(str(grp), persist=True))
            self.bgleft[grp] = self.bgleft.get(grp, 0) + 1

            def emit():
                self.S.dma("pool", dst, src, [], [g], g)
                self.bgleft[grp] -= 1
            self.bgq.append((grp, emit))

        for l in range(L):
            for c in range(8):
                add_conv((l, "A"), "w_mem_kv", l, c * 128, (c + 1) * 128, 0, D)
            for c in range(8):
                add_conv((l, "A"), "w_in", l, c * 128, (c + 1) * 128, 0, C1)
            for c in range(3):
                add_conv((l, "A"), "w_uq", l, c * 128, (c + 1) * 128, 0, 768)
            for c in range(2):
                add_conv((l, "A"), "w_ukv", l, c * 128, (c + 1) * 128, 0, 1024)
            for c in range(8):
                add_conv((l, "B"), "w_in", l, c * 128, (c + 1) * 128, C1, D_IN)
            for c in range(12):
                add_conv((l, "B"), "w_branch", l, c * 128, (c + 1) * 128, 0, D)
            for c in range(8):
                add_conv((l, "B"), "w_out", l, c * 128, (c + 1) * 128, 0, D)
            for c in range(8):
                add_conv((l, "C"), "w_ff1", l, c * 128, (c + 1) * 128, 0, 4 * D)
            for c in range(32):
                add_conv((l, "D"), "w_ff2", l, c * 128, (c + 1) * 128, 0, D)

        per = self.per
        self.ident = per.tile([128, 128], BF16, "ident")
        self.identf = per.tile([128, 128], F32, "identf")
        self.jx = per.tile([128, 128], F32, "jx")
        self.ones_bf = per.tile([128, 128], BF16, "ones_bf")
        self.ones_f = per.tile([128, 128], F32, "ones_f")
        self.eps_t = per.tile([128, 1], F32, "eps")
        self.cos_t = per.tile([128, NT, 16], F32, "cos")
        self.sin_t = per.tile([128, NT, 16], F32, "sin")
        self.b15 = per.tile([128, 4], F32, "b15")
        self.nlam = per.tile([128, 1], F32, "nlam")
        self.gdo = per.tile([128, 1], F32, "gdo")

        self.setup(pos_d, t5_d, cid_d, cjx_d, coh_d, ccm_d)
        stop = self.stop_after
        for l in range(L):
            xin = x_d if l == 0 else out_d
            S.barrier()
            if l > 0:
                S.new_epoch()
            self.phase_mem(l, mem_d)
            S.barrier()
            self.phase_p1(l, xin)
            S.barrier()
            if stop == "p1":
                break
            self.phase_attn(l)
            S.barrier()
            if stop == "attn":
                break
            self.phase_p3(l, xin, out_d)
            S.barrier()
            if stop == "p3":
                break
            self.phase_p4(l, out_d)
        self.bg_pump(len(self.bgq))
        S.barrier(final=True)
        S.replay()
        return nc

    def setup(self, pos_d, t5_d, cid_d, cjx_d, coh_d, ccm_d):
        ar = self.ar
        ar.reset()
        op = self.op
        self.load(self.identf, cid_d[:, :])
        self.load(self.jx, cjx_d[:, :])
        op("dve", lambda e: e.tensor_copy(out=self.ident.ap, in_=self.identf.ap), [self.identf], [self.ident])
        op("dve", lambda e: e.memset(self.ones_bf.ap, 1.0), [], [self.ones_bf])
        op("dve", lambda e: e.memset(self.ones_f.ap, 1.0), [], [self.ones_f])
        op("dve", lambda e: e.memset(self.eps_t.ap, EPS), [], [self.eps_t])
        self.load(self.b15, t5_d[FAR_BUCKET, :].partition_broadcast(128))
        posi = ar.tile([128, NT], I32, "posi")
        posf = ar.tile([128, NT], F32, "posf")
        ang = ar.tile([128, NT, 16], F32, "ang")
        a2 = ar.tile([128, NT * 16], F32, "a2")
        kf = ar.tile([128, NT * 16], F32, "kf")
        ki = ar.tile([128, NT * 16], I32, "ki")
        self.load(posi, pos_d[:, :])
        op("dve", lambda e: e.tensor_copy(out=posf.ap, in_=posi.ap), [posi], [posf])
        inv = np.power(np.float32(10000.0), -np.arange(16, dtype=np.float32) / np.float32(16)).astype(np.float32)
        for j in range(16):
            op("dve", (lambda j: lambda e: e.tensor_scalar(out=ang.ap[:, :, j], in0=posf.ap, scalar1=float(inv[j]),
                                                           scalar2=None, op0=ALU.mult))(j), [posf], [ang])
        angf = ang.ap.rearrange("p t j -> p (t j)")
        for tab, shift in ((self.sin_t, 0.0), (self.cos_t, math.pi / 2)):
            tabf = tab.ap.rearrange("p t j -> p (t j)")
            op("dve", lambda e: e.tensor_scalar(out=a2.ap, in0=angf, scalar1=float(shift), scalar2=None, op0=ALU.add),
               [ang], [a2])
            op("dve", lambda e: e.tensor_scalar(out=ki.ap, in0=a2.ap, scalar1=float(1 / (2 * math.pi)), scalar2=None,
                                                op0=ALU.mult), [a2], [ki])
            op("dve", lambda e: e.tensor_copy(out=kf.ap, in_=ki.ap), [ki], [kf])
            op("dve", lambda e: e.scalar_tensor_tensor(out=a2.ap, in0=kf.ap, scalar=float(-2 * math.pi), in1=a2.ap,
                                                       op0=ALU.mult, op1=ALU.add), [kf, a2], [a2])
            op("dve", lambda e: e.tensor_scalar(out=kf.ap, in0=a2.ap, scalar1=float(math.pi), scalar2=float(2 * math.pi),
                                                op0=ALU.is_gt, op1=ALU.mult), [a2], [kf])
            op("dve", lambda e: e.tensor_tensor(out=a2.ap, in0=a2.ap, in1=kf.ap, op=ALU.subtract), [a2, kf], [a2])
            op("dve", lambda e: e.tensor_scalar(out=kf.ap, in0=a2.ap, scalar1=float(-math.pi), scalar2=float(-2 * math.pi),
                                                op0=ALU.is_lt, op1=ALU.mult), [a2], [kf])
            op("dve", lambda e: e.tensor_tensor(out=a2.ap, in0=a2.ap, in1=kf.ap, op=ALU.subtract), [a2, kf], [a2])
            op("dve", lambda e: e.tensor_scalar(out=a2.ap, in0=a2.ap, scalar1=float(-PI_LO), scalar2=float(PI_LO),
                                                op0=ALU.max, op1=ALU.min), [a2], [a2])
            op("act", (lambda tabf: lambda e: e.activation(out=tabf, in_=a2.ap, func=AF.Sin))(tabf), [a2], [tab])
        tab32 = ar.tile([32, 4], F32, "tab32")
        oh = ar.tile([32, GLEN], F32, "oh")
        gsb = ar.tile([4, GLEN], F32, "gsb")
        self.load(tab32, t5_d[:, :])
        self.load(oh, coh_d[:, :])
        for c0 in range(0, GLEN, 512):
            c1 = min(c0 + 512, GLEN)
            op("pe", (lambda c0, c1: lambda e: e.matmul(self.ps[0].ap[0:4, 0:c1 - c0], lhsT=tab32.ap, rhs=oh.ap[:, c0:c1],
                                                        start=True, stop=True))(c0, c1), [tab32, oh], [self.ps[0]])
            op("dve", (lambda c0, c1: lambda e: e.tensor_copy(out=gsb.ap[:, c0:c1], in_=self.ps[0].ap[0:4, 0:c1 - c0]))(c0, c1),
               [self.ps[0]], [gsb])
        self.store(self.sc["G"][:, :], gsb)
        S = self.S
        S.barrier()
        hk = [ar.tile([128, 512], F32, f"hk{i}") for i in range(2)]
        eb = [ar.tile([128, 512], F32, f"eb{i}") for i in range(2)]
        cm = [ar.tile([128, 512], F32, f"cm{i}") for i in range(2)]
        n = 0
        for h in range(4):
            for dl, i in NEAR_IDX.items():
                s = n % 2
                n += 1
                off = GD0 - dl - 127
                assert 0 <= off and off + 127 + 511 < GLEN
                self.load(hk[s], bass.AP(self.sc["G"], h * GLEN + off, [[1, 128], [1, 512]]))
                self.load(cm[s], ccm_d[i, :, :])
                bk = 1 + s
                op("pe", (lambda s, bk: lambda e: e.matmul(self.ps[bk].ap, lhsT=self.jx.ap, rhs=hk[s].ap, start=True, stop=True))(s, bk),
                   [self.jx, hk[s]], [self.ps[bk]])
                op("act", (lambda s, bk: lambda e: e.activation(out=eb[s].ap, in_=self.ps[bk].ap, func=AF.Exp))(s, bk),
                   [self.ps[bk]], [eb[s]])
                op("dve", (lambda s: lambda e: e.tensor_tensor(out=eb[s].ap, in0=eb[s].ap, in1=cm[s].ap, op=ALU.mult))(s),
                   [eb[s], cm[s]], [eb[s]])
                self.store(self.sc["EB"][h, i, :, :], eb[s])

    def norm_tile(self, xt, gbc, hout, junk, ssq):
        self.op("act", lambda e: e.activation(out=junk.ap, in_=xt.ap, func=AF.Square, accum_out=ssq.ap), [xt], [junk, ssq])
        self.rstd_from_ssq(ssq, D)
        self.op("dve", lambda e: e.scalar_tensor_tensor(out=hout.ap, in0=xt.ap, scalar=ssq.ap, in1=gbc.ap,
                                                        op0=ALU.mult, op1=ALU.mult), [xt, ssq, gbc], [hout])

    def bg_pump(self, n):
        for _ in range(min(n, len(self.bgq))):
            self.bgq.pop(0)[1]()

    def bg_require(self, grp):
        while self.bgleft.get(grp, 0) > 0:
            self.bg_pump(1)
        return self.bgobj[grp]

    def wload(self, dst, src3, nchunk, grp):
        g = self.bg_require(grp)
        for c in range(nchunk):
            self.S.dma("sp", dst.ap[:, c, :], src3[c * 128:(c + 1) * 128, :], [g], [dst.o], dst.o)

    def bcast_load(self, dst, vec_ap):
        self.load(dst, vec_ap.partition_broadcast(128))

    def phase_mem(self, l, mem_d):
        ar = self.ar
        ar.reset()
        op, W = self.op, self.W
        wkv = ar.tile([128, 8, D], BF16, "wkv")
        self.wload(wkv, self.wbf[l]["w_mem_kv"], 8, (l, "A"))
        gmem = ar.tile([128, D], F32, "gmem")
        gk = ar.tile([128, 128], F32, "gmemk")
        self.bcast_load(gmem, W["g_mem"][l])
        self.bcast_load(gk, W["g_mem_k"][l])
        junk = ar.tile([128, D], BF16, "junk")
        kmT = ar.tile([128, 4, MEM], BF16, "kmT")
        for t in range(2):
            xt = ar.tile([128, D], F32, f"mx{t}")
            hm = ar.tile([128, D], BF16, f"hm{t}")
            hmT = ar.tile([128, 8, 128], BF16, f"hmT{t}")
            ssq = ar.tile([128, 1], F32, f"mssq{t}")
            self.load(xt, mem_d[t * 128:(t + 1) * 128, :])
            self.norm_tile(xt, gmem, hm, junk, ssq)
            pb = self.transposes(hm, [hm.ap[:, c * 128:(c + 1) * 128] for c in range(8)], 0, 128, 128)
            op("act", lambda e: e.copy(out=hmT.ap.rearrange("p c t -> p (c t)"), in_=pb), [self.ps[0]], [hmT])
            for cb in range(2):
                bk = 1 + cb
                for c in range(8):
                    op("pe", (lambda c, cb, bk: lambda e: e.matmul(self.ps[bk].ap, lhsT=hmT.ap[:, c, :],
                                                                    rhs=wkv.ap[:, c, cb * 512:(cb + 1) * 512],
                                                                    start=(c == 0), stop=(c == 7)))(c, cb, bk),
                       [hmT, wkv], [self.ps[bk]], inc=(c == 7))
            sq = ar.tile([128, 512], F32, f"msq{t}")
            s4 = ar.tile([128, 4], F32, f"ms4{t}")
            kn = ar.tile([128, 4, 128], F32, f"mkn{t}")
            kb_ = ar.tile([128, 4, 128], BF16, f"mkb{t}")
            vb = ar.tile([128, 512], BF16, f"mvb{t}")
            op("act", lambda e: e.activation(out=sq.ap, in_=self.ps[1].ap, func=AF.Square), [self.ps[1]], [sq])
            op("dve", lambda e: e.tensor_reduce(out=s4.ap, in_=sq.ap.rearrange("p (h d) -> p h d", h=4), axis=AX.X, op=ALU.add),
               [sq], [s4])
            self.rstd_from_ssq(s4, 128)
            op("dve", lambda e: e.tensor_tensor(out=kn.ap, in0=self.ps[1].ap.rearrange("p (h d) -> p h d", h=4),
                                                in1=s4.ap.unsqueeze(2).to_broadcast([128, 4, 128]), op=ALU.mult),
               [self.ps[1], s4], [kn])
            op("dve", lambda e: e.tensor_tensor(out=kb_.ap, in0=kn.ap, in1=gk.ap.unsqueeze(1).to_broadcast([128, 4, 128]),
                                                op=ALU.mult), [kn, gk], [kb_])
            op("act", lambda e: e.copy(out=vb.ap, in_=self.ps[2].ap), [self.ps[2]], [vb])
            self.store(self.sc["vm"][t * 128:(t + 1) * 128, :], vb)
            pb = self.transposes(kb_, [kb_.ap[:, h, :] for h in range(4)], 3, 128, 128)
            op("dve", (lambda t, pb: lambda e: e.tensor_copy(out=kmT.ap[:, :, t * 128:(t + 1) * 128],
                                                             in_=pb[:, 0:512].rearrange("p (h m) -> p h m", h=4)))(t, pb),
               [self.ps[3]], [kmT])
        self.store(self.sc["kmT"][:, :, :], kmT)

    def phase_p1(self, l, xin):
        ar = self.ar
        ar.reset()
        op, W, sc = self.op, self.W, self.sc
        ps = self.ps
        w1 = ar.tile([128, 8, C1], BF16, "w1")
        wuq = ar.tile([128, 3, 768], BF16, "wuq")
        wukv = ar.tile([128, 2, 1024], BF16, "wukv")
        self.wload(w1, self.wbf[l]["w_in"][:, 0:C1], 8, (l, "A"))
        self.wload(wuq, self.wbf[l]["w_uq"], 3, (l, "A"))
        self.wload(wukv, self.wbf[l]["w_ukv"], 2, (l, "A"))
        gmix = ar.tile([128, D], F32, "gmix")
        gcq = ar.tile([128, 384], F32, "gcq")
        gckv = ar.tile([128, 256], F32, "gckv")
        gq = ar.tile([128, 96], F32, "gq")
        gk = ar.tile([128, 96], F32, "gk")
        gdq = ar.tile([128, 64], F32, "gdq")
        gdk = ar.tile([128, 64], F32, "gdk")
        gmq = ar.tile([128, 128], F32, "gmq")
        for t_, nm in ((gmix, "g_mix"), (gcq, "g_cq"), (gckv, "g_ckv"), (gq, "g_mla_q"), (gk, "g_mla_k"),
                       (gdq, "g_diff_q"), (gdk, "g_diff_k"), (gmq, "g_mem_q")):
            self.bcast_load(t_, W[nm][l])
        NB = 2
        junk = [ar.tile([128, D], BF16, f"junk{i}") for i in range(NB)]
        xt = [[ar.tile([128, D], F32, f"xt{i}{j}") for j in range(2)] for i in range(NB)]
        ssq = [ar.tile([128, 1], F32, f"ssq{i}") for i in range(NB)]
        hb = [ar.tile([128, D], BF16, f"hb{i}") for i in range(NB)]
        hT = [ar.tile([128, 8, 128], BF16, f"hT{i}") for i in range(NB)]
        sq_q = [ar.tile([128, 1], F32, f"sqq{i}") for i in range(NB)]
        cqn = [ar.tile([128, 384], BF16, f"cqn{i}") for i in range(NB)]
        cqT = [ar.tile([128, 3, 128], BF16, f"cqT{i}") for i in range(NB)]
        sqt = [ar.tile([128, 1024], F32, f"sqt{i}") for i in range(NB)]
        s8q = [ar.tile([128, 8], F32, f"s8q{i}") for i in range(NB)]
        qn = [ar.tile([128, 8, 96], F32, f"qn{i}") for i in range(NB)]
        qb = [ar.tile([128, 8, 96], BF16, f"qb{i}") for i in range(NB)]
        rt = [ar.tile([128, 4, 8, 16], F32, f"rt{i}") for i in range(NB)]
        qTs = [ar.tile([96, 8, 128], BF16, f"qTs{i}") for i in range(NB)]
        ckr = [ar.tile([128, 32], F32, f"ckr{i}") for i in range(NB)]
        sq_kv = [ar.tile([128, 1], F32, f"sqkv{i}") for i in range(NB)]
        ckvn = [ar.tile([128, 256], BF16, f"ckvn{i}") for i in range(NB)]
        ckvT = [ar.tile([128, 2, 128], BF16, f"ckvT{i}") for i in range(NB)]
        s8k = [ar.tile([128, 8], F32, f"s8k{i}") for i in range(NB)]
        s1k = [ar.tile([128, 1], F32, f"s1k{i}") for i in range(NB)]
        kn = [ar.tile([128, 8, 96], F32, f"kn{i}") for i in range(NB)]
        kb_ = [ar.tile([128, 8, 96], BF16, f"kb{i}") for i in range(NB)]
        kTs = [ar.tile([96, 8, 128], BF16, f"kTs{i}") for i in range(NB)]
        vb = [ar.tile([128, 8, 64], BF16, f"vb{i}") for i in range(NB)]
        s8d = [ar.tile([128, 8], F32, f"s8d{i}") for i in range(NB)]
        dn = [ar.tile([128, 8, 64], F32, f"dn{i}") for i in range(NB)]
        db = [[ar.tile([128, 8, 64], BF16, f"db{j}{i}") for i in range(NB)] for j in range(2)]
        dTs = [[ar.tile([128, 4, 128], BF16, f"dTs{j}{i}") for i in range(NB)] for j in range(2)]
        dvb = [ar.tile([128, 512], BF16, f"dvb{i}") for i in range(NB)]
        s4m = [ar.tile([128, 4], F32, f"s4m{i}") for i in range(NB)]
        mn = [ar.tile([128, 4, 128], F32, f"mn{i}") for i in range(NB)]
        mb = [ar.tile([128, 4, 128], BF16, f"mb{i}") for i in range(NB)]
        mTs = [ar.tile([128, 4, 128], BF16, f"mTs{i}") for i in range(NB)]

        def bc(ap2, shape, axis):
            return ap2.unsqueeze(axis).to_broadcast(shape)

        rtab = {}
        for nm_, g_ in (("q", gq), ("k", gk)):
            tb = ar.tile([128, 4, NT, 16], F32, f"rtab{nm_}")
            for i_, (src_, lo) in enumerate(((self.cos_t, 64), (self.sin_t, 80), (self.cos_t, 80), (self.sin_t, 64))):
                op("pool", (lambda tb, i_, src_, lo, g_: lambda e: e.tensor_tensor(
                    out=tb.ap[:, i_], in0=src_.ap, in1=g_.ap[:, lo:lo + 16].unsqueeze(1).to_broadcast([128, NT, 16]),
                    op=ALU.mult))(tb, i_, src_, lo, g_), [src_, g_], [tb])
            rtab[nm_] = tb

        def rope(src, dst, s, t, which, op=op):
            tb = rtab[which]
            c1, s2, c2, s1 = (bc(tb.ap[:, i_, t, :], [128, 8, 16], 1) for i_ in range(4))
            x1 = src.ap[:, :, 64:80]
            x2 = src.ap[:, :, 80:96]
            r = rt[s]
            op("dve", lambda e: e.tensor_tensor(out=r.ap[:, 0], in0=x1, in1=c1, op=ALU.mult), [src, tb], [r])
            op("dve", lambda e: e.tensor_tensor(out=r.ap[:, 1], in0=x2, in1=s2, op=ALU.mult), [src, tb], [r])
            op("dve", lambda e: e.tensor_tensor(out=r.ap[:, 2], in0=x2, in1=c2, op=ALU.mult), [src, tb], [r])
            op("dve", lambda e: e.tensor_tensor(out=r.ap[:, 3], in0=x1, in1=s1, op=ALU.mult), [src, tb], [r])
            op("dve", lambda e: e.tensor_tensor(out=dst.ap[:, :, 64:80], in0=r.ap[:, 0], in1=r.ap[:, 1], op=ALU.subtract), [r], [dst])
            op("dve", lambda e: e.tensor_tensor(out=dst.ap[:, :, 80:96], in0=r.ap[:, 2], in1=r.ap[:, 3], op=ALU.add), [r], [dst])

        def load_x(t, s):
            self.load(xt[s][(t // NB) % 2], xin[t * 128:(t + 1) * 128, :])

        def rstd_ap(tl, ap, dim):
            op("act", lambda e: e.activation(out=ap, in_=ap, func=AF.Sqrt, bias=self.eps_t.ap, scale=1.0 / dim), [tl, self.eps_t], [tl])
            op("dve", lambda e: e.reciprocal(out=ap, in_=ap), [tl], [tl])

        def rstd_q(qq, tl, ap, dim):
            qq("act", lambda e: e.activation(out=ap, in_=ap, func=AF.Sqrt, bias=self.eps_t.ap, scale=1.0 / dim), [tl, self.eps_t], [tl])
            qq("dve", lambda e: e.reciprocal(out=ap, in_=ap), [tl], [tl])

        def merge(*gs):
            gs = list(gs)
            while gs:
                for g_ in list(gs):
                    try:
                        next(g_)
                        yield
                    except StopIteration:
                        gs.remove(g_)

        class Q:
            def __init__(q):
                q.items = []

            def __call__(q, eng, fn, reads=(), writes=(), inc=True):
                q.items.append((eng, fn, reads, writes, inc))

            def call(q, fn):
                q.items.append((None, fn, None, None, None))

            def flush(q):
                prev = None
                items, q.items = q.items, []
                for eng, fn, r, w, inc in items:
                    if eng is None:
                        fn()
                        continue
                    if prev is not None and eng != prev:
                        yield
                    op(eng, fn, r, w, inc)
                    prev = eng
                yield

        def tile_gen(t, s):
            BT, Z0, Z1, Z2 = 4 * s, 4 * s + 1, 4 * s + 2, 4 * s + 3
            tok = slice(t * 128, (t + 1) * 128)
            x_ = xt[s][(t // NB) % 2]
            if t + NB < NT:
                load_x(t + NB, s)
            self.bg_pump(1)
            q0 = Q()
            q0("act", lambda e: e.activation(out=junk[s].ap, in_=x_.ap, func=AF.Square, accum_out=ssq[s].ap), [x_], [junk[s], ssq[s]])
            q0("act", lambda e: e.activation(out=ssq[s].ap, in_=ssq[s].ap, func=AF.Sqrt, bias=self.eps_t.ap, scale=1.0 / D),
               [ssq[s], self.eps_t], [ssq[s]])
            q0("dve", lambda e: e.reciprocal(out=ssq[s].ap, in_=ssq[s].ap), [ssq[s]], [ssq[s]])
            q0("dve", lambda e: e.scalar_tensor_tensor(out=hb[s].ap, in0=x_.ap, scalar=ssq[s].ap, in1=gmix.ap,
                                                       op0=ALU.mult, op1=ALU.mult), [x_, ssq[s], gmix], [hb[s]])
            yield from q0.flush()
            pb = self.transposes(hb[s], [hb[s].ap[:, c * 128:(c + 1) * 128] for c in range(8)], BT, 128, 128)
            op("act", lambda e: e.copy(out=hT[s].ap.rearrange("p c t -> p (c t)"), in_=pb), [ps[BT]], [hT[s]])
            self.store(sc["hT"][t], hT[s])
            yield

            def zmm_now(bank, c0, c1):
                for c in range(8):
                    op("pe", lambda e: e.matmul(ps[bank].ap[:, 0:c1 - c0], lhsT=hT[s].ap[:, c, :], rhs=w1.ap[:, c, c0:c1],
                                                start=(c == 0), stop=(c == 7)), [hT[s], w1], [ps[bank]], inc=(c == 7))

            def chain_q():
                qq = Q()
                qq.call((lambda *a: (lambda: zmm_now(*a)))(Z0, 0, 384))
                yield from qq.flush()
                qq("act", lambda e: e.activation(out=junk[s].ap[:, 0:384], in_=ps[Z0].ap[:, 0:384], func=AF.Square,
                                                 accum_out=sq_q[s].ap), [ps[Z0]], [junk[s], sq_q[s]])
                rstd_q(qq, sq_q[s], sq_q[s].ap, 384)
                qq("dve", lambda e: e.scalar_tensor_tensor(out=cqn[s].ap, in0=ps[Z0].ap[:, 0:384], scalar=sq_q[s].ap,
                                                           in1=gcq.ap, op0=ALU.mult, op1=ALU.mult), [ps[Z0], sq_q[s], gcq], [cqn[s]])
                yield from qq.flush()
                pb = self.psb(BT)
                qq.call((lambda *a: (lambda: self.transposes(*a)))(cqn[s], [cqn[s].ap[:, c * 128:(c + 1) * 128] for c in range(3)], BT, 128, 128))
                qq("act", lambda e: e.copy(out=cqT[s].ap.rearrange("p c t -> p (c t)"), in_=pb[:, 0:384]), [ps[BT]], [cqT[s]])
                yield from qq.flush()
                for hb_ in range(2):
                    hs = slice(hb_ * 4, (hb_ + 1) * 4)
                    def qup_now(hb_):
                        for c in range(3):
                            op("pe", lambda e: e.matmul(ps[Z0].ap[:, 0:384], lhsT=cqT[s].ap[:, c, :], rhs=wuq.ap[:, c, hb_ * 384:(hb_ + 1) * 384],
                                                        start=(c == 0), stop=(c == 2)), [cqT[s], wuq], [ps[Z0]], inc=(c == 2))
                    qq.call((lambda a: (lambda: qup_now(a)))(hb_))
                    yield from qq.flush()
                    qv = ps[Z0].ap[:, 0:384].rearrange("p (h d) -> p h d", h=4)
                    qq("act", lambda e: e.activation(out=qn[s].ap[:, hs, :], in_=qv, func=AF.Square), [ps[Z0]], [qn[s]])
                    qq("dve", lambda e: e.tensor_reduce(out=s8q[s].ap[:, hs], in_=qn[s].ap[:, hs, :], axis=AX.X, op=ALU.add), [qn[s]], [s8q[s]])
                    rstd_q(qq, s8q[s], s8q[s].ap[:, hs], 96)
                    yield from qq.flush()
                    qq("dve", lambda e: e.tensor_tensor(out=qn[s].ap[:, hs, :], in0=qv, in1=bc(s8q[s].ap[:, hs], [128, 4, 96], 2), op=ALU.mult),
                       [ps[Z0], s8q[s]], [qn[s]])
                    yield from qq.flush()
                qq("dve", lambda e: e.tensor_tensor(out=qb[s].ap[:, :, 0:64], in0=qn[s].ap[:, :, 0:64],
                                                    in1=bc(gq.ap[:, 0:64], [128, 8, 64], 1), op=ALU.mult), [qn[s], gq], [qb[s]])
                rope(qn[s], qb[s], s, t, "q", qq)
                yield from qq.flush()
                pb = self.psb(BT)
                qq.call((lambda *a: (lambda: self.transposes(*a)))(qb[s], [qb[s].ap[:, h, :] for h in range(8)], BT, 96, 128))
                qq("act", lambda e: e.copy(out=qTs[s].ap.rearrange("p h t -> p (h t)"), in_=pb[0:96, :]), [ps[BT]], [qTs[s]])
                qq.call((lambda *a: (lambda: self.store(*a)))(sc["qT"][:, :, tok].rearrange("h d t -> d h t"), qTs[s]))
                yield from qq.flush()

            def chain_k():
                qq = Q()
                qq.call((lambda *a: (lambda: zmm_now(*a)))(Z1, 384, 672))
                yield from qq.flush()
                qq("act", lambda e: e.activation(out=junk[s].ap[:, 384:640], in_=ps[Z1].ap[:, 0:256], func=AF.Square,
                                                 accum_out=sq_kv[s].ap), [ps[Z1]], [junk[s], sq_kv[s]])
                rstd_q(qq, sq_kv[s], sq_kv[s].ap, 256)
                qq("dve", lambda e: e.scalar_tensor_tensor(out=ckvn[s].ap, in0=ps[Z1].ap[:, 0:256], scalar=sq_kv[s].ap,
                                                           in1=gckv.ap, op0=ALU.mult, op1=ALU.mult), [ps[Z1], sq_kv[s], gckv], [ckvn[s]])
                qq("dve", lambda e: e.tensor_copy(out=ckr[s].ap, in_=ps[Z1].ap[:, 256:288]), [ps[Z1]], [ckr[s]])
                yield from qq.flush()
                pb = self.psb(BT)
                qq.call((lambda *a: (lambda: self.transposes(*a)))(ckvn[s], [ckvn[s].ap[:, c * 128:(c + 1) * 128] for c in range(2)], BT, 128, 128))
                qq("act", lambda e: e.copy(out=ckvT[s].ap.rearrange("p c t -> p (c t)"), in_=pb[:, 0:256]), [ps[BT]], [ckvT[s]])
                qq("act", lambda e: e.activation(out=junk[s].ap[:, 640:672], in_=ckr[s].ap, func=AF.Square, accum_out=s1k[s].ap),
                   [ckr[s]], [junk[s], s1k[s]])
                yield from qq.flush()
                for hb_ in range(2):
                    hs = slice(hb_ * 4, (hb_ + 1) * 4)
                    def kvup_now(hb_):
                        for c in range(2):
                            op("pe", lambda e: e.matmul(ps[Z1].ap, lhsT=ckvT[s].ap[:, c, :], rhs=wukv.ap[:, c, hb_ * 512:(hb_ + 1) * 512],
                                                        start=(c == 0), stop=(c == 1)), [ckvT[s], wukv], [ps[Z1]], inc=(c == 1))
                    qq.call((lambda a: (lambda: kvup_now(a)))(hb_))
                    yield from qq.flush()
                    kv = ps[Z1].ap.rearrange("p (h d) -> p h d", h=4)
                    qq("act", lambda e: e.copy(out=vb[s].ap[:, hs, :], in_=kv[:, :, 64:128]), [ps[Z1]], [vb[s]])
                    qq("act", lambda e: e.activation(out=kn[s].ap[:, hs, 0:64], in_=kv[:, :, 0:64], func=AF.Square), [ps[Z1]], [kn[s]])
                    qq("dve", lambda e: e.tensor_reduce(out=s8k[s].ap[:, hs], in_=kn[s].ap[:, hs, 0:64], axis=AX.X, op=ALU.add), [kn[s]], [s8k[s]])
                    qq("dve", lambda e: e.tensor_scalar(out=s8k[s].ap[:, hs], in0=s8k[s].ap[:, hs], scalar1=s1k[s].ap, scalar2=None, op0=ALU.add),
                       [s8k[s], s1k[s]], [s8k[s]])
                    rstd_q(qq, s8k[s], s8k[s].ap[:, hs], 96)
                    yield from qq.flush()
                    qq("dve", lambda e: e.tensor_tensor(out=kn[s].ap[:, hs, 0:64], in0=kv[:, :, 0:64],
                                                        in1=bc(s8k[s].ap[:, hs], [128, 4, 64], 2), op=ALU.mult), [ps[Z1], s8k[s]], [kn[s]])
                    yield from qq.flush()
                qq.call((lambda *a: (lambda: self.store(*a)))(sc["v"][tok, :], vb[s], vb[s].ap.rearrange("p h d -> p (h d)")))
                qq("dve", lambda e: e.tensor_tensor(out=kn[s].ap[:, :, 64:96], in0=bc(ckr[s].ap, [128, 8, 32], 1),
                                                    in1=bc(s8k[s].ap, [128, 8, 32], 2), op=ALU.mult), [ckr[s], s8k[s]], [kn[s]])
                qq("dve", lambda e: e.tensor_tensor(out=kb_[s].ap[:, :, 0:64], in0=kn[s].ap[:, :, 0:64],
                                                    in1=bc(gk.ap[:, 0:64], [128, 8, 64], 1), op=ALU.mult), [kn[s], gk], [kb_[s]])
                yield from qq.flush()
                rope(kn[s], kb_[s], s, t, "k", qq)
                yield from qq.flush()
                pb = self.psb(BT)
                qq.call((lambda *a: (lambda: self.transposes(*a)))(kb_[s], [kb_[s].ap[:, h, :] for h in range(8)], BT, 96, 128))
                qq("act", lambda e: e.copy(out=kTs[s].ap.rearrange("p h t -> p (h t)"), in_=pb[0:96, :]), [ps[BT]], [kTs[s]])
                qq.call((lambda *a: (lambda: self.store(*a)))(sc["kT"][:, :, tok].rearrange("h d t -> d h t"), kTs[s]))
                yield from qq.flush()

            def chain_d():
                qq = Q()
                def diff_part(j, g_):
                    qq("act", lambda e: e.activation(out=sqt[s].ap[:, 0:512], in_=ps[Z2].ap, func=AF.Square), [ps[Z2]], [sqt[s]])
                    qq("dve", lambda e: e.tensor_reduce(out=s8d[s].ap, in_=sqt[s].ap[:, 0:512].rearrange("p (h d) -> p h d", h=8),
                                                        axis=AX.X, op=ALU.add), [sqt[s]], [s8d[s]])
                    rstd_q(qq, s8d[s], s8d[s].ap, 64)
                    qq("dve", lambda e: e.tensor_tensor(out=dn[s].ap, in0=ps[Z2].ap.rearrange("p (h d) -> p h d", h=8),
                                                        in1=bc(s8d[s].ap, [128, 8, 64], 2), op=ALU.mult), [ps[Z2], s8d[s]], [dn[s]])
                    qq("pool", lambda e: e.tensor_tensor(out=db[j][s].ap, in0=dn[s].ap, in1=bc(g_.ap, [128, 8, 64], 1), op=ALU.mult),
                       [dn[s], g_], [db[j][s]])

                def diff_tr(j, dst):
                    dflat = db[j][s].ap.rearrange("p h d -> p (h d)")
                    pb = self.psb(BT)
                    qq.call((lambda *a: (lambda: self.transposes(*a)))(db[j][s], [dflat[:, c * 128:(c + 1) * 128] for c in range(4)], BT, 128, 128))
                    qq("act", lambda e: e.copy(out=dTs[j][s].ap.rearrange("p c t -> p (c t)"), in_=pb[:, 0:512]), [ps[BT]], [dTs[j][s]])
                    qq.call((lambda *a: (lambda: self.store(*a)))(dst[:, tok].rearrange("(c p) t -> p c t", p=128), dTs[j][s]))

                qq.call((lambda *a: (lambda: zmm_now(*a)))(Z2, 672, 1184))
                yield from qq.flush()
                diff_part(0, gdq)
                yield from qq.flush()
                qq.call((lambda *a: (lambda: zmm_now(*a)))(Z2, 1184, 1696))
                yield from qq.flush()
                diff_tr(0, sc["dqT"])
                yield from qq.flush()
                diff_part(1, gdk)
                yield from qq.flush()
                qq.call((lambda *a: (lambda: zmm_now(*a)))(Z2, 1696, 2208))
                yield from qq.flush()
                diff_tr(1, sc["dkT"])
                qq("act", lambda e: e.copy(out=dvb[s].ap, in_=ps[Z2].ap), [ps[Z2]], [dvb[s]])
                qq.call((lambda *a: (lambda: self.store(*a)))(sc["dv"][tok, :], dvb[s]))
                yield from qq.flush()
                qq.call((lambda *a: (lambda: zmm_now(*a)))(Z2, 2208, 2720))
                yield from qq.flush()
                qq("act", lambda e: e.activation(out=sqt[s].ap[:, 512:1024], in_=ps[Z2].ap, func=AF.Square), [ps[Z2]], [sqt[s]])
                qq("dve", lambda e: e.tensor_reduce(out=s4m[s].ap, in_=sqt[s].ap[:, 512:1024].rearrange("p (h d) -> p h d", h=4),
                                                    axis=AX.X, op=ALU.add), [sqt[s]], [s4m[s]])
                rstd_q(qq, s4m[s], s4m[s].ap, 128)
                yield from qq.flush()
                qq("dve", lambda e: e.tensor_tensor(out=mn[s].ap, in0=ps[Z2].ap.rearrange("p (h d) -> p h d", h=4),
                                                    in1=bc(s4m[s].ap, [128, 4, 128], 2), op=ALU.mult), [ps[Z2], s4m[s]], [mn[s]])
                qq("pool", lambda e: e.tensor_tensor(out=mb[s].ap, in0=mn[s].ap, in1=bc(gmq.ap, [128, 4, 128], 1), op=ALU.mult),
                   [mn[s], gmq], [mb[s]])
                yield from qq.flush()
                pb = self.psb(BT)
                qq.call((lambda *a: (lambda: self.transposes(*a)))(mb[s], [mb[s].ap[:, h, :] for h in range(4)], BT, 128, 128))
                qq("act", lambda e: e.copy(out=mTs[s].ap.rearrange("p c t -> p (c t)"), in_=pb[:, 0:512]), [ps[BT]], [mTs[s]])
                qq.call((lambda *a: (lambda: self.store(*a)))(sc["mqT"][:, tok].rearrange("(c p) t -> p c t", p=128), mTs[s]))
                yield from qq.flush()

            for _ in merge(chain_q(), chain_k(), chain_d()):
                yield

        for s in range(NB):
            load_x(s, s)
        gens = [self._chain([(lambda t, s: (lambda: tile_gen(t, s)))(t, s) for t in range(s, NT, NB)]) for s in range(NB)]
        self._interleave(gens, lead=(18, 0))

    @staticmethod
    def _chain(makers):
        for mk in makers:
            for _ in mk():
                yield

    @staticmethod
    def _interleave(gens, lead=()):
        gens = list(gens)
        for g, n in zip(list(gens), lead):
            for _ in range(n):
                try:
                    next(g)
                except StopIteration:
                    gens.remove(g)
                    break
        while gens:
            for g in list(gens):
                try:
                    next(g)
                except StopIteration:
                    gens.remove(g)

    def phase_attn(self, l):
        ar = self.ar
        ar.reset()
        op, W, sc, ps = self.op, self.W, self.sc, self.ps
        wg = ar.tile([128, 8, 3072], BF16, "wg", top=True)
        wb = ar.tile([128, 12, D], BF16, "wb", top=True)
        wo = ar.tile([128, 8, D], BF16, "wo", top=True)
        self.p3w = (wg, wb, wo)
        bg = []
        gB = self.bg_require((l, "B"))
        for dst, src3, nch in ((wg, self.wbf[l]["w_in"][:, C1:D_IN], 8), (wb, self.wbf[l]["w_branch"], 12),
                               (wo, self.wbf[l]["w_out"], 8)):
            for c in range(nch):
                bg.append((lambda dst, src3, c: lambda: self.S.dma("sp", dst.ap[:, c, :], src3[c * 128:(c + 1) * 128, :],
                                                                   [gB], [dst.o], dst.o))(dst, src3, c))
        lam_init = 0.8 - 0.6 * math.exp(-0.3 * l)
        lv = [ar.tile([128, 64], F32, f"lv{i}") for i in range(4)]
        for t_, nm in zip(lv, ("lam_q1", "lam_k1", "lam_q2", "lam_k2")):
            self.bcast_load(t_, W[nm][l])
        lj = ar.tile([128, 64], F32, "lj")
        ls = [ar.tile([128, 1], F32, f"ls{i}") for i in range(2)]
        for i in range(2):
            op("dve", (lambda i: lambda e: e.tensor_tensor(out=lj.ap, in0=lv[2 * i].ap, in1=lv[2 * i + 1].ap, op=ALU.mult))(i),
               [lv[2 * i], lv[2 * i + 1]], [lj])
            op("dve", (lambda i: lambda e: e.tensor_reduce(out=ls[i].ap, in_=lj.ap, axis=AX.X, op=ALU.add))(i), [lj], [ls[i]])
            op("act", (lambda i: lambda e: e.activation(out=ls[i].ap, in_=ls[i].ap, func=AF.Exp))(i), [ls[i]], [ls[i]])
        op("dve", lambda e: e.tensor_tensor(out=self.nlam.ap, in0=ls[1].ap, in1=ls[0].ap, op=ALU.subtract), [ls[0], ls[1]], [self.nlam])
        op("dve", lambda e: e.tensor_scalar(out=self.nlam.ap, in0=self.nlam.ap, scalar1=float(-lam_init), scalar2=None, op0=ALU.add),
           [self.nlam], [self.nlam])
        self.load(self.gdo, W["g_diff_out"][l].rearrange("(p o) -> p o", o=1))
        op("dve", lambda e: e.tensor_scalar(out=self.gdo.ap, in0=self.gdo.ap, scalar1=float(1.0 - lam_init), scalar2=None, op0=ALU.mult),
           [self.gdo], [self.gdo])

        NP = 8
        pT = [ar.tile([128, 512], BF16, f"pT{i}") for i in range(NP)]
        pF = [ar.tile([128, 512], F32, f"pF{i}") for i in range(4)]
        st = {"p": 0, "f": 0, "s": 0}
        base_off = ar.off

        kT = [ar.tile([96, SEQ], BF16, f"kTa{i}") for i in range(3)]
        va = [ar.tile([128, NT, 65], BF16, f"va{i}") for i in range(3)]
        NQ = 6
        qT = [ar.tile([96, 512], BF16, f"qTa{i}") for i in range(NQ)]
        done_items = set()
        osb = [ar.tile([65, 512], F32, f"osb{i}") for i in range(2)]
        rl = [ar.tile([65, 512], F32, f"rl{i}") for i in range(2)]
        on = [[ar.tile([64, 512], BF16, f"on{i}{j}") for j in range(2)] for i in range(2)]
        for i in range(3):
            op("dve", (lambda i: lambda e: e.memset(va[i].ap[:, :, 64:65], 1.0))(i), [], [va[i]])
        scale_a = 96 ** -0.5
        gorder = [7, 0, 6, 1, 5, 2, 4, 3]
        items = [(h, g) for h in range(8) for g in gorder]
        loaded_heads = set()
        nxt = {"n": 0, 0: 0, 1: 0}

        def load_kv_a(h):
            if h in loaded_heads or h >= 8:
                return
            loaded_heads.add(h)
            s3 = h % 3
            self.load(kT[s3], sc["kT"][h, :, :])
            self.load(va[s3], sc["v"][:, h * 64:(h + 1) * 64].rearrange("(b p) d -> p b d", p=128), dst_ap=va[s3].ap[:, :, 0:64])

        def load_q_a(n):
            if n < len(items):
                h, g = items[n]
                assert n < NQ or (n - NQ) in done_items
                self.load(qT[n % NQ], sc["qT"][h, :, g * 512:(g + 1) * 512])

        def mla_item(lane, n):
            h, g = items[n]
            s3 = h % 3
            if bg:
                bg.pop(0)()
            self.bg_pump(1)
            load_kv_a(h)
            load_kv_a(h + 1)
            load_q_a(n + 2)
            q = qT[n % NQ]
            assert all(items[m][0] >= h - 1 for m in range(n) if m not in done_items)
            SB = (4 * lane, 4 * lane + 1)
            ob = 4 * lane + 2
            bcb = 4 * lane + 3
            nkb = 4 * g + 4
            cnt = {"s": 0, "p": 0}

            def qk(kb):
                c0 = max(0, kb - 4 * g) * 128
                bank = SB[cnt["s"] % 2]
                cnt["s"] += 1
                op("pe", lambda e: e.matmul(ps[bank].ap[:, c0:512], lhsT=kT[s3].ap[:, kb * 128:(kb + 1) * 128], rhs=q.ap[:, c0:512],
                                            start=True, stop=True), [kT[s3], q], [ps[bank]])
                p = pT[lane * 4 + cnt["p"] % 4]
                cnt["p"] += 1
                op("act", lambda e: e.activation(out=p.ap[:, c0:512], in_=ps[bank].ap[:, c0:512], func=AF.Exp, scale=float(scale_a)),
                   [ps[bank]], [p])
                if kb >= 4 * g:
                    op("pool", lambda e: e.memset(p.ap[64:128, c0:c0 + 64], 0.0), [], [p])
                return (kb, c0, p)

            def pv(item):
                kb, c0, p = item
                op("pe", lambda e: e.matmul(ps[ob].ap[0:65, c0:512], lhsT=va[s3].ap[:, kb, :], rhs=p.ap[:, c0:512],
                                            start=(kb == 0), stop=(kb == nkb - 1)), [va[s3], p], [ps[ob]])

            pend = []
            for kb in range(nkb):
                pend.append(qk(kb))
                if len(pend) > 1:
                    pv(pend.pop(0))
                if kb == min(nkb - 1, 6) and fin[lane] is not None:
                    fin[lane]()
                    fin[lane] = None
                yield
            while pend:
                pv(pend.pop(0))
            yield
            o_, r_, n_ = osb[lane], rl[lane], on[lane][nxt[lane] % 2]
            nxt[lane] += 1
            op("dve", lambda e: e.tensor_copy(out=o_.ap, in_=ps[ob].ap[0:65, :]), [ps[ob]], [o_])
            op("dve", lambda e: e.reciprocal(out=r_.ap[64:65, :], in_=o_.ap[64:65, :]), [o_], [r_])

            def stage2():
                op("pe", lambda e: e.matmul(ps[bcb].ap[0:64, :], lhsT=self.ones_f.ap[64:65, 0:64], rhs=r_.ap[64:65, :],
                                            start=True, stop=True), [self.ones_f, r_], [ps[bcb]])
                op("dve", lambda e: e.tensor_tensor(out=n_.ap, in0=o_.ap[0:64, :], in1=ps[bcb].ap[0:64, :], op=ALU.mult), [o_, ps[bcb]], [n_])
                self.store(sc["oT"][0, h * 64:(h + 1) * 64, g * 512:(g + 1) * 512], n_)

            fin[lane] = stage2
            done_items.add(n)
            yield

        load_kv_a(0)
        load_q_a(0)
        load_q_a(1)

        fin = [None, None]

        def lane_gen(lane):
            while nxt["n"] < len(items):
                n = nxt["n"]
                nxt["n"] += 1
                for _ in mla_item(lane, n):
                    yield
            if fin[lane] is not None:
                fin[lane]()
                fin[lane] = None

        self._interleave([lane_gen(0), lane_gen(1)])
        while bg:
            bg.pop(0)()
        self.S.barrier()

        ar.off = base_off
        kTd_ = [ar.tile([64, SEQ], BF16, f"kTd{i}") for i in range(4)]
        vd_ = [ar.tile([128, NT, 128], BF16, f"vdd{i}") for i in range(2)]
        qTd_ = [[ar.tile([64, 512], BF16, f"qTd{c}{i}") for i in range(2)] for c in range(2)]
        ebt = [ar.tile([128, NNEAR, 512], F32, f"ebt{i}") for i in range(2)]
        evs = [[ar.tile([128, 512], F32, f"ev{i}{j}") for j in range(4)] for i in range(2)]
        finb = [None]
        obf = [ar.tile([128, 512], BF16, f"obf{i}") for i in range(2)]
        scale_b = 64 ** -0.5
        SBd = ((0, 1), (2, 3))
        OBd = ((4, 5), (6, 7))
        MSB = 0

        def load_h_b(h):
            if h >= 4:
                return
            for c in range(2):
                u = 2 * h + c
                self.load(kTd_[(h % 2) * 2 + c], sc["dkT"][u * 64:(u + 1) * 64, :])
            self.load(vd_[h % 2], sc["dv"][:, h * 128:(h + 1) * 128].rearrange("(b p) d -> p b d", p=128))
            self.load(ebt[h % 2], sc["EB"][h].rearrange("n p q -> p n q"))

        def load_q_b(i):
            if i >= 4 * NG:
                return
            h, g = divmod(i, NG)
            for c in range(2):
                u = 2 * h + c
                self.load(qTd_[c][i % 2], sc["dqT"][u * 64:(u + 1) * 64, g * 512:(g + 1) * 512])

        load_h_b(0)
        load_q_b(0)
        for h in range(4):
            load_h_b(h + 1)
            vt = vd_[h % 2]
            et = ebt[h % 2]
            for g in range(NG):
                i = h * NG + g
                load_q_b(i + 1)
                self.bg_pump(1)
                nkb = 4 * g + 4
                cnt = {"s": 0}

                def qk(kb, c):
                    q = qTd_[c][i % 2]
                    kt = kTd_[(h % 2) * 2 + c]
                    dl = (kb - 4 * g) * 128
                    c0 = max(0, dl)
                    bank = SBd[c][kb % 2]
                    op("pe", lambda e: e.matmul(ps[bank].ap[:, c0:512], lhsT=kt.ap[:, kb * 128:(kb + 1) * 128], rhs=q.ap[:, c0:512],
                                                start=True, stop=True), [kt, q], [ps[bank]])
                    p = pT[c * 4 + kb % 4]
                    if dl in NEAR_IDX:
                        ni = NEAR_IDX[dl]
                        f = pF[c * 2 + kb % 2]
                        op("act", lambda e: e.activation(out=f.ap[:, c0:512], in_=ps[bank].ap[:, c0:512], func=AF.Exp,
                                                         scale=float(scale_b)), [ps[bank]], [f])
                        op("dve", lambda e: e.tensor_tensor(out=p.ap[:, c0:512], in0=f.ap[:, c0:512], in1=et.ap[:, ni, c0:512],
                                                            op=ALU.mult), [f, et], [p])
                    else:
                        op("act", lambda e: e.activation(out=p.ap[:, c0:512], in_=ps[bank].ap[:, c0:512], func=AF.Exp,
                                                         bias=self.b15.ap[:, h:h + 1], scale=float(scale_b)), [ps[bank], self.b15], [p])
                    return (kb, c0, p)

                def pv(item, c):
                    kb, c0, p = item
                    obk, lbk = OBd[c]
                    op("pe", lambda e: e.matmul(ps[obk].ap[:, c0:512], lhsT=vt.ap[:, kb, :], rhs=p.ap[:, c0:512],
                                                start=(kb == 0), stop=(kb == nkb - 1)), [vt, p], [ps[obk]], inc=False)
                    op("pe", lambda e: e.matmul(ps[lbk].ap[:, c0:512], lhsT=self.ones_bf.ap, rhs=p.ap[:, c0:512],
                                                start=(kb == 0), stop=(kb == nkb - 1)), [self.ones_bf, p], [ps[lbk]])

                pend = []
                for kb in range(nkb):
                    cur = [qk(kb, 0), qk(kb, 1)]
                    if pend:
                        pv(pend[0], 0)
                        pv(pend[1], 1)
                    pend = cur
                    if kb == min(nkb - 1, 11) and finb[0] is not None:
                        finb[0]()
                        finb[0] = None
                pv(pend[0], 0)
                pv(pend[1], 1)
                ob_ = obf[i % 2]
                ev = evs[i % 2]
                op("dve", lambda e: e.tensor_copy(out=ev[0].ap, in_=ps[OBd[0][1]].ap), [ps[OBd[0][1]]], [ev[0]])
                op("dve", lambda e: e.tensor_copy(out=ev[1].ap, in_=ps[OBd[0][0]].ap), [ps[OBd[0][0]]], [ev[1]])
                op("dve", lambda e: e.tensor_copy(out=ev[2].ap, in_=ps[OBd[1][1]].ap), [ps[OBd[1][1]]], [ev[2]])
                op("dve", lambda e: e.tensor_copy(out=ev[3].ap, in_=ps[OBd[1][0]].ap), [ps[OBd[1][0]]], [ev[3]])

                def tail(ev=ev, ob_=ob_, h=h, g=g):
                    op("dve", lambda e: e.reciprocal(out=ev[0].ap, in_=ev[0].ap), [ev[0]], [ev[0]])
                    op("pool", lambda e: e.tensor_tensor(out=ev[1].ap, in0=ev[1].ap, in1=ev[0].ap, op=ALU.mult), [ev[0], ev[1]], [ev[1]])
                    op("dve", lambda e: e.reciprocal(out=ev[2].ap, in_=ev[2].ap), [ev[2]], [ev[2]])
                    op("pool", lambda e: e.tensor_tensor(out=ev[3].ap, in0=ev[3].ap, in1=ev[2].ap, op=ALU.mult), [ev[2], ev[3]], [ev[3]])
                    op("dve", lambda e: e.scalar_tensor_tensor(out=ev[1].ap, in0=ev[3].ap, scalar=self.nlam.ap, in1=ev[1].ap,
                                                               op0=ALU.mult, op1=ALU.add), [ev[3], self.nlam, ev[1]], [ev[1]])
                    op("pool", lambda e: e.tensor_tensor(out=ev[0].ap, in0=ev[1].ap, in1=ev[1].ap, op=ALU.mult), [ev[1]], [ev[0]])

                    def tail2():
                        op("pe", lambda e: e.matmul(ps[MSB].ap, lhsT=self.ones_f.ap, rhs=ev[0].ap, start=True, stop=True),
                           [self.ones_f, ev[0]], [ps[MSB]])
                        op("act", lambda e: e.activation(out=ev[2].ap, in_=ps[MSB].ap, func=AF.Ln, bias=self.eps_t.ap, scale=1.0 / 128),
                           [ps[MSB], self.eps_t], [ev[2]])
                        op("act", lambda e: e.activation(out=ev[2].ap, in_=ev[2].ap, func=AF.Exp, scale=-0.5), [ev[2]], [ev[2]])
                        op("dve", lambda e: e.scalar_tensor_tensor(out=ob_.ap, in0=ev[1].ap, scalar=self.gdo.ap, in1=ev[2].ap,
                                                                   op0=ALU.mult, op1=ALU.mult), [ev[1], self.gdo, ev[2]], [ob_])
                        self.store(sc["oT"][1, h * 128:(h + 1) * 128, g * 512:(g + 1) * 512], ob_)
                    return tail2

                finb[0] = tail()
        if finb[0] is not None:
            finb[0]()
            finb[0] = None
        self.S.barrier()

        ar.off = base_off
        r1 = ar.tile([128, 512], F32, "r1c")
        kmT = ar.tile([128, 4, MEM], BF16, "kmTc")
        vm = ar.tile([128, 2, 512], BF16, "vmc")
        qTc = [ar.tile([128, 512], BF16, f"qTc{i}") for i in range(3)]
        ocb = [ar.tile([128, 512], BF16, f"ocb{i}") for i in range(2)]
        self.load(kmT, sc["kmT"][:, :, :])
        self.load(vm, sc["vm"][:, :].rearrange("(b p) d -> p b d", p=128))
        scale_c = 128 ** -0.5
        SBc = (0, 1, 7)
        OBc = ((2, 3), (4, 5))

        def load_q_c(i):
            h, g = divmod(i, NG)
            self.load(qTc[i % 3], sc["mqT"][h * 128:(h + 1) * 128, g * 512:(g + 1) * 512])

        load_q_c(0)
        for h in range(4):
            for g in range(NG):
                i = h * NG + g
                if i + 1 < 4 * NG:
                    load_q_c(i + 1)
                q = qTc[i % 3]
                obk, lbk = OBc[i % 2]
                items = []
                for mbk in range(2):
                    bank = SBc[st["s"] % 3]
                    st["s"] += 1
                    op("pe", (lambda mbk, bank, q: lambda e: e.matmul(ps[bank].ap, lhsT=kmT.ap[:, h, mbk * 128:(mbk + 1) * 128], rhs=q.ap,
                                                                      start=True, stop=True))(mbk, bank, q), [kmT, q], [ps[bank]])
                    p = pT[st["p"] % NP]
                    st["p"] += 1
                    op("act", (lambda bank, p: lambda e: e.activation(out=p.ap, in_=ps[bank].ap, func=AF.Exp, scale=float(scale_c)))(bank, p),
                       [ps[bank]], [p])
                    items.append((mbk, p))
                for mbk, p in items:
                    op("pe", (lambda mbk, p: lambda e: e.matmul(ps[obk].ap, lhsT=vm.ap[:, mbk, h * 128:(h + 1) * 128], rhs=p.ap,
                                                                start=(mbk == 0), stop=(mbk == 1)))(mbk, p), [vm, p], [ps[obk]], inc=False)
                    op("pe", (lambda mbk, p: lambda e: e.matmul(ps[lbk].ap, lhsT=self.ones_bf.ap, rhs=p.ap,
                                                                start=(mbk == 0), stop=(mbk == 1)))(mbk, p), [self.ones_bf, p], [ps[lbk]])
                ob_ = ocb[i % 2]
                op("dve", (lambda lbk: lambda e: e.reciprocal(out=r1.ap, in_=ps[lbk].ap))(lbk), [ps[lbk]], [r1])
                op("dve", (lambda obk, ob_: lambda e: e.tensor_tensor(out=ob_.ap, in0=r1.ap, in1=ps[obk].ap, op=ALU.mult))(obk, ob_),
                   [r1, ps[obk]], [ob_])
                self.store(sc["oT"][2, h * 128:(h + 1) * 128, g * 512:(g + 1) * 512], ob_)

    def phase_p3(self, l, xin, out_d):
        ar = self.ar
        ar.reset(keep_top=True)
        op, W, sc, ps = self.op, self.W, self.sc, self.ps
        wg, wb, wo = self.p3w
        hT = [ar.tile([128, 8, 512], BF16, f"hT3{i}") for i in range(2)]
        oT = [ar.tile([128, 12, 512], BF16, f"oT3{i}") for i in range(2)]
        yT = ar.tile([128, 8, 512], BF16, "yT")
        gs = [ar.tile([128, 512], F32, f"gs{i}") for i in range(6)]
        tm = [ar.tile([128, 512], F32, f"tm{i}") for i in range(4)]
        xo = [ar.tile([128, D], F32, f"xo{i}") for i in range(2)]

        def loads(g):
            s = g % 2
            for tt in range(4):
                self.load(hT[s], sc["hT"][g * 4 + tt], dst_ap=hT[s].ap[:, :, tt * 128:(tt + 1) * 128])
            for n in range(3):
                self.load(oT[s], sc["oT"][n, :, g * 512:(g + 1) * 512].rearrange("(c p) t -> p c t", p=128),
                          dst_ap=oT[s].ap[:, n * 4:(n + 1) * 4, :])

        loads(0)
        k = 0
        for g in range(NG):
            s = g % 2
            self.bg_pump(2)
            if g + 1 < NG:
                loads(g + 1)
            for fc in range(8):
                gset = gs[(fc % 2) * 3:(fc % 2) * 3 + 3]
                for n in range(3):
                    for kc in range(8):
                        op("pe", (lambda n, kc: lambda e: e.matmul(ps[n].ap, lhsT=wg.ap[:, kc, n * D + fc * 128:n * D + (fc + 1) * 128],
                                                                   rhs=hT[s].ap[:, kc, :], start=(kc == 0), stop=(kc == 7)))(n, kc),
                           [wg, hT[s]], [ps[n]], inc=(kc == 7))
                for n in range(3):
                    for c in range(4):
                        op("pe", (lambda n, c: lambda e: e.matmul(ps[3 + n].ap, lhsT=wb.ap[:, n * 4 + c, fc * 128:(fc + 1) * 128],
                                                                  rhs=oT[s].ap[:, n * 4 + c, :], start=(c == 0), stop=(c == 3)))(n, c),
                           [wb, oT[s]], [ps[3 + n]], inc=(c == 3))
                for n in range(3):
                    op("act", (lambda n, gset: lambda e: e.activation(out=gset[n].ap, in_=ps[n].ap, func=AF.Sigmoid))(n, gset),
                       [ps[n]], [gset[n]])
                t0, t1 = tm[(k % 2) * 2], tm[(k % 2) * 2 + 1]
                k += 1
                op("dve", (lambda gset, t0: lambda e: e.tensor_tensor(out=t0.ap, in0=gset[0].ap, in1=ps[3].ap, op=ALU.mult))(gset, t0),
                   [gset[0], ps[3]], [t0])
                op("dve", (lambda gset, t1: lambda e: e.tensor_tensor(out=t1.ap, in0=gset[1].ap, in1=ps[4].ap, op=ALU.mult))(gset, t1),
                   [gset[1], ps[4]], [t1])
                op("dve", (lambda t0, t1: lambda e: e.tensor_tensor(out=t0.ap, in0=t0.ap, in1=t1.ap, op=ALU.add))(t0, t1), [t0, t1], [t0])
                op("dve", (lambda gset, t1: lambda e: e.tensor_tensor(out=t1.ap, in0=gset[2].ap, in1=ps[5].ap, op=ALU.mult))(gset, t1),
                   [gset[2], ps[5]], [t1])
                op("dve", (lambda t0, t1, fc: lambda e: e.tensor_tensor(out=yT.ap[:, fc, :], in0=t0.ap, in1=t1.ap, op=ALU.add))(t0, t1, fc),
                   [t0, t1], [yT])
            for tt in range(4):
                x_o = xo[tt % 2]
                r0 = g * 512 + tt * 128
                self.load(x_o, xin[r0:r0 + 128, :])
                for cb in range(2):
                    bank = 6 + cb
                    for fc in range(8):
                        op("pe", (lambda fc, cb, bank: lambda e: e.matmul(ps[bank].ap, lhsT=yT.ap[:, fc, tt * 128:(tt + 1) * 128],
                                                                          rhs=wo.ap[:, fc, cb * 512:(cb + 1) * 512],
                                                                          start=(fc == 0), stop=(fc == 7)))(fc, cb, bank),
                           [yT, wo], [ps[bank]], inc=(fc == 7))
                    op("dve", (lambda cb, bank, x_o: lambda e: e.tensor_tensor(out=x_o.ap[:, cb * 512:(cb + 1) * 512],
                                                                               in0=x_o.ap[:, cb * 512:(cb + 1) * 512],
                                                                               in1=ps[bank].ap, op=ALU.add))(cb, bank, x_o),
                       [x_o, ps[bank]], [x_o])
                self.store(out_d[r0:r0 + 128, :], x_o)

    def phase_p4(self, l, out_d):
        ar = self.ar
        ar.reset()
        op, W, ps = self.op, self.W, self.ps
        w1 = ar.tile([128, 8, 4 * D], BF16, "wf1")
        w2 = ar.tile([128, 32, D], BF16, "wf2")
        self.wload(w1, self.wbf[l]["w_ff1"], 8, (l, "C"))
        self.wload(w2, self.wbf[l]["w_ff2"], 32, (l, "D"))
        gm = ar.tile([128, D], F32, "gmlp")
        self.bcast_load(gm, W["g_mlp"][l])
        xt = [ar.tile([128, D], F32, f"x4{i}") for i in range(2)]
        hb = [ar.tile([128, D], BF16, f"hb4{i}") for i in range(2)]
        ssq = [ar.tile([128, 1], F32, f"ssq4{i}") for i in range(2)]
        hT = ar.tile([128, 8, 512], BF16, "hT4")
        uT = ar.tile([128, 32, 512], BF16, "uT")
        rr = [ar.tile([128, 512], F32, f"rr{i}") for i in range(2)]
        xo = [ar.tile([128, D], F32, f"xo4{i}") for i in range(2)]

        def loadx(i):
            self.load(xt[i % 2], out_d[i * 128:(i + 1) * 128, :])

        loadx(0)
        k = 0
        for g in range(NG):
            s = g % 2
            self.bg_pump(2)
            for tt in range(4):
                i = g * 4 + tt
                if i + 1 < NT:
                    loadx(i + 1)
                xv = xt[i % 2]
                h_ = hb[tt % 2]
                self.norm_tile(xv, gm, h_, h_, ssq[tt % 2])
                bank = tt % 2
                pb = self.transposes(h_, [h_.ap[:, c * 128:(c + 1) * 128] for c in range(8)], bank, 128, 128)
                op("act", (lambda tt, pb: lambda e: e.copy(out=hT.ap[:, :, tt * 128:(tt + 1) * 128],
                                                           in_=pb.rearrange("p (c t) -> p c t", c=8)))(tt, pb), [ps[bank]], [hT])
            for f in range(32):
                bank = 2 + f % 3
                for kc in range(8):
                    op("pe", (lambda f, kc, bank: lambda e: e.matmul(ps[bank].ap, lhsT=w1.ap[:, kc, f * 128:(f + 1) * 128], rhs=hT.ap[:, kc, :],
                                                                     start=(kc == 0), stop=(kc == 7)))(f, kc, bank),
                       [w1, hT], [ps[bank]], inc=(kc == 7))
                r = rr[k % 2]
                k += 1
                op("act", (lambda bank, r: lambda e: e.activation(out=r.ap, in_=ps[bank].ap, func=AF.Relu))(bank, r), [ps[bank]], [r])
                op("dve", (lambda f, r: lambda e: e.tensor_tensor(out=uT.ap[:, f, :], in0=r.ap, in1=r.ap, op=ALU.mult))(f, r), [r], [uT])
            for tt in range(4):
                x_o = xo[tt % 2]
                r0 = g * 512 + tt * 128
                self.load(x_o, out_d[r0:r0 + 128, :])
                for cb in range(2):
                    bank = 5 + (tt * 2 + cb) % 3
                    for f in range(32):
                        op("pe", (lambda f, cb, bank: lambda e: e.matmul(ps[bank].ap, lhsT=uT.ap[:, f, tt * 128:(tt + 1) * 128],
                                                                         rhs=w2.ap[:, f, cb * 512:(cb + 1) * 512],
                                                                         start=(f == 0), stop=(f == 31)))(f, cb, bank),
                           [uT, w2], [ps[bank]], inc=(f == 31))
                    op("dve", (lambda cb, bank, x_o: lambda e: e.tensor_tensor(out=x_o.ap[:, cb * 512:(cb + 1) * 512],
                                                                               in0=x_o.ap[:, cb * 512:(cb + 1) * 512],
                                                                               in1=ps[bank].ap, op=ALU.add))(cb, bank, x_o),
                       [x_o, ps[bank]], [x_o])
                self.store(out_d[r0:r0 + 128, :], x_o)


WNAMES = ("g_mix", "g_mem", "w_in", "g_cq", "w_uq", "g_ckv", "w_ukv", "g_mla_q", "g_mla_k", "g_diff_q", "g_diff_k",
          "lam_q1", "lam_k1", "lam_q2", "lam_k2", "g_diff_out", "w_mem_kv", "g_mem_q", "g_mem_k", "w_branch", "w_out",
          "g_mlp", "w_ff1", "w_ff2")


def make_in_maps(inputs, cores):
    ident, jx, oh, cm = _consts()
    shared = {k: np.ascontiguousarray(np.asarray(inputs[k], dtype=np.float32)) for k in WNAMES}
    shared["t5_table"] = np.ascontiguousarray(np.asarray(inputs["t5_table"], dtype=np.float32))
    shared.update({"c_ident": ident, "c_jx": jx, "c_oh": oh, "c_cm": cm})
    x = np.asarray(inputs["x"], dtype=np.float32)
    mem = np.asarray(inputs["mem"], dtype=np.float32)
    pos = np.asarray(inputs["positions"]).astype(np.int32)
    maps = []
    for b in cores:
        m = dict(shared)
        m["x"] = np.ascontiguousarray(x[b])
        m["mem"] = np.ascontiguousarray(mem[b])
        m["pos"] = np.ascontiguousarray(pos[b].reshape(NT, 128).T)
        maps.append(m)
    return maps


def kernel(**inputs):
    nc = Builder().build()
    maps = make_in_maps(inputs, range(8))
    res = run_bass_kernel_spmd(nc, maps, core_ids=list(range(8)))
    return np.stack([np.asarray(r["out"], dtype=np.float32) for r in res.results], axis=0)
```

```python
import math
import numpy as np
import concourse.bass as bass
import concourse.mybir as mybir
from concourse.bass_utils import run_bass_kernel_spmd

F32 = mybir.dt.float32
BF16 = mybir.dt.bfloat16
I32 = mybir.dt.int32
ALU = mybir.AluOpType
AF = mybir.ActivationFunctionType
AX = mybir.AxisListType

D = 1024
SEQ = 4096
DEPTH = 4
NT = SEQ // 128
NG = SEQ // 512
MEM = 256
D_IN = 5792
C1 = 2720
EPS = 1e-6
ENGS = ("pe", "act", "dve", "pool", "sp")
PI_LO = 3.1415925


class Obj:
    __slots__ = ("name", "w", "r", "dkey", "persist")

    def __init__(self, name="", persist=False):
        self.name = name
        self.w = {}
        self.r = {}
        self.dkey = None
        self.persist = persist


class _Rec:
    def __init__(self):
        self.call = None

    def __getattr__(self, name):
        def f(*a, **k):
            self.call = (name, a, k)
            return self
        return f


class Sched:
    def __init__(self, nc):
        self.nc = nc
        self.prog = {e: [] for e in ENGS}
        self.sems = {}
        self.cur = {}
        self.waited = {e: {} for e in ENGS}
        self.nsem = 0
        self.epoch = -1
        self.live = {}
        self.persist_keys = set()
        self.free_dma = {True: [], False: []}
        self.used_dma = {True: [], False: []}
        self.new_epoch()

    def _alloc(self, key, is_dma):
        h = self.nc.alloc_semaphore(f"s{self.nsem}_{key}")
        self.nsem += 1
        self.sems[key] = [h, 0, is_dma]

    def new_epoch(self):
        self.epoch += 1
        for e in ENGS:
            if e == "sp":
                continue
            key = f"{e}{self.epoch}"
            self._alloc(key, False)
            self.cur[e] = key

    def _waits(self, eng, reads, writes):
        need = {}
        for o in reads:
            for k, v in o.w.items():
                if need.get(k, 0) < v:
                    need[k] = v
        for o in writes:
            for d in (o.w, o.r):
                for k, v in d.items():
                    if need.get(k, 0) < v:
                        need[k] = v
        self._emit_waits(eng, need)

    def _emit_waits(self, eng, need):
        wd = self.waited[eng]
        for k, v in need.items():
            h, total, is_dma = self.sems[k]
            if is_dma:
                v = total
            elif eng == "pe" and k.startswith("pe"):
                continue
            if wd.get(k, 0) >= v:
                continue
            wd[k] = v
            self.prog[eng].append(("wait", h, v))

    def op(self, eng, fn, reads=(), writes=(), inc=True):
        rec = _Rec()
        fn(rec)
        fn = rec.call
        self._waits(eng, reads, writes)
        for o in reads:
            self.live[id(o)] = o
        for o in writes:
            self.live[id(o)] = o
        key = self.cur[eng]
        s = self.sems[key]
        if inc:
            s[1] += 1
            val = s[1]
            self.prog[eng].append(("op", fn, s[0]))
        else:
            val = s[1] + 1
            self.prog[eng].append(("op", fn, None))
        for o in reads:
            o.r[key] = val
        for o in writes:
            o.w = {key: val}
            o.r = {}

    def dma(self, eng, out, in_, reads, writes, slot):
        need = {}
        for o in reads:
            for k, v in o.w.items():
                if need.get(k, 0) < v:
                    need[k] = v
        for o in writes:
            for k, v in o.r.items():
                if need.get(k, 0) < v:
                    need[k] = v
            for k, v in o.w.items():
                if k != slot.dkey and need.get(k, 0) < v:
                    need[k] = v
        self._emit_waits(eng, need)
        for o in list(reads) + list(writes) + [slot]:
            self.live[id(o)] = o
        if slot.dkey is None and slot.persist:
            slot.dkey = f"d{self.nsem}"
            self._alloc(slot.dkey, True)
            self.persist_keys.add(slot.dkey)
        if slot.dkey is None:
            sw = (eng == "pool")
            if self.free_dma[sw]:
                slot.dkey = self.free_dma[sw].pop()
            else:
                slot.dkey = f"d{self.nsem}"
                self._alloc(slot.dkey, True)
            self.used_dma[sw].append(slot.dkey)
        s = self.sems[slot.dkey]
        s[1] += 16
        self.prog[eng].append(("dma", out, in_, s[0]))
        for o in reads:
            o.r[slot.dkey] = s[1]
        for o in writes:
            o.w = {slot.dkey: s[1]}
            o.r = {}

    def barrier(self, final=False):
        need = {}
        for k, (h, total, is_dma) in self.sems.items():
            if k in self.persist_keys and not final:
                continue
            if total > 0 and (is_dma or k in self.cur.values()):
                need[k] = total
        for e in ENGS:
            self._emit_waits(e, dict(need))
        for o in self.live.values():
            if o.persist:
                continue
            o.w = {}
            o.r = {}
            o.dkey = None
        self.live = {}
        for sw in (True, False):
            self.free_dma[sw].extend(self.used_dma[sw])
            self.used_dma[sw] = []

    def replay(self):
        nc = self.nc
        prog = self.prog

        def run(items, e):
            for it in items:
                if it[0] == "wait":
                    e.wait_ge(it[1], it[2])
                elif it[0] == "op":
                    name, a, k = it[1]
                    ins = getattr(e, name)(*a, **k)
                    if it[2] is not None:
                        ins.then_inc(it[2], 1)
                else:
                    e.dma_start(out=it[1], in_=it[2]).then_inc(it[3], 16)

        with nc.Block() as block:
            @block.tensor
            def _(e):
                run(prog["pe"], e)

            @block.scalar
            def _(e):
                run(prog["act"], e)

            @block.vector
            def _(e):
                run(prog["dve"], e)

            @block.gpsimd
            def _(e):
                run(prog["pool"], e)

            @block.sync
            def _(e):
                run(prog["sp"], e)


class Tl:
    __slots__ = ("ap", "o")

    def __init__(self, ap, name=""):
        self.ap = ap
        self.o = Obj(name)


def _dsz(dt):
    return 4 if dt in (F32, I32) else 2


class Arena:
    def __init__(self, nc, name, nbytes):
        self.t = nc.alloc_sbuf_tensor(name, [128, nbytes // 4], F32)
        self.size = nbytes
        self.off = 0
        self.top = nbytes

    def reset(self, keep_top=False):
        self.off = 0
        if not keep_top:
            self.top = self.size

    def tile(self, shape, dt, name="", top=False):
        n = 1
        for s in shape[1:]:
            n *= s
        nb = (n * _dsz(dt) + 31) // 32 * 32
        assert self.off + nb <= self.top, (name, self.off, nb, self.top)
        if top:
            self.top -= nb
            w0 = self.top // 4
            self.off -= nb
        else:
            w0 = self.off // 4
        ap = self.t[0:shape[0], w0:w0 + nb // 4]
        if dt != F32:
            ap = ap.bitcast(dt)
        ap = ap[:, 0:n]
        if len(shape) == 3:
            ap = ap.rearrange("p (a b) -> p a b", a=shape[1], b=shape[2])
        elif len(shape) == 4:
            ap = ap.rearrange("p (a b c) -> p a b c", a=shape[1], b=shape[2], c=shape[3])
        self.off += nb
        return Tl(ap, name)


def _t5_bucket_np(rel):
    n = 16
    ret = np.where(rel > 0, n, 0)
    a = np.abs(rel)
    max_exact = 8
    af = np.maximum(a, 1).astype(np.float32)
    large = max_exact + (np.log(af / np.float32(max_exact)) / np.float32(math.log(128 / max_exact))
                         * np.float32(n - max_exact)).astype(np.int32)
    large = np.minimum(large, n - 1)
    return ret + np.where(a < max_exact, a, large)


GD0 = 511
GLEN = 1280
_rel = GD0 - np.arange(GLEN)
_bk = _t5_bucket_np(_rel)
FAR_BUCKET = int(_t5_bucket_np(np.array([-4095]))[0])
_dfar = -4095
for _d in range(-4095, 64):
    if int(_t5_bucket_np(np.array([_d]))[0]) != FAR_BUCKET:
        break
    _dfar = _d
NEAR = [dl for dl in range(-128 * 8, 0, 128) if dl + 127 > _dfar] + [0, 128, 256, 384]
NEAR_IDX = {dl: i for i, dl in enumerate(NEAR)}
NNEAR = len(NEAR)


def _consts():
    ident = np.eye(128, dtype=np.float32)
    jx = np.ascontiguousarray(ident[::-1])
    oh = np.zeros((32, GLEN), np.float32)
    oh[_bk, np.arange(GLEN)] = 1.0
    k = np.arange(128)[:, None]
    q = np.arange(512)[None, :]
    cm = np.ones((NNEAR, 128, 512), np.float32)
    for dl, i in NEAR_IDX.items():
        if dl >= 0:
            cm[i] = ((dl + k) // 64 <= q // 64).astype(np.float32)
    return ident, jx, oh, cm


class Builder:
    def __init__(self, n_layers=DEPTH, dbg=(), stop_after=None):
        self.n_layers = n_layers
        self.stop_after = stop_after
        self.dbg = set(dbg)
        nc = bass.Bass("TRN2", target_bir_lowering=False)
        self.nc = nc
        self.S = Sched(nc)
        self.inp = {}
        self.per = Arena(nc, "per", 7 * 1024)
        self.ar = Arena(nc, "arena", 198 * 1024)
        self.ps = [Tl(nc.alloc_psum_tensor(f"ps{i}", [128, 512], F32)[:], f"ps{i}") for i in range(8)]

    def din(self, name, shape, dt=F32):
        t = self.nc.dram_tensor(name, list(shape), dt, kind="ExternalInput")
        self.inp[name] = t
        return t

    def dscr(self, name, shape, dt):
        kind = "ExternalOutput" if name in self.dbg else "Internal"
        return self.nc.dram_tensor(name, list(shape), dt, kind=kind)

    def op(self, eng, fn, reads=(), writes=(), inc=True):
        self.S.op(eng, fn, [t.o for t in reads], [t.o for t in writes], inc)

    def load(self, dst, src_ap, eng="sp", dst_ap=None):
        self.S.dma(eng, dst.ap if dst_ap is None else dst_ap, src_ap, [], [dst.o], dst.o)

    def store(self, dst_ap, src, src_ap=None):
        self.S.dma("sp", dst_ap, src.ap if src_ap is None else src_ap, [src.o], [], src.o)

    def psb(self, i):
        return self.ps[i].ap.bitcast(BF16)

    def rstd_from_ssq(self, ssq, dim):
        eps = self.eps_t
        self.op("act", lambda e: e.activation(out=ssq.ap, in_=ssq.ap, func=AF.Sqrt, bias=eps.ap, scale=1.0 / dim),
                [ssq, eps], [ssq])
        self.op("dve", lambda e: e.reciprocal(out=ssq.ap, in_=ssq.ap), [ssq], [ssq])

    def transposes(self, src, src_aps, bank, rows, cols_each):
        pb = self.psb(bank)
        n = len(src_aps)
        for i, a in enumerate(src_aps):
            self.op("pe", (lambda a, i: lambda e: e.transpose(out=pb[0:rows, i * 128:(i + 1) * 128], in_=a,
                                                               identity=self.ident.ap))(a, i),
                    [src, self.ident], [self.ps[bank]], inc=(i == n - 1))
        return pb

    def build(self):
        nc, S = self.nc, self.S
        L = self.n_layers
        x_d = self.din("x", [SEQ, D])
        mem_d = self.din("mem", [MEM, D])
        pos_d = self.din("pos", [128, NT], I32)
        t5_d = self.din("t5_table", [32, 4])
        cid_d = self.din("c_ident", [128, 128])
        cjx_d = self.din("c_jx", [128, 128])
        coh_d = self.din("c_oh", [32, GLEN])
        ccm_d = self.din("c_cm", [NNEAR, 128, 512])
        W = {}
        for nm, shp in (("g_mix", [DEPTH, D]), ("g_mem", [DEPTH, D]), ("w_in", [DEPTH, D, D_IN]),
                        ("g_cq", [DEPTH, 384]), ("w_uq", [DEPTH, 384, 768]), ("g_ckv", [DEPTH, 256]),
                        ("w_ukv", [DEPTH, 256, 1024]), ("g_mla_q", [DEPTH, 96]), ("g_mla_k", [DEPTH, 96]),
                        ("g_diff_q", [DEPTH, 64]), ("g_diff_k", [DEPTH, 64]), ("lam_q1", [DEPTH, 64]),
                        ("lam_k1", [DEPTH, 64]), ("lam_q2", [DEPTH, 64]), ("lam_k2", [DEPTH, 64]),
                        ("g_diff_out", [DEPTH, 128]), ("w_mem_kv", [DEPTH, D, D]), ("g_mem_q", [DEPTH, 128]),
                        ("g_mem_k", [DEPTH, 128]), ("w_branch", [DEPTH, 3, 512, D]), ("w_out", [DEPTH, D, D]),
                        ("g_mlp", [DEPTH, D]), ("w_ff1", [DEPTH, D, 4 * D]), ("w_ff2", [DEPTH, 4 * D, D])):
            W[nm] = self.din(nm, shp)
        self.W = W
        out_d = nc.dram_tensor("out", [SEQ, D], F32, kind="ExternalOutput")
        sc = {}
        sc["hT"] = self.dscr("hTd", [NT, 128, 8, 128], BF16)
        sc["qT"] = self.dscr("qTd", [8, 96, SEQ], BF16)
        sc["kT"] = self.dscr("kTd", [8, 96, SEQ], BF16)
        sc["v"] = self.dscr("vd", [SEQ, 512], BF16)
        sc["dqT"] = self.dscr("dqTd", [512, SEQ], BF16)
        sc["dkT"] = self.dscr("dkTd", [512, SEQ], BF16)
        sc["dv"] = self.dscr("dvd", [SEQ, 512], BF16)
        sc["mqT"] = self.dscr("mqTd", [512, SEQ], BF16)
        sc["oT"] = self.dscr("oTd", [3, 512, SEQ], BF16)
        sc["kmT"] = self.dscr("kmTd", [128, 4, MEM], BF16)
        sc["vm"] = self.dscr("vmd", [MEM, 512], BF16)
        sc["G"] = self.dscr("Gd", [4, GLEN], F32)
        sc["EB"] = self.dscr("EBd", [4, NNEAR, 128, 512], F32)
        self.sc = sc
        WSH = {"w_in": [D, D_IN], "w_uq": [384, 768], "w_ukv": [256, 1024], "w_mem_kv": [D, D], "w_branch": [1536, D],
               "w_out": [D, D], "w_ff1": [D, 4 * D], "w_ff2": [4 * D, D]}
        self.wbf = [{nm: self.nc.dram_tensor(f"bf_{nm}_{l}", shp, BF16, kind="Internal") for nm, shp in WSH.items()} for l in range(L)]
        self.bgq = []
        self.bgobj = {}
        self.bgleft = {}

        def add_conv(grp, nm, l, r0, r1, c0, c1):
            src = (W[nm][l].rearrange("n r c -> (n r) c") if nm == "w_branch" else W[nm][l])[r0:r1, c0:c1]
            dst = self.wbf[l][nm][r0:r1, c0:c1]
            g = self.bgobj.setdefault(grp, Obj(str(grp), persist=True))
            self.bgleft[grp] = self.bgleft.get(grp, 0) + 1

            def emit():
                self.S.dma("pool", dst, src, [], [g], g)
                self.bgleft[grp] -= 1
            self.bgq.append((grp, emit))

        for l in range(L):
            for c in range(8):
                add_conv((l, "A"), "w_mem_kv", l, c * 128, (c + 1) * 128, 0, D)
            for c in range(8):
                add_conv((l, "A"), "w_in", l, c * 128, (c + 1) * 128, 0, C1)
            for c in range(3):
                add_conv((l, "A"), "w_uq", l, c * 128, (c + 1) * 128, 0, 768)
            for c in range(2):
                add_conv((l, "A"), "w_ukv", l, c * 128, (c + 1) * 128, 0, 1024)
            for c in range(8):
                add_conv((l, "B"), "w_in", l, c * 128, (c + 1) * 128, C1, D_IN)
            for c in range(12):
                add_conv((l, "B"), "w_branch", l, c * 128, (c + 1) * 128, 0, D)
            for c in range(8):
                add_conv((l, "B"), "w_out", l, c * 128, (c + 1) * 128, 0, D)
            for c in range(8):
                add_conv((l, "C"), "w_ff1", l, c * 128, (c + 1) * 128, 0, 4 * D)
            for c in range(32):
                add_conv((l, "D"), "w_ff2", l, c * 128, (c + 1) * 128, 0, D)

        per = self.per
        self.ident = per.tile([128, 128], BF16, "ident")
        self.identf = per.tile([128, 128], F32, "identf")
        self.jx = per.tile([128, 128], F32, "jx")
        self.ones_bf = per.tile([128, 128], BF16, "ones_bf")
        self.ones_f = per.tile([128, 128], F32, "ones_f")
        self.eps_t = per.tile([128, 1], F32, "eps")
        self.cos_t = per.tile([128, NT, 16], F32, "cos")
        self.sin_t = per.tile([128, NT, 16], F32, "sin")
        self.b15 = per.tile([128, 4], F32, "b15")
        self.nlam = per.tile([128, 1], F32, "nlam")
        self.gdo = per.tile([128, 1], F32, "gdo")

        self.setup(pos_d, t5_d, cid_d, cjx_d, coh_d, ccm_d)
        stop = self.stop_after
        for l in range(L):
            xin = x_d if l == 0 else out_d
            S.barrier()
            if l > 0:
                S.new_epoch()
            self.phase_mem(l, mem_d)
            S.barrier()
            self.phase_p1(l, xin)
            S.barrier()
            if stop == "p1":
                break
            self.phase_attn(l)
            S.barrier()
            if stop == "attn":
                break
            self.phase_p3(l, xin, out_d)
            S.barrier()
            if stop == "p3":
                break
            self.phase_p4(l, out_d)
        self.bg_pump(len(self.bgq))
        S.barrier(final=True)
        S.replay()
        return nc

    def setup(self, pos_d, t5_d, cid_d, cjx_d, coh_d, ccm_d):
        ar = self.ar
        ar.reset()
        op = self.op
        self.load(self.identf, cid_d[:, :])
        self.load(self.jx, cjx_d[:, :])
        op("dve", lambda e: e.tensor_copy(out=self.ident.ap, in_=self.identf.ap), [self.identf], [self.ident])
        op("dve", lambda e: e.memset(self.ones_bf.ap, 1.0), [], [self.ones_bf])
        op("dve", lambda e: e.memset(self.ones_f.ap, 1.0), [], [self.ones_f])
        op("dve", lambda e: e.memset(self.eps_t.ap, EPS), [], [self.eps_t])
        self.load(self.b15, t5_d[FAR_BUCKET, :].partition_broadcast(128))
        posi = ar.tile([128, NT], I32, "posi")
        posf = ar.tile([128, NT], F32, "posf")
        ang = ar.tile([128, NT, 16], F32, "ang")
        a2 = ar.tile([128, NT * 16], F32, "a2")
        kf = ar.tile([128, NT * 16], F32, "kf")
        ki = ar.tile([128, NT * 16], I32, "ki")
        self.load(posi, pos_d[:, :])
        op("dve", lambda e: e.tensor_copy(out=posf.ap, in_=posi.ap), [posi], [posf])
        inv = np.power(np.float32(10000.0), -np.arange(16, dtype=np.float32) / np.float32(16)).astype(np.float32)
        for j in range(16):
            op("dve", (lambda j: lambda e: e.tensor_scalar(out=ang.ap[:, :, j], in0=posf.ap, scalar1=float(inv[j]),
                                                           scalar2=None, op0=ALU.mult))(j), [posf], [ang])
        angf = ang.ap.rearrange("p t j -> p (t j)")
        for tab, shift in ((self.sin_t, 0.0), (self.cos_t, math.pi / 2)):
            tabf = tab.ap.rearrange("p t j -> p (t j)")
            op("dve", lambda e: e.tensor_scalar(out=a2.ap, in0=angf, scalar1=float(shift), scalar2=None, op0=ALU.add),
               [ang], [a2])
            op("dve", lambda e: e.tensor_scalar(out=ki.ap, in0=a2.ap, scalar1=float(1 / (2 * math.pi)), scalar2=None,
                                                op0=ALU.mult), [a2], [ki])
            op("dve", lambda e: e.tensor_copy(out=kf.ap, in_=ki.ap), [ki], [kf])
            op("dve", lambda e: e.scalar_tensor_tensor(out=a2.ap, in0=kf.ap, scalar=float(-2 * math.pi), in1=a2.ap,
                                                       op0=ALU.mult, op1=ALU.add), [kf, a2], [a2])
            op("dve", lambda e: e.tensor_scalar(out=kf.ap, in0=a2.ap, scalar1=float(math.pi), scalar2=float(2 * math.pi),
                                                op0=ALU.is_gt, op1=ALU.mult), [a2], [kf])
            op("dve", lambda e: e.tensor_tensor(out=a2.ap, in0=a2.ap, in1=kf.ap, op=ALU.subtract), [a2, kf], [a2])
            op("dve", lambda e: e.tensor_scalar(out=kf.ap, in0=a2.ap, scalar1=float(-math.pi), scalar2=float(-2 * math.pi),
                                                op0=ALU.is_lt, op1=ALU.mult), [a2], [kf])
            op("dve", lambda e: e.tensor_tensor(out=a2.ap, in0=a2.ap, in1=kf.ap, op=ALU.subtract), [a2, kf], [a2])
            op("dve", lambda e: e.tensor_scalar(out=a2.ap, in0=a2.ap, scalar1=float(-PI_LO), scalar2=float(PI_LO),
                                                op0=ALU.max, op1=ALU.min), [a2], [a2])
            op("act", (lambda tabf: lambda e: e.activation(out=tabf, in_=a2.ap, func=AF.Sin))(tabf), [a2], [tab])
        tab32 = ar.tile([32, 4], F32, "tab32")
        oh = ar.tile([32, GLEN], F32, "oh")
        gsb = ar.tile([4, GLEN], F32, "gsb")
        self.load(tab32, t5_d[:, :])
        self.load(oh, coh_d[:, :])
        for c0 in range(0, GLEN, 512):
            c1 = min(c0 + 512, GLEN)
            op("pe", (lambda c0, c1: lambda e: e.matmul(self.ps[0].ap[0:4, 0:c1 - c0], lhsT=tab32.ap, rhs=oh.ap[:, c0:c1],
                                                        start=True, stop=True))(c0, c1), [tab32, oh], [self.ps[0]])
            op("dve", (lambda c0, c1: lambda e: e.tensor_copy(out=gsb.ap[:, c0:c1], in_=self.ps[0].ap[0:4, 0:c1 - c0]))(c0, c1),
               [self.ps[0]], [gsb])
        self.store(self.sc["G"][:, :], gsb)
        S = self.S
        S.barrier()
        hk = [ar.tile([128, 512], F32, f"hk{i}") for i in range(2)]
        eb = [ar.tile([128, 512], F32, f"eb{i}") for i in range(2)]
        cm = [ar.tile([128, 512], F32, f"cm{i}") for i in range(2)]
        n = 0
        for h in range(4):
            for dl, i in NEAR_IDX.items():
                s = n % 2
                n += 1
                off = GD0 - dl - 127
                assert 0 <= off and off + 127 + 511 < GLEN
                self.load(hk[s], bass.AP(self.sc["G"], h * GLEN + off, [[1, 128], [1, 512]]))
                self.load(cm[s], ccm_d[i, :, :])
                bk = 1 + s
                op("pe", (lambda s, bk: lambda e: e.matmul(self.ps[bk].ap, lhsT=self.jx.ap, rhs=hk[s].ap, start=True, stop=True))(s, bk),
                   [self.jx, hk[s]], [self.ps[bk]])
                op("act", (lambda s, bk: lambda e: e.activation(out=eb[s].ap, in_=self.ps[bk].ap, func=AF.Exp))(s, bk),
                   [self.ps[bk]], [eb[s]])
                op("dve", (lambda s: lambda e: e.tensor_tensor(out=eb[s].ap, in0=eb[s].ap, in1=cm[s].ap, op=ALU.mult))(s),
                   [eb[s], cm[s]], [eb[s]])
                self.store(self.sc["EB"][h, i, :, :], eb[s])

    def norm_tile(self, xt, gbc, hout, junk, ssq):
        self.op("act", lambda e: e.activation(out=junk.ap, in_=xt.ap, func=AF.Square, accum_out=ssq.ap), [xt], [junk, ssq])
        self.rstd_from_ssq(ssq, D)
        self.op("dve", lambda e: e.scalar_tensor_tensor(out=hout.ap, in0=xt.ap, scalar=ssq.ap, in1=gbc.ap,
                                                        op0=ALU.mult, op1=ALU.mult), [xt, ssq, gbc], [hout])

    def bg_pump(self, n):
        for _ in range(min(n, len(self.bgq))):
            self.bgq.pop(0)[1]()

    def bg_require(self, grp):
        while self.bgleft.get(grp, 0) > 0:
            self.bg_pump(1)
        return self.bgobj[grp]

    def wload(self, dst, src3, nchunk, grp):
        g = self.bg_require(grp)
        for c in range(nchunk):
            self.S.dma("sp", dst.ap[:, c, :], src3[c * 128:(c + 1) * 128, :], [g], [dst.o], dst.o)

    def bcast_load(self, dst, vec_ap):
        self.load(dst, vec_ap.partition_broadcast(128))

    def phase_mem(self, l, mem_d):
        ar = self.ar
        ar.reset()
        op, W = self.op, self.W
        wkv = ar.tile([128, 8, D], BF16, "wkv")
        self.wload(wkv, self.wbf[l]["w_mem_kv"], 8, (l, "A"))
        gmem = ar.tile([128, D], F32, "gmem")
        gk = ar.tile([128, 128], F32, "gmemk")
        self.bcast_load(gmem, W["g_mem"][l])
        self.bcast_load(gk, W["g_mem_k"][l])
        junk = ar.tile([128, D], BF16, "junk")
        kmT = ar.tile([128, 4, MEM], BF16, "kmT")
        for t in range(2):
            xt = ar.tile([128, D], F32, f"mx{t}")
            hm = ar.tile([128, D], BF16, f"hm{t}")
            hmT = ar.tile([128, 8, 128], BF16, f"hmT{t}")
            ssq = ar.tile([128, 1], F32, f"mssq{t}")
            self.load(xt, mem_d[t * 128:(t + 1) * 128, :])
            self.norm_tile(xt, gmem, hm, junk, ssq)
            pb = self.transposes(hm, [hm.ap[:, c * 128:(c + 1) * 128] for c in range(8)], 0, 128, 128)
            op("act", lambda e: e.copy(out=hmT.ap.rearrange("p c t -> p (c t)"), in_=pb), [self.ps[0]], [hmT])
            for cb in range(2):
                bk = 1 + cb
                for c in range(8):
                    op("pe", (lambda c, cb, bk: lambda e: e.matmul(self.ps[bk].ap, lhsT=hmT.ap[:, c, :],
                                                                    rhs=wkv.ap[:, c, cb * 512:(cb + 1) * 512],
                                                                    start=(c == 0), stop=(c == 7)))(c, cb, bk),
                       [hmT, wkv], [self.ps[bk]], inc=(c == 7))
            sq = ar.tile([128, 512], F32, f"msq{t}")
            s4 = ar.tile([128, 4], F32, f"ms4{t}")
            kn = ar.tile([128, 4, 128], F32, f"mkn{t}")
            kb_ = ar.tile([128, 4, 128], BF16, f"mkb{t}")
            vb = ar.tile([128, 512], BF16, f"mvb{t}")
            op("act", lambda e: e.activation(out=sq.ap, in_=self.ps[1].ap, func=AF.Square), [self.ps[1]], [sq])
            op("dve", lambda e: e.tensor_reduce(out=s4.ap, in_=sq.ap.rearrange("p (h d) -> p h d", h=4), axis=AX.X, op=ALU.add),
               [sq], [s4])
            self.rstd_from_ssq(s4, 128)
            op("dve", lambda e: e.tensor_tensor(out=kn.ap, in0=self.ps[1].ap.rearrange("p (h d) -> p h d", h=4),
                                                in1=s4.ap.unsqueeze(2).to_broadcast([128, 4, 128]), op=ALU.mult),
               [self.ps[1], s4], [kn])
            op("dve", lambda e: e.tensor_tensor(out=kb_.ap, in0=kn.ap, in1=gk.ap.unsqueeze(1).to_broadcast([128, 4, 128]),
                                                op=ALU.mult), [kn, gk], [kb_])
            op("act", lambda e: e.copy(out=vb.ap, in_=self.ps[2].ap), [self.ps[2]], [vb])
            self.store(self.sc["vm"][t * 128:(t + 1) * 128, :], vb)
            pb = self.transposes(kb_, [kb_.ap[:, h, :] for h in range(4)], 3, 128, 128)
            op("dve", (lambda t, pb: lambda e: e.tensor_copy(out=kmT.ap[:, :, t * 128:(t + 1) * 128],
                                                             in_=pb[:, 0:512].rearrange("p (h m) -> p h m", h=4)))(t, pb),
               [self.ps[3]], [kmT])
        self.store(self.sc["kmT"][:, :, :], kmT)

    def phase_p1(self, l, xin):
        ar = self.ar
        ar.reset()
        op, W, sc = self.op, self.W, self.sc
        ps = self.ps
        w1 = ar.tile([128, 8, C1], BF16, "w1")
        wuq = ar.tile([128, 3, 768], BF16, "wuq")
        wukv = ar.tile([128, 2, 1024], BF16, "wukv")
        self.wload(w1, self.wbf[l]["w_in"][:, 0:C1], 8, (l, "A"))
        self.wload(wuq, self.wbf[l]["w_uq"], 3, (l, "A"))
        self.wload(wukv, self.wbf[l]["w_ukv"], 2, (l, "A"))
        gmix = ar.tile([128, D], F32, "gmix")
        gcq = ar.tile([128, 384], F32, "gcq")
        gckv = ar.tile([128, 256], F32, "gckv")
        gq = ar.tile([128, 96], F32, "gq")
        gk = ar.tile([128, 96], F32, "gk")
        gdq = ar.tile([128, 64], F32, "gdq")
        gdk = ar.tile([128, 64], F32, "gdk")
        gmq = ar.tile([128, 128], F32, "gmq")
        for t_, nm in ((gmix, "g_mix"), (gcq, "g_cq"), (gckv, "g_ckv"), (gq, "g_mla_q"), (gk, "g_mla_k"),
                       (gdq, "g_diff_q"), (gdk, "g_diff_k"), (gmq, "g_mem_q")):
            self.bcast_load(t_, W[nm][l])
        NB = 2
        junk = [ar.tile([128, D], BF16, f"junk{i}") for i in range(NB)]
        xt = [[ar.tile([128, D], F32, f"xt{i}{j}") for j in range(2)] for i in range(NB)]
        ssq = [ar.tile([128, 1], F32, f"ssq{i}") for i in range(NB)]
        hb = [ar.tile([128, D], BF16, f"hb{i}") for i in range(NB)]
        hT = [ar.tile([128, 8, 128], BF16, f"hT{i}") for i in range(NB)]
        sq_q = [ar.tile([128, 1], F32, f"sqq{i}") for i in range(NB)]
        cqn = [ar.tile([128, 384], BF16, f"cqn{i}") for i in range(NB)]
        cqT = [ar.tile([128, 3, 128], BF16, f"cqT{i}") for i in range(NB)]
        sqt = [ar.tile([128, 1024], F32, f"sqt{i}") for i in range(NB)]
        s8q = [ar.tile([128, 8], F32, f"s8q{i}") for i in range(NB)]
        qn = [ar.tile([128, 8, 96], F32, f"qn{i}") for i in range(NB)]
        qb = [ar.tile([128, 8, 96], BF16, f"qb{i}") for i in range(NB)]
        rt = [ar.tile([128, 4, 8, 16], F32, f"rt{i}") for i in range(NB)]
        qTs = [ar.tile([96, 8, 128], BF16, f"qTs{i}") for i in range(NB)]
        ckr = [ar.tile([128, 32], F32, f"ckr{i}") for i in range(NB)]
        sq_kv = [ar.tile([128, 1], F32, f"sqkv{i}") for i in range(NB)]
        ckvn = [ar.tile([128, 256], BF16, f"ckvn{i}") for i in range(NB)]
        ckvT = [ar.tile([128, 2, 128], BF16, f"ckvT{i}") for i in range(NB)]
        s8k = [ar.tile([128, 8], F32, f"s8k{i}") for i in range(NB)]
        s1k = [ar.tile([128, 1], F32, f"s1k{i}") for i in range(NB)]
        kn = [ar.tile([128, 8, 96], F32, f"kn{i}") for i in range(NB)]
        kb_ = [ar.tile([128, 8, 96], BF16, f"kb{i}") for i in range(NB)]
        kTs = [ar.tile([96, 8, 128], BF16, f"kTs{i}") for i in range(NB)]
        vb = [ar.tile([128, 8, 64], BF16, f"vb{i}") for i in range(NB)]
        s8d = [ar.tile([128, 8], F32, f"s8d{i}") for i in range(NB)]
        dn = [ar.tile([128, 8, 64], F32, f"dn{i}") for i in range(NB)]
        db = [[ar.tile([128, 8, 64], BF16, f"db{j}{i}") for i in range(NB)] for j in range(2)]
        dTs = [[ar.tile([128, 4, 128], BF16, f"dTs{j}{i}") for i in range(NB)] for j in range(2)]
        dvb = [ar.tile([128, 512], BF16, f"dvb{i}") for i in range(NB)]
        s4m = [ar.tile([128, 4], F32, f"s4m{i}") for i in range(NB)]
        mn = [ar.tile([128, 4, 128], F32, f"mn{i}") for i in range(NB)]
        mb = [ar.tile([128, 4, 128], BF16, f"mb{i}") for i in range(NB)]
        mTs = [ar.tile([128, 4, 128], BF16, f"mTs{i}") for i in range(NB)]

        def bc(ap2, shape, axis):
            return ap2.unsqueeze(axis).to_broadcast(shape)

        rtab = {}
        for nm_, g_ in (("q", gq), ("k", gk)):
            tb = ar.tile([128, 4, NT, 16], F32, f"rtab{nm_}")
            for i_, (src_, lo) in enumerate(((self.cos_t, 64), (self.sin_t, 80), (self.cos_t, 80), (self.sin_t, 64))):
                op("pool", (lambda tb, i_, src_, lo, g_: lambda e: e.tensor_tensor(
                    out=tb.ap[:, i_], in0=src_.ap, in1=g_.ap[:, lo:lo + 16].unsqueeze(1).to_broadcast([128, NT, 16]),
                    op=ALU.mult))(tb, i_, src_, lo, g_), [src_, g_], [tb])
            rtab[nm_] = tb

        def rope(src, dst, s, t, which, op=op):
            tb = rtab[which]
            c1, s2, c2, s1 = (bc(tb.ap[:, i_, t, :], [128, 8, 16], 1) for i_ in range(4))
            x1 = src.ap[:, :, 64:80]
            x2 = src.ap[:, :, 80:96]
            r = rt[s]
            op("dve", lambda e: e.tensor_tensor(out=r.ap[:, 0], in0=x1, in1=c1, op=ALU.mult), [src, tb], [r])
            op("dve", lambda e: e.tensor_tensor(out=r.ap[:, 1], in0=x2, in1=s2, op=ALU.mult), [src, tb], [r])
            op("dve", lambda e: e.tensor_tensor(out=r.ap[:, 2], in0=x2, in1=c2, op=ALU.mult), [src, tb], [r])
            op("dve", lambda e: e.tensor_tensor(out=r.ap[:, 3], in0=x1, in1=s1, op=ALU.mult), [src, tb], [r])
            op("dve", lambda e: e.tensor_tensor(out=dst.ap[:, :, 64:80], in0=r.ap[:, 0], in1=r.ap[:, 1], op=ALU.subtract), [r], [dst])
            op("dve", lambda e: e.tensor_tensor(out=dst.ap[:, :, 80:96], in0=r.ap[:, 2], in1=r.ap[:, 3], op=ALU.add), [r], [dst])

        def load_x(t, s):
            self.load(xt[s][(t // NB) % 2], xin[t * 128:(t + 1) * 128, :])

        def rstd_ap(tl, ap, dim):
            op("act", lambda e: e.activation(out=ap, in_=ap, func=AF.Sqrt, bias=self.eps_t.ap, scale=1.0 / dim), [tl, self.eps_t], [tl])
            op("dve", lambda e: e.reciprocal(out=ap, in_=ap), [tl], [tl])

        def rstd_q(qq, tl, ap, dim):
            qq("act", lambda e: e.activation(out=ap, in_=ap, func=AF.Sqrt, bias=self.eps_t.ap, scale=1.0 / dim), [tl, self.eps_t], [tl])
            qq("dve", lambda e: e.reciprocal(out=ap, in_=ap), [tl], [tl])

        def merge(*gs):
            gs = list(gs)
            while gs:
                for g_ in list(gs):
                    try:
                        next(g_)
                        yield
                    except StopIteration:
                        gs.remove(g_)

        class Q:
            def __init__(q):
                q.items = []

            def __call__(q, eng, fn, reads=(), writes=(), inc=True):
                q.items.append((eng, fn, reads, writes, inc))

            def call(q, fn):
                q.items.append((None, fn, None, None, None))

            def flush(q):
                prev = None
                items, q.items = q.items, []
                for eng, fn, r, w, inc in items:
                    if eng is None:
                        fn()
                        continue
                    if prev is not None and eng != prev:
                        yield
                    op(eng, fn, r, w, inc)
                    prev = eng
                yield

        def tile_gen(t, s):
            BT, Z0, Z1, Z2 = 4 * s, 4 * s + 1, 4 * s + 2, 4 * s + 3
            tok = slice(t * 128, (t + 1) * 128)
            x_ = xt[s][(t // NB) % 2]
            if t + NB < NT:
                load_x(t + NB, s)
            self.bg_pump(1)
            q0 = Q()
            q0("act", lambda e: e.activation(out=junk[s].ap, in_=x_.ap, func=AF.Square, accum_out=ssq[s].ap), [x_], [junk[s], ssq[s]])
            q0("act", lambda e: e.activation(out=ssq[s].ap, in_=ssq[s].ap, func=AF.Sqrt, bias=self.eps_t.ap, scale=1.0 / D),
               [ssq[s], self.eps_t], [ssq[s]])
            q0("dve", lambda e: e.reciprocal(out=ssq[s].ap, in_=ssq[s].ap), [ssq[s]], [ssq[s]])
            q0("dve", lambda e: e.scalar_tensor_tensor(out=hb[s].ap, in0=x_.ap, scalar=ssq[s].ap, in1=gmix.ap,
                                                       op0=ALU.mult, op1=ALU.mult), [x_, ssq[s], gmix], [hb[s]])
            yield from q0.flush()
            pb = self.transposes(hb[s], [hb[s].ap[:, c * 128:(c + 1) * 128] for c in range(8)], BT, 128, 128)
            op("act", lambda e: e.copy(out=hT[s].ap.rearrange("p c t -> p (c t)"), in_=pb), [ps[BT]], [hT[s]])
            self.store(sc["hT"][t], hT[s])
            yield

            def zmm_now(bank, c0, c1):
                for c in range(8):
                    op("pe", lambda e: e.matmul(ps[bank].ap[:, 0:c1 - c0], lhsT=hT[s].ap[:, c, :], rhs=w1.ap[:, c, c0:c1],
                                                start=(c == 0), stop=(c == 7)), [hT[s], w1], [ps[bank]], inc=(c == 7))

            def chain_q():
                qq = Q()
                qq.call((lambda *a: (lambda: zmm_now(*a)))(Z0, 0, 384))
                yield from qq.flush()
                qq("act", lambda e: e.activation(out=junk[s].ap[:, 0:384], in_=ps[Z0].ap[:, 0:384], func=AF.Square,
                                                 accum_out=sq_q[s].ap), [ps[Z0]], [junk[s], sq_q[s]])
                rstd_q(qq, sq_q[s], sq_q[s].ap, 384)
                qq("dve", lambda e: e.scalar_tensor_tensor(out=cqn[s].ap, in0=ps[Z0].ap[:, 0:384], scalar=sq_q[s].ap,
                                                           in1=gcq.ap, op0=ALU.mult, op1=ALU.mult), [ps[Z0], sq_q[s], gcq], [cqn[s]])
                yield from qq.flush()
                pb = self.psb(BT)
                qq.call((lambda *a: (lambda: self.transposes(*a)))(cqn[s], [cqn[s].ap[:, c * 128:(c + 1) * 128] for c in range(3)], BT, 128, 128))
                qq("act", lambda e: e.copy(out=cqT[s].ap.rearrange("p c t -> p (c t)"), in_=pb[:, 0:384]), [ps[BT]], [cqT[s]])
                yield from qq.flush()
                for hb_ in range(2):
                    hs = slice(hb_ * 4, (hb_ + 1) * 4)
                    def qup_now(hb_):
                        for c in range(3):
                            op("pe", lambda e: e.matmul(ps[Z0].ap[:, 0:384], lhsT=cqT[s].ap[:, c, :], rhs=wuq.ap[:, c, hb_ * 384:(hb_ + 1) * 384],
                                                        start=(c == 0), stop=(c == 2)), [cqT[s], wuq], [ps[Z0]], inc=(c == 2))
                    qq.call((lambda a: (lambda: qup_now(a)))(hb_))
                    yield from qq.flush()
                    qv = ps[Z0].ap[:, 0:384].rearrange("p (h d) -> p h d", h=4)
                    qq("act", lambda e: e.activation(out=qn[s].ap[:, hs, :], in_=qv, func=AF.Square), [ps[Z0]], [qn[s]])
                    qq("dve", lambda e: e.tensor_reduce(out=s8q[s].ap[:, hs], in_=qn[s].ap[:, hs, :], axis=AX.X, op=ALU.add), [qn[s]], [s8q[s]])
                    rstd_q(qq, s8q[s], s8q[s].ap[:, hs], 96)
                    yield from qq.flush()
                    qq("dve", lambda e: e.tensor_tensor(out=qn[s].ap[:, hs, :], in0=qv, in1=bc(s8q[s].ap[:, hs], [128, 4, 96], 2), op=ALU.mult),
                       [ps[Z0], s8q[s]], [qn[s]])
                    yield from qq.flush()
                qq("dve", lambda e: e.tensor_tensor(out=qb[s].ap[:, :, 0:64], in0=qn[s].ap[:, :, 0:64],
                                                    in1=bc(gq.ap[:, 0:64], [128, 8, 64], 1), op=ALU.mult), [qn[s], gq], [qb[s]])
                rope(qn[s], qb[s], s, t, "q", qq)
                yield from qq.flush()
                pb = self.psb(BT)
                qq.call((lambda *a: (lambda: self.transposes(*a)))(qb[s], [qb[s].ap[:, h, :] for h in range(8)], BT, 96, 128))
                qq("act", lambda e: e.copy(out=qTs[s].ap.rearrange("p h t -> p (h t)"), in_=pb[0:96, :]), [ps[BT]], [qTs[s]])
                qq.call((lambda *a: (lambda: self.store(*a)))(sc["qT"][:, :, tok].rearrange("h d t -> d h t"), qTs[s]))
                yield from qq.flush()

            def chain_k():
                qq = Q()
                qq.call((lambda *a: (lambda: zmm_now(*a)))(Z1, 384, 672))
                yield from qq.flush()
                qq("act", lambda e: e.activation(out=junk[s].ap[:, 384:640], in_=ps[Z1].ap[:, 0:256], func=AF.Square,
                                                 accum_out=sq_kv[s].ap), [ps[Z1]], [junk[s], sq_kv[s]])
                rstd_q(qq, sq_kv[s], sq_kv[s].ap, 256)
                qq("dve", lambda e: e.scalar_tensor_tensor(out=ckvn[s].ap, in0=ps[Z1].ap[:, 0:256], scalar=sq_kv[s].ap,
                                                           in1=gckv.ap, op0=ALU.mult, op1=ALU.mult), [ps[Z1], sq_kv[s], gckv], [ckvn[s]])
                qq("dve", lambda e: e.tensor_copy(out=ckr[s].ap, in_=ps[Z1].ap[:, 256:288]), [ps[Z1]], [ckr[s]])
                yield from qq.flush()
                pb = self.psb(BT)
                qq.call((lambda *a: (lambda: self.transposes(*a)))(ckvn[s], [ckvn[s].ap[:, c * 128:(c + 1) * 128] for c in range(2)], BT, 128, 128))
                qq("act", lambda e: e.copy(out=ckvT[s].ap.rearrange("p c t -> p (c t)"), in_=pb[:, 0:256]), [ps[BT]], [ckvT[s]])
                qq("act", lambda e: e.activation(out=junk[s].ap[:, 640:672], in_=ckr[s].ap, func=AF.Square, accum_out=s1k[s].ap),
                   [ckr[s]], [junk[s], s1k[s]])
                yield from qq.flush()
                for hb_ in range(2):
                    hs = slice(hb_ * 4, (hb_ + 1) * 4)
                    def kvup_now(hb_):
                        for c in range(2):
                            op("pe", lambda e: e.matmul(ps[Z1].ap, lhsT=ckvT[s].ap[:, c, :], rhs=wukv.ap[:, c, hb_ * 512:(hb_ + 1) * 512],
                                                        start=(c == 0), stop=(c == 1)), [ckvT[s], wukv], [ps[Z1]], inc=(c == 1))
                    qq.call((lambda a: (lambda: kvup_now(a)))(hb_))
                    yield from qq.flush()
                    kv = ps[Z1].ap.rearrange("p (h d) -> p h d", h=4)
                    qq("act", lambda e: e.copy(out=vb[s].ap[:, hs, :], in_=kv[:, :, 64:128]), [ps[Z1]], [vb[s]])
                    qq("act", lambda e: e.activation(out=kn[s].ap[:, hs, 0:64], in_=kv[:, :, 0:64], func=AF.Square), [ps[Z1]], [kn[s]])
                    qq("dve", lambda e: e.tensor_reduce(out=s8k[s].ap[:, hs], in_=kn[s].ap[:, hs, 0:64], axis=AX.X, op=ALU.add), [kn[s]], [s8k[s]])
                    qq("dve", lambda e: e.tensor_scalar(out=s8k[s].ap[:, hs], in0=s8k[s].ap[:, hs], scalar1=s1k[s].ap, scalar2=None, op0=ALU.add),
                       [s8k[s], s1k[s]], [s8k[s]])
                    rstd_q(qq, s8k[s], s8k[s].ap[:, hs], 96)
                    yield from qq.flush()
                    qq("dve", lambda e: e.tensor_tensor(out=kn[s].ap[:, hs, 0:64], in0=kv[:, :, 0:64],
                                                        in1=bc(s8k[s].ap[:, hs], [128, 4, 64], 2), op=ALU.mult), [ps[Z1], s8k[s]], [kn[s]])
                    yield from qq.flush()
                qq.call((lambda *a: (lambda: self.store(*a)))(sc["v"][tok, :], vb[s], vb[s].ap.rearrange("p h d -> p (h d)")))
                qq("dve", lambda e: e.tensor_tensor(out=kn[s].ap[:, :, 64:96], in0=bc(ckr[s].ap, [128, 8, 32], 1),
                                                    in1=bc(s8k[s].ap, [128, 8, 32], 2), op=ALU.mult), [ckr[s], s8k[s]], [kn[s]])
                qq("dve", lambda e: e.tensor_tensor(out=kb_[s].ap[:, :, 0:64], in0=kn[s].ap[:, :, 0:64],
                                                    in1=bc(gk.ap[:, 0:64], [128, 8, 64], 1), op=ALU.mult), [kn[s], gk], [kb_[s]])
                yield from qq.flush()
                rope(kn[s], kb_[s], s, t, "k", qq)
                yield from qq.flush()
                pb = self.psb(BT)
                qq.call((lambda *a: (lambda: self.transposes(*a)))(kb_[s], [kb_[s].ap[:, h, :] for h in range(8)], BT, 96, 128))
                qq("act", lambda e: e.copy(out=kTs[s].ap.rearrange("p h t -> p (h t)"), in_=pb[0:96, :]), [ps[BT]], [kTs[s]])
                qq.call((lambda *a: (lambda: self.store(*a)))(sc["kT"][:, :, tok].rearrange("h d t -> d h t"), kTs[s]))
                yield from qq.flush()

            def chain_d():
                qq = Q()
                def diff_part(j, g_):
                    qq("act", lambda e: e.activation(out=sqt[s].ap[:, 0:512], in_=ps[Z2].ap, func=AF.Square), [ps[Z2]], [sqt[s]])
                    qq("dve", lambda e: e.tensor_reduce(out=s8d[s].ap, in_=sqt[s].ap[:, 0:512].rearrange("p (h d) -> p h d", h=8),
                                                        axis=AX.X, op=ALU.add), [sqt[s]], [s8d[s]])
                    rstd_q(qq, s8d[s], s8d[s].ap, 64)
                    qq("dve", lambda e: e.tensor_tensor(out=dn[s].ap, in0=ps[Z2].ap.rearrange("p (h d) -> p h d", h=8),
                                                        in1=bc(s8d[s].ap, [128, 8, 64], 2), op=ALU.mult), [ps[Z2], s8d[s]], [dn[s]])
                    qq("pool", lambda e: e.tensor_tensor(out=db[j][s].ap, in0=dn[s].ap, in1=bc(g_.ap, [128, 8, 64], 1), op=ALU.mult),
                       [dn[s], g_], [db[j][s]])

                def diff_tr(j, dst):
                    dflat = db[j][s].ap.rearrange("p h d -> p (h d)")
                    pb = self.psb(BT)
                    qq.call((lambda *a: (lambda: self.transposes(*a)))(db[j][s], [dflat[:, c * 128:(c + 1) * 128] for c in range(4)], BT, 128, 128))
                    qq("act", lambda e: e.copy(out=dTs[j][s].ap.rearrange("p c t -> p (c t)"), in_=pb[:, 0:512]), [ps[BT]], [dTs[j][s]])
                    qq.call((lambda *a: (lambda: self.store(*a)))(dst[:, tok].rearrange("(c p) t -> p c t", p=128), dTs[j][s]))

                qq.call((lambda *a: (lambda: zmm_now(*a)))(Z2, 672, 1184))
                yield from qq.flush()
                diff_part(0, gdq)
                yield from qq.flush()
                qq.call((lambda *a: (lambda: zmm_now(*a)))(Z2, 1184, 1696))
                yield from qq.flush()
                diff_tr(0, sc["dqT"])
                yield from qq.flush()
                diff_part(1, gdk)
                yield from qq.flush()
                qq.call((lambda *a: (lambda: zmm_now(*a)))(Z2, 1696, 2208))
                yield from qq.flush()
                diff_tr(1, sc["dkT"])
                qq("act", lambda e: e.copy(out=dvb[s].ap, in_=ps[Z2].ap), [ps[Z2]], [dvb[s]])
                qq.call((lambda *a: (lambda: self.store(*a)))(sc["dv"][tok, :], dvb[s]))
                yield from qq.flush()
                qq.call((lambda *a: (lambda: zmm_now(*a)))(Z2, 2208, 2720))
                yield from qq.flush()
                qq("act", lambda e: e.activation(out=sqt[s].ap[:, 512:1024], in_=ps[Z2].ap, func=AF.Square), [ps[Z2]], [sqt[s]])
                qq("dve", lambda e: e.tensor_reduce(out=s4m[s].ap, in_=sqt[s].ap[:, 512:1024].rearrange("p (h d) -> p h d", h=4),
                                                    axis=AX.X, op=ALU.add), [sqt[s]], [s4m[s]])
                rstd_q(qq, s4m[s], s4m[s].ap, 128)
                yield from qq.flush()
                qq("dve", lambda e: e.tensor_tensor(out=mn[s].ap, in0=ps[Z2].ap.rearrange("p (h d) -> p h d", h=4),
                                                    in1=bc(s4m[s].ap, [128, 4, 128], 2), op=ALU.mult), [ps[Z2], s4m[s]], [mn[s]])
                qq("pool", lambda e: e.tensor_tensor(out=mb[s].ap, in0=mn[s].ap, in1=bc(gmq.ap, [128, 4, 128], 1), op=ALU.mult),
                   [mn[s], gmq], [mb[s]])
                yield from qq.flush()
                pb = self.psb(BT)
                qq.call((lambda *a: (lambda: self.transposes(*a)))(mb[s], [mb[s].ap[:, h, :] for h in range(4)], BT, 128, 128))
                qq("act", lambda e: e.copy(out=mTs[s].ap.rearrange("p c t -> p (c t)"), in_=pb[:, 0:512]), [ps[BT]], [mTs[s]])
                qq.call((lambda *a: (lambda: self.store(*a)))(sc["mqT"][:, tok].rearrange("(c p) t -> p c t", p=128), mTs[s]))
                yield from qq.flush()

            for _ in merge(chain_q(), chain_k(), chain_d()):
                yield

        for s in range(NB):
            load_x(s, s)
        gens = [self._chain([(lambda t, s: (lambda: tile_gen(t, s)))(t, s) for t in range(s, NT, NB)]) for s in range(NB)]
        self._interleave(gens, lead=(18, 0))

    @staticmethod
    def _chain(makers):
        for mk in makers:
            for _ in mk():
                yield

    @staticmethod
    def _interleave(gens, lead=()):
        gens = list(gens)
        for g, n in zip(list(gens), lead):
            for _ in range(n):
                try:
                    next(g)
                except StopIteration:
                    gens.remove(g)
                    break
        while gens:
            for g in list(gens):
                try:
                    next(g)
                except StopIteration:
                    gens.remove(g)

    def phase_attn(self, l):
        ar = self.ar
        ar.reset()
        op, W, sc, ps = self.op, self.W, self.sc, self.ps
        wg = ar.tile([128, 8, 3072], BF16, "wg", top=True)
        wb = ar.tile([128, 12, D], BF16, "wb", top=True)
        wo = ar.tile([128, 8, D], BF16, "wo", top=True)
        self.p3w = (wg, wb, wo)
        bg = []
        gB = self.bg_require((l, "B"))
        for dst, src3, nch in ((wg, self.wbf[l]["w_in"][:, C1:D_IN], 8), (wb, self.wbf[l]["w_branch"], 12),
                               (wo, self.wbf[l]["w_out"], 8)):
            for c in range(nch):
                bg.append((lambda dst, src3, c: lambda: self.S.dma("sp", dst.ap[:, c, :], src3[c * 128:(c + 1) * 128, :],
                                                                   [gB], [dst.o], dst.o))(dst, src3, c))
        lam_init = 0.8 - 0.6 * math.exp(-0.3 * l)
        lv = [ar.tile([128, 64], F32, f"lv{i}") for i in range(4)]
        for t_, nm in zip(lv, ("lam_q1", "lam_k1", "lam_q2", "lam_k2")):
            self.bcast_load(t_, W[nm][l])
        lj = ar.tile([128, 64], F32, "lj")
        ls = [ar.tile([128, 1], F32, f"ls{i}") for i in range(2)]
        for i in range(2):
            op("dve", (lambda i: lambda e: e.tensor_tensor(out=lj.ap, in0=lv[2 * i].ap, in1=lv[2 * i + 1].ap, op=ALU.mult))(i),
               [lv[2 * i], lv[2 * i + 1]], [lj])
            op("dve", (lambda i: lambda e: e.tensor_reduce(out=ls[i].ap, in_=lj.ap, axis=AX.X, op=ALU.add))(i), [lj], [ls[i]])
            op("act", (lambda i: lambda e: e.activation(out=ls[i].ap, in_=ls[i].ap, func=AF.Exp))(i), [ls[i]], [ls[i]])
        op("dve", lambda e: e.tensor_tensor(out=self.nlam.ap, in0=ls[1].ap, in1=ls[0].ap, op=ALU.subtract), [ls[0], ls[1]], [self.nlam])
        op("dve", lambda e: e.tensor_scalar(out=self.nlam.ap, in0=self.nlam.ap, scalar1=float(-lam_init), scalar2=None, op0=ALU.add),
           [self.nlam], [self.nlam])
        self.load(self.gdo, W["g_diff_out"][l].rearrange("(p o) -> p o", o=1))
        op("dve", lambda e: e.tensor_scalar(out=self.gdo.ap, in0=self.gdo.ap, scalar1=float(1.0 - lam_init), scalar2=None, op0=ALU.mult),
           [self.gdo], [self.gdo])

        NP = 8
        pT = [ar.tile([128, 512], BF16, f"pT{i}") for i in range(NP)]
        pF = [ar.tile([128, 512], F32, f"pF{i}") for i in range(4)]
        st = {"p": 0, "f": 0, "s": 0}
        base_off = ar.off

        kT = [ar.tile([96, SEQ], BF16, f"kTa{i}") for i in range(3)]
        va = [ar.tile([128, NT, 65], BF16, f"va{i}") for i in range(3)]
        NQ = 6
        qT = [ar.tile([96, 512], BF16, f"qTa{i}") for i in range(NQ)]
        done_items = set()
        osb = [ar.tile([65, 512], F32, f"osb{i}") for i in range(2)]
        rl = [ar.tile([65, 512], F32, f"rl{i}") for i in range(2)]
        on = [[ar.tile([64, 512], BF16, f"on{i}{j}") for j in range(2)] for i in range(2)]
        for i in range(3):
            op("dve", (lambda i: lambda e: e.memset(va[i].ap[:, :, 64:65], 1.0))(i), [], [va[i]])
        scale_a = 96 ** -0.5
        gorder = [7, 0, 6, 1, 5, 2, 4, 3]
        items = [(h, g) for h in range(8) for g in gorder]
        loaded_heads = set()
        nxt = {"n": 0, 0: 0, 1: 0}

        def load_kv_a(h):
            if h in loaded_heads or h >= 8:
                return
            loaded_heads.add(h)
            s3 = h % 3
            self.load(kT[s3], sc["kT"][h, :, :])
            self.load(va[s3], sc["v"][:, h * 64:(h + 1) * 64].rearrange("(b p) d -> p b d", p=128), dst_ap=va[s3].ap[:, :, 0:64])

        def load_q_a(n):
            if n < len(items):
                h, g = items[n]
                assert n < NQ or (n - NQ) in done_items
                self.load(qT[n % NQ], sc["qT"][h, :, g * 512:(g + 1) * 512])

        def mla_item(lane, n):
            h, g = items[n]
            s3 = h % 3
            if bg:
                bg.pop(0)()
            self.bg_pump(1)
            load_kv_a(h)
            load_kv_a(h + 1)
            load_q_a(n + 2)
            q = qT[n % NQ]
            assert all(items[m][0] >= h - 1 for m in range(n) if m not in done_items)
            SB = (4 * lane, 4 * lane + 1)
            ob = 4 * lane + 2
            bcb = 4 * lane + 3
            nkb = 4 * g + 4
            cnt = {"s": 0, "p": 0}

            def qk(kb):
                c0 = max(0, kb - 4 * g) * 128
                bank = SB[cnt["s"] % 2]
                cnt["s"] += 1
                op("pe", lambda e: e.matmul(ps[bank].ap[:, c0:512], lhsT=kT[s3].ap[:, kb * 128:(kb + 1) * 128], rhs=q.ap[:, c0:512],
                                            start=True, stop=True), [kT[s3], q], [ps[bank]])
                p = pT[lane * 4 + cnt["p"] % 4]
                cnt["p"] += 1
                op("act", lambda e: e.activation(out=p.ap[:, c0:512], in_=ps[bank].ap[:, c0:512], func=AF.Exp, scale=float(scale_a)),
                   [ps[bank]], [p])
                if kb >= 4 * g:
                    op("pool", lambda e: e.memset(p.ap[64:128, c0:c0 + 64], 0.0), [], [p])
                return (kb, c0, p)

            def pv(item):
                kb, c0, p = item
                op("pe", lambda e: e.matmul(ps[ob].ap[0:65, c0:512], lhsT=va[s3].ap[:, kb, :], rhs=p.ap[:, c0:512],
                                            start=(kb == 0), stop=(kb == nkb - 1)), [va[s3], p], [ps[ob]])

            pend = []
            for kb in range(nkb):
                pend.append(qk(kb))
                if len(pend) > 1:
                    pv(pend.pop(0))
                if kb == min(nkb - 1, 6) and fin[lane] is not None:
                    fin[lane]()
                    fin[lane] = None
                yield
            while pend:
                pv(pend.pop(0))
            yield
            o_, r_, n_ = osb[lane], rl[lane], on[lane][nxt[lane] % 2]
            nxt[lane] += 1
            op("dve", lambda e: e.tensor_copy(out=o_.ap, in_=ps[ob].ap[0:65, :]), [ps[ob]], [o_])
            op("dve", lambda e: e.reciprocal(out=r_.ap[64:65, :], in_=o_.ap[64:65, :]), [o_], [r_])

            def stage2():
                op("pe", lambda e: e.matmul(ps[bcb].ap[0:64, :], lhsT=self.ones_f.ap[64:65, 0:64], rhs=r_.ap[64:65, :],
                                            start=True, stop=True), [self.ones_f, r_], [ps[bcb]])
                op("dve", lambda e: e.tensor_tensor(out=n_.ap, in0=o_.ap[0:64, :], in1=ps[bcb].ap[0:64, :], op=ALU.mult), [o_, ps[bcb]], [n_])
                self.store(sc["oT"][0, h * 64:(h + 1) * 64, g * 512:(g + 1) * 512], n_)

            fin[lane] = stage2
            done_items.add(n)
            yield

        load_kv_a(0)
        load_q_a(0)
        load_q_a(1)

        fin = [None, None]

        def lane_gen(lane):
            while nxt["n"] < len(items):
                n = nxt["n"]
                nxt["n"] += 1
                for _ in mla_item(lane, n):
                    yield
            if fin[lane] is not None:
                fin[lane]()
                fin[lane] = None

        self._interleave([lane_gen(0), lane_gen(1)])
        while bg:
            bg.pop(0)()
        self.S.barrier()

        ar.off = base_off
        kTd_ = [ar.tile([64, SEQ], BF16, f"kTd{i}") for i in range(4)]
        vd_ = [ar.tile([128, NT, 128], BF16, f"vdd{i}") for i in range(2)]
        qTd_ = [[ar.tile([64, 512], BF16, f"qTd{c}{i}") for i in range(2)] for c in range(2)]
        ebt = [ar.tile([128, NNEAR, 512], F32, f"ebt{i}") for i in range(2)]
        evs = [[ar.tile([128, 512], F32, f"ev{i}{j}") for j in range(4)] for i in range(2)]
        finb = [None]
        obf = [ar.tile([128, 512], BF16, f"obf{i}") for i in range(2)]
        scale_b = 64 ** -0.5
        SBd = ((0, 1), (2, 3))
        OBd = ((4, 5), (6, 7))
        MSB = 0

        def load_h_b(h):
            if h >= 4:
                return
            for c in range(2):
                u = 2 * h + c
                self.load(kTd_[(h % 2) * 2 + c], sc["dkT"][u * 64:(u + 1) * 64, :])
            self.load(vd_[h % 2], sc["dv"][:, h * 128:(h + 1) * 128].rearrange("(b p) d -> p b d", p=128))
            self.load(ebt[h % 2], sc["EB"][h].rearrange("n p q -> p n q"))

        def load_q_b(i):
            if i >= 4 * NG:
                return
            h, g = divmod(i, NG)
            for c in range(2):
                u = 2 * h + c
                self.load(qTd_[c][i % 2], sc["dqT"][u * 64:(u + 1) * 64, g * 512:(g + 1) * 512])

        load_h_b(0)
        load_q_b(0)
        for h in range(4):
            load_h_b(h + 1)
            vt = vd_[h % 2]
            et = ebt[h % 2]
            for g in range(NG):
                i = h * NG + g
                load_q_b(i + 1)
                self.bg_pump(1)
                nkb = 4 * g + 4
                cnt = {"s": 0}

                def qk(kb, c):
                    q = qTd_[c][i % 2]
                    kt = kTd_[(h % 2) * 2 + c]
                    dl = (kb - 4 * g) * 128
                    c0 = max(0, dl)
                    bank = SBd[c][kb % 2]
                    op("pe", lambda e: e.matmul(ps[bank].ap[:, c0:512], lhsT=kt.ap[:, kb * 128:(kb + 1) * 128], rhs=q.ap[:, c0:512],
                                                start=True, stop=True), [kt, q], [ps[bank]])
                    p = pT[c * 4 + kb % 4]
                    if dl in NEAR_IDX:
                        ni = NEAR_IDX[dl]
                        f = pF[c * 2 + kb % 2]
                        op("act", lambda e: e.activation(out=f.ap[:, c0:512], in_=ps[bank].ap[:, c0:512], func=AF.Exp,
                                                         scale=float(scale_b)), [ps[bank]], [f])
                        op("dve", lambda e: e.tensor_tensor(out=p.ap[:, c0:512], in0=f.ap[:, c0:512], in1=et.ap[:, ni, c0:512],
                                                            op=ALU.mult), [f, et], [p])
                    else:
                        op("act", lambda e: e.activation(out=p.ap[:, c0:512], in_=ps[bank].ap[:, c0:512], func=AF.Exp,
                                                         bias=self.b15.ap[:, h:h + 1], scale=float(scale_b)), [ps[bank], self.b15], [p])
                    return (kb, c0, p)

                def pv(item, c):
                    kb, c0, p = item
                    obk, lbk = OBd[c]
                    op("pe", lambda e: e.matmul(ps[obk].ap[:, c0:512], lhsT=vt.ap[:, kb, :], rhs=p.ap[:, c0:512],
                                                start=(kb == 0), stop=(kb == nkb - 1)), [vt, p], [ps[obk]], inc=False)
                    op("pe", lambda e: e.matmul(ps[lbk].ap[:, c0:512], lhsT=self.ones_bf.ap, rhs=p.ap[:, c0:512],
                                                start=(kb == 0), stop=(kb == nkb - 1)), [self.ones_bf, p], [ps[lbk]])

                pend = []
                for kb in range(nkb):
                    cur = [qk(kb, 0), qk(kb, 1)]
                    if pend:
                        pv(pend[0], 0)
                        pv(pend[1], 1)
                    pend = cur
                    if kb == min(nkb - 1, 11) and finb[0] is not None:
                        finb[0]()
                        finb[0] = None
                pv(pend[0], 0)
                pv(pend[1], 1)
                ob_ = obf[i % 2]
                ev = evs[i % 2]
                op("dve", lambda e: e.tensor_copy(out=ev[0].ap, in_=ps[OBd[0][1]].ap), [ps[OBd[0][1]]], [ev[0]])
                op("dve", lambda e: e.tensor_copy(out=ev[1].ap, in_=ps[OBd[0][0]].ap), [ps[OBd[0][0]]], [ev[1]])
                op("dve", lambda e: e.tensor_copy(out=ev[2].ap, in_=ps[OBd[1][1]].ap), [ps[OBd[1][1]]], [ev[2]])
                op("dve", lambda e: e.tensor_copy(out=ev[3].ap, in_=ps[OBd[1][0]].ap), [ps[OBd[1][0]]], [ev[3]])

                def tail(ev=ev, ob_=ob_, h=h, g=g):
                    op("dve", lambda e: e.reciprocal(out=ev[0].ap, in_=ev[0].ap), [ev[0]], [ev[0]])
                    op("pool", lambda e: e.tensor_tensor(out=ev[1].ap, in0=ev[1].ap, in1=ev[0].ap, op=ALU.mult), [ev[0], ev[1]], [ev[1]])
                    op("dve", lambda e: e.reciprocal(out=ev[2].ap, in_=ev[2].ap), [ev[2]], [ev[2]])
                    op("pool", lambda e: e.tensor_tensor(out=ev[3].ap, in0=ev[3].ap, in1=ev[2].ap, op=ALU.mult), [ev[2], ev[3]], [ev[3]])
                    op("dve", lambda e: e.scalar_tensor_tensor(out=ev[1].ap, in0=ev[3].ap, scalar=self.nlam.ap, in1=ev[1].ap,
                                                               op0=ALU.mult, op1=ALU.add), [ev[3], self.nlam, ev[1]], [ev[1]])
                    op("pool", lambda e: e.tensor_tensor(out=ev[0].ap, in0=ev[1].ap, in1=ev[1].ap, op=ALU.mult), [ev[1]], [ev[0]])

                    def tail2():
                        op("pe", lambda e: e.matmul(ps[MSB].ap, lhsT=self.ones_f.ap, rhs=ev[0].ap, start=True, stop=True),
                           [self.ones_f, ev[0]], [ps[MSB]])
                        op("act", lambda e: e.activation(out=ev[2].ap, in_=ps[MSB].ap, func=AF.Ln, bias=self.eps_t.ap, scale=1.0 / 128),
                           [ps[MSB], self.eps_t], [ev[2]])
                        op("act", lambda e: e.activation(out=ev[2].ap, in_=ev[2].ap, func=AF.Exp, scale=-0.5), [ev[2]], [ev[2]])
                        op("dve", lambda e: e.scalar_tensor_tensor(out=ob_.ap, in0=ev[1].ap, scalar=self.gdo.ap, in1=ev[2].ap,
                                                                   op0=ALU.mult, op1=ALU.mult), [ev[1], self.gdo, ev[2]], [ob_])
                        self.store(sc["oT"][1, h * 128:(h + 1) * 128, g * 512:(g + 1) * 512], ob_)
                    return tail2

                finb[0] = tail()
        if finb[0] is not None:
            finb[0]()
            finb[0] = None
        self.S.barrier()

        ar.off = base_off
        r1 = ar.tile([128, 512], F32, "r1c")
        kmT = ar.tile([128, 4, MEM], BF16, "kmTc")
        vm = ar.tile([128, 2, 512], BF16, "vmc")
        qTc = [ar.tile([128, 512], BF16, f"qTc{i}") for i in range(3)]
        ocb = [ar.tile([128, 512], BF16, f"ocb{i}") for i in range(2)]
        self.load(kmT, sc["kmT"][:, :, :])
        self.load(vm, sc["vm"][:, :].rearrange("(b p) d -> p b d", p=128))
        scale_c = 128 ** -0.5
        SBc = (0, 1, 7)
        OBc = ((2, 3), (4, 5))

        def load_q_c(i):
            h, g = divmod(i, NG)
            self.load(qTc[i % 3], sc["mqT"][h * 128:(h + 1) * 128, g * 512:(g + 1) * 512])

        load_q_c(0)
        for h in range(4):
            for g in range(NG):
                i = h * NG + g
                if i + 1 < 4 * NG:
                    load_q_c(i + 1)
                q = qTc[i % 3]
                obk, lbk = OBc[i % 2]
                items = []
                for mbk in range(2):
                    bank = SBc[st["s"] % 3]
                    st["s"] += 1
                    op("pe", (lambda mbk, bank, q: lambda e: e.matmul(ps[bank].ap, lhsT=kmT.ap[:, h, mbk * 128:(mbk + 1) * 128], rhs=q.ap,
                                                                      start=True, stop=True))(mbk, bank, q), [kmT, q], [ps[bank]])
                    p = pT[st["p"] % NP]
                    st["p"] += 1
                    op("act", (lambda bank, p: lambda e: e.activation(out=p.ap, in_=ps[bank].ap, func=AF.Exp, scale=float(scale_c)))(bank, p),
                       [ps[bank]], [p])
                    items.append((mbk, p))
                for mbk, p in items:
                    op("pe", (lambda mbk, p: lambda e: e.matmul(ps[obk].ap, lhsT=vm.ap[:, mbk, h * 128:(h + 1) * 128], rhs=p.ap,
                                                                start=(mbk == 0), stop=(mbk == 1)))(mbk, p), [vm, p], [ps[obk]], inc=False)
                    op("pe", (lambda mbk, p: lambda e: e.matmul(ps[lbk].ap, lhsT=self.ones_bf.ap, rhs=p.ap,
                                                                start=(mbk == 0), stop=(mbk == 1)))(mbk, p), [self.ones_bf, p], [ps[lbk]])
                ob_ = ocb[i % 2]
                op("dve", (lambda lbk: lambda e: e.reciprocal(out=r1.ap, in_=ps[lbk].ap))(lbk), [ps[lbk]], [r1])
                op("dve", (lambda obk, ob_: lambda e: e.tensor_tensor(out=ob_.ap, in0=r1.ap, in1=ps[obk].ap, op=ALU.mult))(obk, ob_),
                   [r1, ps[obk]], [ob_])
                self.store(sc["oT"][2, h * 128:(h + 1) * 128, g * 512:(g + 1) * 512], ob_)

    def phase_p3(self, l, xin, out_d):
        ar = self.ar
        ar.reset(keep_top=True)
        op, W, sc, ps = self.op, self.W, self.sc, self.ps
        wg, wb, wo = self.p3w
        hT = [ar.tile([128, 8, 512], BF16, f"hT3{i}") for i in range(2)]
        oT = [ar.tile([128, 12, 512], BF16, f"oT3{i}") for i in range(2)]
        yT = ar.tile([128, 8, 512], BF16, "yT")
        gs = [ar.tile([128, 512], F32, f"gs{i}") for i in range(6)]
        tm = [ar.tile([128, 512], F32, f"tm{i}") for i in range(4)]
        xo = [ar.tile([128, D], F32, f"xo{i}") for i in range(2)]

        def loads(g):
            s = g % 2
            for tt in range(4):
                self.load(hT[s], sc["hT"][g * 4 + tt], dst_ap=hT[s].ap[:, :, tt * 128:(tt + 1) * 128])
            for n in range(3):
                self.load(oT[s], sc["oT"][n, :, g * 512:(g + 1) * 512].rearrange("(c p) t -> p c t", p=128),
                          dst_ap=oT[s].ap[:, n * 4:(n + 1) * 4, :])

        loads(0)
        k = 0
        for g in range(NG):
            s = g % 2
            self.bg_pump(2)
            if g + 1 < NG:
                loads(g + 1)
            for fc in range(8):
                gset = gs[(fc % 2) * 3:(fc % 2) * 3 + 3]
                for n in range(3):
                    for kc in range(8):
                        op("pe", (lambda n, kc: lambda e: e.matmul(ps[n].ap, lhsT=wg.ap[:, kc, n * D + fc * 128:n * D + (fc + 1) * 128],
                                                                   rhs=hT[s].ap[:, kc, :], start=(kc == 0), stop=(kc == 7)))(n, kc),
                           [wg, hT[s]], [ps[n]], inc=(kc == 7))
                for n in range(3):
                    for c in range(4):
                        op("pe", (lambda n, c: lambda e: e.matmul(ps[3 + n].ap, lhsT=wb.ap[:, n * 4 + c, fc * 128:(fc + 1) * 128],
                                                                  rhs=oT[s].ap[:, n * 4 + c, :], start=(c == 0), stop=(c == 3)))(n, c),
                           [wb, oT[s]], [ps[3 + n]], inc=(c == 3))
                for n in range(3):
                    op("act", (lambda n, gset: lambda e: e.activation(out=gset[n].ap, in_=ps[n].ap, func=AF.Sigmoid))(n, gset),
                       [ps[n]], [gset[n]])
                t0, t1 = tm[(k % 2) * 2], tm[(k % 2) * 2 + 1]
                k += 1
                op("dve", (lambda gset, t0: lambda e: e.tensor_tensor(out=t0.ap, in0=gset[0].ap, in1=ps[3].ap, op=ALU.mult))(gset, t0),
                   [gset[0], ps[3]], [t0])
                op("dve", (lambda gset, t1: lambda e: e.tensor_tensor(out=t1.ap, in0=gset[1].ap, in1=ps[4].ap, op=ALU.mult))(gset, t1),
                   [gset[1], ps[4]], [t1])
                op("dve", (lambda t0, t1: lambda e: e.tensor_tensor(out=t0.ap, in0=t0.ap, in1=t1.ap, op=ALU.add))(t0, t1), [t0, t1], [t0])
                op("dve", (lambda gset, t1: lambda e: e.tensor_tensor(out=t1.ap, in0=gset[2].ap, in1=ps[5].ap, op=ALU.mult))(gset, t1),
                   [gset[2], ps[5]], [t1])
                op("dve", (lambda t0, t1, fc: lambda e: e.tensor_tensor(out=yT.ap[:, fc, :], in0=t0.ap, in1=t1.ap, op=ALU.add))(t0, t1, fc),
                   [t0, t1], [yT])
            for tt in range(4):
                x_o = xo[tt % 2]
                r0 = g * 512 + tt * 128
                self.load(x_o, xin[r0:r0 + 128, :])
                for cb in range(2):
                    bank = 6 + cb
                    for fc in range(8):
                        op("pe", (lambda fc, cb, bank: lambda e: e.matmul(ps[bank].ap, lhsT=yT.ap[:, fc, tt * 128:(tt + 1) * 128],
                                                                          rhs=wo.ap[:, fc, cb * 512:(cb + 1) * 512],
                                                                          start=(fc == 0), stop=(fc == 7)))(fc, cb, bank),
                           [yT, wo], [ps[bank]], inc=(fc == 7))
                    op("dve", (lambda cb, bank, x_o: lambda e: e.tensor_tensor(out=x_o.ap[:, cb * 512:(cb + 1) * 512],
                                                                               in0=x_o.ap[:, cb * 512:(cb + 1) * 512],
                                                                               in1=ps[bank].ap, op=ALU.add))(cb, bank, x_o),
                       [x_o, ps[bank]], [x_o])
                self.store(out_d[r0:r0 + 128, :], x_o)

    def phase_p4(self, l, out_d):
        ar = self.ar
        ar.reset()
        op, W, ps = self.op, self.W, self.ps
        w1 = ar.tile([128, 8, 4 * D], BF16, "wf1")
        w2 = ar.tile([128, 32, D], BF16, "wf2")
        self.wload(w1, self.wbf[l]["w_ff1"], 8, (l, "C"))
        self.wload(w2, self.wbf[l]["w_ff2"], 32, (l, "D"))
        gm = ar.tile([128, D], F32, "gmlp")
        self.bcast_load(gm, W["g_mlp"][l])
        xt = [ar.tile([128, D], F32, f"x4{i}") for i in range(2)]
        hb = [ar.tile([128, D], BF16, f"hb4{i}") for i in range(4)]
        ssq = [ar.tile([128, 1], F32, f"ssq4{i}") for i in range(4)]
        hT = ar.tile([128, 8, 512], BF16, "hT4")
        uT = ar.tile([128, 32, 512], BF16, "uT")
        rr = [ar.tile([128, 512], F32, f"rr{i}") for i in range(2)]
        xo = [ar.tile([128, D], F32, f"xo4{i}") for i in range(1)]

        def loadx(i):
            if i < NT:
                self.load(xt[i % 2], out_d[i * 128:(i + 1) * 128, :])

        def norm(g, tt):
            i = g * 4 + tt
            loadx(i + 1)
            self.norm_tile(xt[i % 2], gm, hb[tt], hb[tt], ssq[tt])

        def transp(g):
            for tt in range(4):
                bank = tt % 2
                h_ = hb[tt]
                pb = self.transposes(h_, [h_.ap[:, c * 128:(c + 1) * 128] for c in range(8)], bank, 128, 128)
                op("act", lambda e: e.copy(out=hT.ap[:, :, tt * 128:(tt + 1) * 128], in_=pb.rearrange("p (c t) -> p c t", c=8)),
                   [ps[bank]], [hT])

        loadx(0)
        for tt in range(4):
            norm(0, tt)
        transp(0)
        k = 0
        for g in range(NG):
            self.bg_pump(2)
            for f in range(32):
                bank = 2 + f % 3
                for kc in range(8):
                    op("pe", lambda e: e.matmul(ps[bank].ap, lhsT=w1.ap[:, kc, f * 128:(f + 1) * 128], rhs=hT.ap[:, kc, :],
                                                start=(kc == 0), stop=(kc == 7)), [w1, hT], [ps[bank]], inc=(kc == 7))
                r = rr[k % 2]
                k += 1
                op("act", lambda e: e.activation(out=r.ap, in_=ps[bank].ap, func=AF.Relu), [ps[bank]], [r])
                op("dve", lambda e: e.tensor_tensor(out=uT.ap[:, f, :], in0=r.ap, in1=r.ap, op=ALU.mult), [r], [uT])
                if g + 1 < NG and f % 6 == 4 and f // 6 < 4:
                    norm(g + 1, f // 6)
            if g + 1 < NG:
                transp(g + 1)
            for tt in range(4):
                x_o = xo[0]
                r0 = g * 512 + tt * 128
                self.load(x_o, out_d[r0:r0 + 128, :])
                for cb in range(2):
                    bank = 5 + (tt * 2 + cb) % 3
                    for f in range(32):
                        op("pe", lambda e: e.matmul(ps[bank].ap, lhsT=uT.ap[:, f, tt * 128:(tt + 1) * 128],
                                                    rhs=w2.ap[:, f, cb * 512:(cb + 1) * 512],
                                                    start=(f == 0), stop=(f == 31)), [uT, w2], [ps[bank]], inc=(f == 31))
                    op("dve", lambda e: e.tensor_tensor(out=x_o.ap[:, cb * 512:(cb + 1) * 512],
                                                        in0=x_o.ap[:, cb * 512:(cb + 1) * 512],
                                                        in1=ps[bank].ap, op=ALU.add), [x_o, ps[bank]], [x_o])
                self.store(out_d[r0:r0 + 128, :], x_o)


WNAMES = ("g_mix", "g_mem", "w_in", "g_cq", "w_uq", "g_ckv", "w_ukv", "g_mla_q", "g_mla_k", "g_diff_q", "g_diff_k",
          "lam_q1", "lam_k1", "lam_q2", "lam_k2", "g_diff_out", "w_mem_kv", "g_mem_q", "g_mem_k", "w_branch", "w_out",
          "g_mlp", "w_ff1", "w_ff2")


def make_in_maps(inputs, cores):
    ident, jx, oh, cm = _consts()
    shared = {k: np.ascontiguousarray(np.asarray(inputs[k], dtype=np.float32)) for k in WNAMES}
    shared["t5_table"] = np.ascontiguousarray(np.asarray(inputs["t5_table"], dtype=np.float32))
    shared.update({"c_ident": ident, "c_jx": jx, "c_oh": oh, "c_cm": cm})
    x = np.asarray(inputs["x"], dtype=np.float32)
    mem = np.asarray(inputs["mem"], dtype=np.float32)
    pos = np.asarray(inputs["positions"]).astype(np.int32)
    maps = []
    for b in cores:
        m = dict(shared)
        m["x"] = np.ascontiguousarray(x[b])
        m["mem"] = np.ascontiguousarray(mem[b])
        m["pos"] = np.ascontiguousarray(pos[b].reshape(NT, 128).T)
        maps.append(m)
    return maps


def kernel(**inputs):
    nc = Builder().build()
    maps = make_in_maps(inputs, range(8))
    res = run_bass_kernel_spmd(nc, maps, core_ids=list(range(8)))
    return np.stack([np.asarray(r["out"], dtype=np.float32) for r in res.results], axis=0)
```
